# Optimizing a Trainium2 kernel written in Bass

```python
import math
import jax, jax.numpy as jnp
from jax import lax
import numpy as np

D_MODEL = 1024
BATCH = 8
SEQ = 4096
DEPTH = 1

N_META = 16
BLOCK = 128
PREFIX = BLOCK
N_PAD = PREFIX - N_META

GLA_HEADS = 4
GLA_DK = 64
GLA_DV = 128
GLA_GATE_RANK = 16
GLA_TAU = 16.0
GLA_CHUNK = 64
GLA_QK = GLA_HEADS * GLA_DK
GLA_V = GLA_HEADS * GLA_DV

FOX_HEADS = 8
FOX_DH = 64
FOX_W = FOX_HEADS * FOX_DH

MIX_W = GLA_V + FOX_W
IN_SPLITS = (GLA_QK, GLA_QK, GLA_V, GLA_V, GLA_GATE_RANK, FOX_W, FOX_W, FOX_W, FOX_HEADS)
IN_W = 2 * GLA_QK + 2 * GLA_V + GLA_GATE_RANK + 3 * FOX_W + FOX_HEADS

PEER_HEADS = 8
PEER_NKEYS = 128
PEER_EXPERTS = PEER_NKEYS * PEER_NKEYS
PEER_DKEY = 256
PEER_TOPK = 16
PEER_TOK_BLOCK = 128

DN_ALPHA = (2.0 * DEPTH) ** 0.25
DN_BETA = (8.0 * DEPTH) ** -0.25
LN_EPS = 1e-5
NEG = -1e30

kernel_name = "hymba_gla_fox_peer_deepnorm"


def layer_norm(x, g, b):
    xf = x.astype(jnp.float32)
    mu = jnp.mean(xf, -1, keepdims=True)
    var = jnp.mean(jnp.square(xf - mu), -1, keepdims=True)
    return ((xf - mu) * lax.rsqrt(var + LN_EPS) * g.astype(jnp.float32) + b.astype(jnp.float32)).astype(x.dtype)


def head_rmsnorm(o, g):
    r = o * lax.rsqrt(jnp.mean(jnp.square(o), -1, keepdims=True) + LN_EPS)
    return r * g.astype(jnp.float32).reshape(o.shape[-2:])


def gla_mix(q, k, v, glog, valid):
    B, L, H, dk = q.shape
    dv = v.shape[-1]
    C = GLA_CHUNK
    N = L // C
    scale = dk ** -0.5
    k = k * valid[None, :, None, None].astype(k.dtype)

    def chunked(t):
        return t.reshape(B, N, C, H, t.shape[-1]).transpose(0, 3, 1, 2, 4)

    q, k, v, glog = chunked(q), chunked(k), chunked(v), chunked(glog)
    bcum = jnp.cumsum(glog, axis=3)
    b_ref = bcum[:, :, :, C // 2 - 1:C // 2, :]
    q_in = q * jnp.exp(bcum - b_ref)
    k_in = k * jnp.exp(b_ref - bcum)
    causal = jnp.tril(jnp.ones((C, C), dtype=bool))
    a = jnp.einsum('bhncd,bhnsd->bhncs', q_in, k_in) * scale
    a = jnp.where(causal, a, 0.0)
    o_intra = jnp.einsum('bhncs,bhnsv->bhncv', a, v)

    b_last = bcum[:, :, :, -1:, :]
    d_state = jnp.einsum('bhncd,bhncv->bhndv', k * jnp.exp(b_last - bcum), v)
    decay = jnp.exp(b_last[:, :, :, 0, :])[..., None]

    def step(S, inp):
        dec, ds = inp
        return dec * S + ds, S

    S0 = jnp.zeros((B, H, dk, dv), jnp.float32)
    _, S_prev = lax.scan(step, S0, (jnp.moveaxis(decay, 2, 0), jnp.moveaxis(d_state, 2, 0)))
    S_prev = jnp.moveaxis(S_prev, 0, 2)
    o_inter = jnp.einsum('bhncd,bhndv->bhncv', q * jnp.exp(bcum), S_prev) * scale
    o = o_intra + o_inter
    return o.transpose(0, 2, 3, 1, 4).reshape(B, L, H, dv)


def fox_mix(q, k, v, logf, valid):
    B, L, H, d = q.shape
    scale = d ** -0.5
    q, k, v = (t.transpose(0, 2, 1, 3) for t in (q, k, v))
    c = jnp.cumsum(logf, axis=1).transpose(0, 2, 1)
    pos = jnp.arange(L)
    outs = []
    for i in range(L // BLOCK):
        s0, e = i * BLOCK, (i + 1) * BLOCK
        logits = jnp.einsum('bhqd,bhkd->bhqk', q[:, :, s0:e], k[:, :, :e]) * scale
        logits = logits + c[:, :, s0:e, None] - c[:, :, None, :e]
        mask = (pos[None, :e] <= pos[s0:e, None]) & valid[None, :e]
        p = jax.nn.softmax(jnp.where(mask, logits, NEG), axis=-1)
        outs.append(jnp.einsum('bhqk,bhkd->bhqd', p, v[:, :, :e]))
    o = jnp.concatenate(outs, axis=2)
    return o.transpose(0, 2, 1, 3)


def hybrid_mixer(h, w_in, w_gate_up, b_gate, b_forget, gla_norm_g, fox_norm_g, w_out, valid):
    B, L, _ = h.shape
    proj = (h @ w_in).astype(jnp.float32)
    bounds, acc = [], 0
    for w in IN_SPLITS[:-1]:
        acc += w
        bounds.append(acc)
    qa, ka, va, ra, ga, qb, kb, vb, fb = jnp.split(proj, bounds, axis=-1)

    glog = jax.nn.log_sigmoid(ga @ w_gate_up.astype(jnp.float32) + b_gate.astype(jnp.float32)) / GLA_TAU
    hA = lambda t, d: t.reshape(B, L, GLA_HEADS, d)
    oa = gla_mix(hA(qa, GLA_DK), hA(ka, GLA_DK), hA(va, GLA_DV), hA(glog, GLA_DK), valid)
    oa = head_rmsnorm(oa, gla_norm_g).reshape(B, L, GLA_V) * jax.nn.silu(ra)

    logf = jax.nn.log_sigmoid(fb + b_forget.astype(jnp.float32))
    hB = lambda t: t.reshape(B, L, FOX_HEADS, FOX_DH)
    ob = fox_mix(hB(qb), hB(kb), hB(vb), logf, valid)
    ob = head_rmsnorm(ob, fox_norm_g).reshape(B, L, FOX_W)

    o = jnp.concatenate([oa, ob], axis=-1).astype(h.dtype)
    return o @ w_out


def peer(h, w_q, sub_keys, u_tab, v_tab):
    B, L, D = h.shape
    T = B * L
    xt = h.reshape(T, D)
    q = (xt @ w_q).astype(jnp.float32).reshape(T, PEER_HEADS, 2, PEER_DKEY // 2)
    s = jnp.einsum('thcd,hcnd->thcn', q, sub_keys.astype(jnp.float32))
    top_s, top_i = lax.top_k(s, PEER_TOPK)
    cand_s = (top_s[:, :, 0, :, None] + top_s[:, :, 1, None, :]).reshape(T, PEER_HEADS, -1)
    cand_e = (top_i[:, :, 0, :, None] * PEER_NKEYS + top_i[:, :, 1, None, :]).reshape(T, PEER_HEADS, -1)
    best_s, best_j = lax.top_k(cand_s, PEER_TOPK)
    experts = jnp.take_along_axis(cand_e, best_j, axis=-1)
    gates = jax.nn.softmax(best_s, axis=-1)

    nb, tb, hk = T // PEER_TOK_BLOCK, PEER_TOK_BLOCK, PEER_HEADS * PEER_TOPK

    def block(args):
        xb, eb, gb = args
        act = jax.nn.gelu(jnp.einsum('td,tkd->tk', xb, u_tab[eb]), approximate=False)
        return jnp.einsum('tk,tkd->td', gb.astype(xb.dtype) * act, v_tab[eb])

    y = lax.map(block, (xt.reshape(nb, tb, D), experts.reshape(nb, tb, hk), gates.reshape(nb, tb, hk)))
    return y.reshape(B, L, D)


def setup_inputs(seed: int = 0) -> dict:
    key = jax.random.key(seed)
    ks = jax.random.split(key, 20)
    nrm = lambda k, shape: jax.random.normal(k, shape, jnp.float32)
    D = D_MODEL
    col_scale = jnp.concatenate([
        jnp.ones((2 * GLA_QK,)), jnp.full((GLA_V,), DN_BETA), jnp.ones((GLA_V + GLA_GATE_RANK + 2 * FOX_W,)),
        jnp.full((FOX_W,), DN_BETA), jnp.ones((FOX_HEADS,))]).astype(jnp.float32)
    return {
        "x": nrm(ks[0], (BATCH, SEQ, D)),
        "meta_tokens": nrm(ks[1], (N_META, D)),
        "emb_ln_g": 1.0 + 0.02 * nrm(ks[2], (D,)),
        "emb_ln_b": 0.02 * nrm(ks[3], (D,)),
        "w_in": nrm(ks[4], (DEPTH, D, IN_W)) * (D ** -0.5) * col_scale,
        "w_gate_up": nrm(ks[5], (DEPTH, GLA_GATE_RANK, GLA_QK)) * (GLA_GATE_RANK ** -0.5),
        "b_gate": 0.02 * nrm(ks[6], (DEPTH, GLA_QK)),
        "b_forget": jax.random.uniform(ks[7], (DEPTH, FOX_HEADS), jnp.float32, 0.0, 4.0),
        "gla_norm_g": 1.0 + 0.02 * nrm(ks[8], (DEPTH, GLA_V)),
        "fox_norm_g": 1.0 + 0.02 * nrm(ks[9], (DEPTH, FOX_W)),
        "w_out": nrm(ks[10], (DEPTH, MIX_W, D)) * (MIX_W ** -0.5) * DN_BETA,
        "ln1_g": 1.0 + 0.02 * nrm(ks[11], (DEPTH, D)),
        "ln1_b": 0.02 * nrm(ks[12], (DEPTH, D)),
        "peer_w_q": nrm(ks[13], (DEPTH, D, PEER_HEADS * PEER_DKEY)) * (D ** -0.5),
        "peer_sub_keys": nrm(ks[14], (DEPTH, PEER_HEADS, 2, PEER_NKEYS, PEER_DKEY // 2)) * ((PEER_DKEY // 2) ** -0.5),
        "peer_u": nrm(ks[15], (DEPTH, PEER_EXPERTS, D)) * (D ** -0.5),
        "peer_v": nrm(ks[16], (DEPTH, PEER_EXPERTS, D)) * DN_BETA * (PEER_HEADS ** -0.5),
        "ln2_g": 1.0 + 0.02 * nrm(ks[17], (DEPTH, D)),
        "ln2_b": 0.02 * nrm(ks[18], (DEPTH, D)),
    }


def reference(x, meta_tokens, emb_ln_g, emb_ln_b, w_in, w_gate_up, b_gate, b_forget, gla_norm_g,
              fox_norm_g, w_out, ln1_g, ln1_b, peer_w_q, peer_sub_keys, peer_u, peer_v, ln2_g, ln2_b):
    B = x.shape[0]
    pad = jnp.zeros((B, N_PAD, D_MODEL), x.dtype)
    meta = jnp.broadcast_to(meta_tokens[None].astype(x.dtype), (B, N_META, D_MODEL))
    h = jnp.concatenate([pad, meta, x], axis=1)
    h = layer_norm(h, emb_ln_g, emb_ln_b)
    L = h.shape[1]
    valid = jnp.arange(L) >= N_PAD
    for l in range(DEPTH):
        mix = hybrid_mixer(h, w_in[l], w_gate_up[l], b_gate[l], b_forget[l], gla_norm_g[l],
                           fox_norm_g[l], w_out[l], valid)
        h = layer_norm(DN_ALPHA * h + mix, ln1_g[l], ln1_b[l])
        ffn = peer(h, peer_w_q[l], peer_sub_keys[l], peer_u[l], peer_v[l])
        h = layer_norm(DN_ALPHA * h + ffn, ln2_g[l], ln2_b[l])
    return h[:, PREFIX:]
```

```python
import numpy as np
from contextlib import ExitStack
import concourse.bass as bass
import concourse.mybir as mybir
from concourse.bass_utils import run_bass_kernel_spmd

ALU = mybir.AluOpType
AF = mybir.ActivationFunctionType
AX = mybir.AxisListType
F32 = mybir.dt.float32
BF16 = mybir.dt.bfloat16
U32 = mybir.dt.uint32

ENGINES = ("pe", "act", "dve", "pool", "sp")
SEM_ROLL = 30000
DMA_POOL = 12
NOSYNC_SAME = ("pe",)

D = 1024
SEQ = 4096
NBLK = 33
INW = 3096
QA0, KA0, VA0, RA0, GA0, QB0, KB0, VB0, FB0 = 0, 256, 512, 1024, 1536, 1552, 2064, 2576, 3088
DN_ALPHA = 2.0 ** 0.25
EPS = 1e-5
NEGBIG = -30000.0


class _Op:
    __slots__ = ("eng", "fn", "deps", "dma", "signal", "ticket", "dsem", "dval", "idx", "prev_same_sem", "noop")

    def __init__(self, eng, fn, deps, dma, idx):
        self.eng = eng
        self.fn = fn
        self.deps = deps
        self.dma = dma
        self.signal = dma
        self.ticket = None
        self.dsem = None
        self.dval = None
        self.idx = idx
        self.prev_same_sem = None
        self.noop = False


class Prog:
    def __init__(self, same_engine_sync=True):
        self.ops = []
        self.last_w = {}
        self.readers = {}
        self.same_engine_sync = same_engine_sync

    def add(self, eng, fn, reads=(), writes=(), dma=False, extra_deps=()):
        idx = len(self.ops)
        deps = set(extra_deps)
        for b in reads:
            w = self.last_w.get(b)
            if w is not None:
                deps.add(w)
        for b in writes:
            w = self.last_w.get(b)
            if w is not None:
                deps.add(w)
            for r in self.readers.get(b, ()):
                deps.add(r)
        for b in reads:
            self.readers.setdefault(b, []).append(idx)
        for b in writes:
            self.last_w[b] = idx
            self.readers[b] = []
        deps.discard(idx)
        self.ops.append(_Op(eng, fn, deps, dma, idx))
        return idx

    def wait_only(self, eng, deps):
        idx = self.add(eng, lambda e: None, extra_deps=deps)
        self.ops[idx].noop = True
        return idx

    def tails(self):
        t = [op.idx for op in self.ops if op.dma]
        for en in ENGINES:
            lst = [op.idx for op in self.ops if op.eng == en and not op.dma and not op.noop]
            if lst:
                t.append(lst[-1])
        return t

    def emit(self, nc, stack):
        ops = self.ops
        for op in ops:
            nd = set()
            for d in op.deps:
                p = ops[d]
                if (not p.dma) and p.eng == op.eng and (op.eng in NOSYNC_SAME or not self.same_engine_sync) and not op.dma:
                    continue
                nd.add(d)
            op.deps = nd
            for d in nd:
                ops[d].signal = True
        cnt = {e: 0 for e in ENGINES}
        dcnt = {e: 0 for e in ENGINES}
        nsem = {e: 1 for e in ENGINES}
        dma_hist = {e: [] for e in ENGINES}
        for op in ops:
            if op.dma:
                n = dcnt[op.eng]
                dcnt[op.eng] += 1
                op.dsem = (op.eng, n % DMA_POOL)
                op.dval = 16 * (n // DMA_POOL + 1)
                if n >= DMA_POOL:
                    op.prev_same_sem = dma_hist[op.eng][n - DMA_POOL]
                dma_hist[op.eng].append(op.idx)
            elif op.signal:
                cnt[op.eng] += 1
                c = cnt[op.eng]
                op.ticket = ((c - 1) // SEM_ROLL, (c - 1) % SEM_ROLL + 1)
                nsem[op.eng] = max(nsem[op.eng], op.ticket[0] + 1)
        sems = {}
        for e in ENGINES:
            for k in range(nsem[e]):
                sems[("c", e, k)] = stack.enter_context(nc.semaphore(f"s_{e}_{k}"))
            if dcnt[e] > 0:
                for k in range(min(DMA_POOL, dcnt[e])):
                    sems[("d", e, k)] = stack.enter_context(nc.semaphore(f"d_{e}_{k}"))
        block = stack.enter_context(nc.Block())
        per_eng = {e: [op for op in ops if op.eng == e] for e in ENGINES}

        def run_engine(ename, eng):
            waited = {}
            for op in per_eng[ename]:
                need = {}
                deps = set(op.deps)
                if op.prev_same_sem is not None:
                    deps.add(op.prev_same_sem)
                for d in deps:
                    p = ops[d]
                    if p.dma:
                        key = ("d",) + p.dsem
                        val = p.dval
                    else:
                        key = ("c", p.eng, p.ticket[0])
                        val = p.ticket[1]
                    if waited.get(key, 0) >= val:
                        continue
                    if need.get(key, 0) < val:
                        need[key] = val
                for key, val in need.items():
                    eng.wait_ge(sems[key], val)
                    waited[key] = val
                inst = op.fn(eng)
                if inst is None:
                    continue
                if op.dma:
                    inst.then_inc(sems[("d",) + op.dsem], 16)
                elif op.signal:
                    inst.then_inc(sems[("c", op.eng, op.ticket[0])], 1)

        if per_eng["pe"]:
            @block.tensor
            def _(e):
                run_engine("pe", e)
        if per_eng["act"]:
            @block.scalar
            def _(e):
                run_engine("act", e)
        if per_eng["dve"]:
            @block.vector
            def _(e):
                run_engine("dve", e)
        if per_eng["pool"]:
            @block.gpsimd
            def _(e):
                run_engine("pool", e)
        if per_eng["sp"]:
            @block.sync
            def _(e):
                run_engine("sp", e)


class K:
    def __init__(self, P):
        self.P = P

    def mm(self, out, lhsT, rhs, start=True, stop=True, r=(), w=()):
        return self.P.add("pe", lambda e: e.matmul(out, lhsT=lhsT, rhs=rhs, start=start, stop=stop), r, w)

    def tr(self, out, in_, ident, r=(), w=()):
        return self.P.add("pe", lambda e: e.transpose(out=out, in_=in_, identity=ident), r, w)

    def act(self, out, in_, func, r=(), w=(), bias=None, scale=None):
        kw = {}
        if bias is not None:
            kw["bias"] = bias
        if scale is not None:
            kw["scale"] = scale
        return self.P.add("act", lambda e: e.activation(out=out, in_=in_, func=func, **kw), r, w)

    def acopy(self, out, in_, r=(), w=()):
        return self.P.add("act", lambda e: e.copy(out=out, in_=in_), r, w)

    def tt(self, eng, out, in0, in1, op, r=(), w=()):
        return self.P.add(eng, lambda e: e.tensor_tensor(out=out, in0=in0, in1=in1, op=op), r, w)

    def ts(self, eng, out, in0, s1, s2, op0, op1=None, r=(), w=()):
        if op1 is None:
            return self.P.add(eng, lambda e: e.tensor_scalar(out=out, in0=in0, scalar1=s1, scalar2=None, op0=op0), r, w)
        return self.P.add(eng, lambda e: e.tensor_scalar(out=out, in0=in0, scalar1=s1, scalar2=s2, op0=op0, op1=op1), r, w)

    def stt(self, eng, out, in0, scalar, in1, op0, op1, r=(), w=()):
        return self.P.add(eng, lambda e: e.scalar_tensor_tensor(out=out, in0=in0, scalar=scalar, in1=in1, op0=op0, op1=op1), r, w)

    def copy(self, eng, out, in_, r=(), w=()):
        return self.P.add(eng, lambda e: e.tensor_copy(out=out, in_=in_), r, w)

    def memset(self, eng, ap, val, r=(), w=()):
        return self.P.add(eng, lambda e: e.memset(ap, val), r, w)

    def reduce(self, out, in_, op, r=(), w=()):
        return self.P.add("dve", lambda e: e.tensor_reduce(out=out, in_=in_, axis=AX.X, op=op), r, w)

    def dma(self, q, out, in_, r=(), w=()):
        return self.P.add(q, lambda e: e.dma_start(out=out, in_=in_), r, w, dma=True)

    def op(self, eng, fn, r=(), w=()):
        return self.P.add(eng, fn, r, w)


def build(nblk=NBLK, ntile=16, debug=False, stage=99):
    nc = bass.Bass("TRN2", target_bir_lowering=False)
    dt_in = lambda name, shape, dt=F32: nc.dram_tensor(name, shape, dt, kind="ExternalInput").ap()
    x = dt_in("x", [SEQ, D])
    meta = dt_in("meta", [16, D])
    lnv = dt_in("lnv", [6, D])
    w_in = dt_in("w_in", [D, INW])
    wgu = dt_in("wgu", [17, 256])
    bfg = dt_in("bfg", [8])
    gng = dt_in("gng", [2, 512])
    w_out = dt_in("w_out", [D, D])
    w_q = dt_in("w_q", [D, 2048])
    skT = dt_in("skT", [16, 128, 128])
    uT = dt_in("uT", [128, 128, 1024])
    vv = dt_in("vv", [128, 128, 1024])
    out = nc.dram_tensor("out", [SEQ, D], F32, kind="ExternalOutput").ap()
    h1d = nc.dram_tensor("h1d", [SEQ, D], F32).ap()
    ub = nc.dram_tensor("ub", [128, 128, 1024], BF16).ap()
    vb = nc.dram_tensor("vb", [128, 128, 1024], BF16).ap()
    wqd = nc.dram_tensor("wqd", [128, 8, 2048], BF16).ap()
    dbg = {}
    if debug:
        for name, shape in (("d_h0", [128, D]), ("d_glog", [128, 256]), ("d_oa", [128, 512]), ("d_ob", [128, 512]),
                            ("d_logf", [128, 8]), ("d_h1", [128, D])):
            dbg[name] = nc.dram_tensor(name, [nblk] + shape, F32, kind="ExternalOutput").ap()

    P = Prog()
    k = K(P)
    out_dmas = []

    with ExitStack() as st:
        def sb(name, shape, dt):
            return st.enter_context(nc.sbuf_tensor(name, shape, dt))

        psb = [st.enter_context(nc.psum_tensor(f"ps{i}", [128, 512], F32)) for i in range(8)]
        gen_rr = [0]

        def bq(i, qs=(0, 1, 2, 3)):
            return [(f"ps{i}", q) for q in qs]

        def gbank():
            i = gen_rr[0] % 2
            gen_rr[0] += 1
            return psb[i], bq(i)

        def fbank(i):
            return psb[i], bq(i)

        ident = sb("ident", [128, 128], F32)
        tri = sb("tri", [128, 128], F32)
        wmid = sb("wmid", [128, 128], F32)
        sup = sb("sup", [128, 128], F32)
        ones = sb("ones", [128, 128], F32)
        sel63 = sb("sel63", [128, 128], F32)
        maskc = sb("maskc", [128, 128], F32)
        trib = sb("trib", [128, 128], BF16)
        iof = sb("iof", [128, 128], F32)
        dif = maskc
        pidx = sel63
        k.op("pool", lambda e: e.iota(iof[:], pattern=[[1, 128]], base=0, channel_multiplier=0,
                                      allow_small_or_imprecise_dtypes=True), w=["iof"])
        k.op("pool", lambda e: e.iota(dif[:], pattern=[[1, 128]], base=0, channel_multiplier=-1,
                                      allow_small_or_imprecise_dtypes=True), w=["maskc"])
        k.op("pool", lambda e: e.iota(pidx[:], pattern=[[0, 128]], base=0, channel_multiplier=1,
                                      allow_small_or_imprecise_dtypes=True), w=["sel63"])
        k.ts("dve", ident[:], dif[:], 0.0, None, ALU.is_equal, r=["maskc"], w=["ident"])
        k.ts("dve", tri[:], dif[:], 0.0, None, ALU.is_ge, r=["maskc"], w=["tri"])
        k.ts("dve", sup[:], dif[:], 0.0, None, ALU.is_lt, r=["maskc"], w=["sup"])
        k.ts("dve", sel63[:], pidx[:], 63.0, None, ALU.is_le, r=["sel63"], w=["sel63"])
        k.tt("dve", wmid[:], tri[:], sel63[:], ALU.subtract, r=["tri", "sel63"], w=["wmid"])
        k.memset("dve", ones[:], 1.0, w=["ones"])
        epsc = sb("epsc", [128, 1], F32)
        k.memset("dve", epsc[:], EPS, w=["epsc"])
        k.ts("dve", maskc[:], tri[:], 0.125, None, ALU.mult, r=["tri"], w=["maskc"])
        k.copy("dve", trib[:], tri[:], r=["tri"], w=["trib"])

        tbl_jobs = []
        if ntile > 0:
            w_q_r = w_q.rearrange("(c p) n -> p c n", p=128)
            for c in range(8):
                k.dma("pool", wqd[:, c, :], w_q_r[:, c, :], w=["wqd"])
            for i in range(128):
                tbl_jobs.append((ub, uT, "ub", i))
                tbl_jobs.append((vb, vv, "vb", i))

        with ExitStack() as st1:
            def sb1(name, shape, dt):
                return st1.enter_context(nc.sbuf_tensor(name, shape, dt))

            winb = sb1("winb", [128, 8, INW], BF16)
            woutb = sb1("woutb", [128, 8, D], BF16)
            wgub = sb1("wgub", [32, 256], BF16)
            lng = sb1("lng", [128, 4, D], F32)
            gngb = sb1("gngb", [128, 2, 512], F32)
            bfb = sb1("bfb", [128, 8], F32)
            w_in_r = w_in.rearrange("(c p) n -> p c n", p=128)
            w_out_r = w_out.rearrange("(c p) n -> p c n", p=128)
            for c in range(8):
                k.dma("pool", winb[:, c, :], w_in_r[:, c, :], w=["winb"])
            for c in range(8):
                k.dma("pool", woutb[:, c, :], w_out_r[:, c, :], w=["woutb"])
            k.dma("pool", wgub[0:17, :], wgu, w=["wgub"])
            for j in range(4):
                k.dma("sp", lng[:, j, :], lnv[j].partition_broadcast(128), w=["lng"])
            for j in range(2):
                k.dma("sp", gngb[:, j, :], gng[j].partition_broadcast(128), w=["gngb"])
            k.dma("sp", bfb[:], bfg.partition_broadcast(128), w=["bfb"])

            KT = sb1("KT", [128, 4, nblk * 128], BF16)
            VP = sb1("VP", [128, nblk, 8, 65], BF16)
            negc = sb1("negc", [128, nblk, 8], F32)
            carry = sb1("carry", [128, 8], F32)
            cmid = sb1("cmid", [128, 8], F32)
            Sf = sb1("Sf", [128, 2, 128], F32)
            Sb = sb1("Sb", [128, 2, 128], BF16)
            k.memset("pool", VP[:], 1.0, w=[("VP", j) for j in range(nblk)])
            k.memset("pool", carry[:], 0.0, w=["carry"])
            k.memset("pool", Sf[:], 0.0, w=["Sf"])
            k.memset("pool", Sb[:], 0.0, w=["Sb"])

            xt = [sb1("xt0", [128, D], F32)]
            h0 = [sb1(f"h0{i}", [128, D], F32) for i in range(2)]
            h0T = sb1("h0T", [128, 8, 128], BF16)
            bst = sb1("bst", [128, 2, 6], F32)
            mv = sb1("mv", [128, 2], F32)
            rstd = sb1("rstd", [128, 1], F32)
            qTf = sb1("qTf", [128, 2, 4, 128], BF16)
            k.memset("pool", qTf[:], 0.0, w=["qTf"])
            qkg = sb1("qkg", [128, 4, 128], F32)
            gaT = sb1("gaT", [32, 128], BF16)
            k.memset("dve", gaT[:], 1.0, w=["gaT"])
            eg = sb1("eg", [128, 256], F32)
            glog = sb1("glog", [128, 256], F32)
            E1 = sb1("E1", [128, 2, 128], F32)
            E2 = sb1("E2", [128, 2, 128], F32)
            E3 = sb1("E3", [128, 2, 128], F32)
            Erev = sb1("Erev", [128, 256], F32)
            q_in = sb1("q_in", [128, 2, 128], BF16)
            k_in = sb1("k_in", [128, 2, 2, 128], BF16)
            k.memset("pool", k_in[:], 0.0, w=["k_in"])
            q_dec = sb1("q_dec", [128, 2, 2, 128], BF16)
            k.memset("pool", q_dec[:], 0.0, w=["q_dec"])
            k_dec = sb1("k_dec", [128, 256], BF16)
            vg = sb1("vg", [128, 512], BF16)
            sr = sb1("sr", [128, 512], F32)
            AM = sb1("AM", [128, 4, 128], BF16)
            o_sb = sb1("o_sb", [128, 4, 128], F32)
            sq = sb1("sq", [128, 512], F32)
            ss = sb1("ss", [128, 8], F32)
            zt = sb1("zt", [128, 8], F32)
            logf = sb1("logf", [128, 8], F32)
            biasb = sb1("biasb", [128, nblk, 8], F32)
            NPT = 8
            PT = [sb1(f"PT{i}", [128, 128], BF16) for i in range(NPT)]
            rinv = sb1("rinv", [128, 8], F32)
            on = o_sb[:].rearrange("p a (b c) -> p (a b) c", c=64)
            ocat = sb1("ocat", [128, D], F32)
            ocatT = sb1("ocatT", [128, 8, 128], BF16)
            rr = sb1("rr", [128, D], F32)

            def layer_norm(src, dst, gi, sname_, dname_):
                for hh in range(2):
                    k.op("dve", lambda e, hh=hh: e.bn_stats(out=bst[:, hh, :], in_=src[:, hh * 512:(hh + 1) * 512]),
                         r=[sname_], w=["bst"])
                k.op("dve", lambda e: e.bn_aggr(out=mv[:], in_=bst[:].rearrange("p a b -> p (a b)")), r=["bst"], w=["mv"])
                k.act(rstd[:], mv[:, 1:2], AF.Ln, bias=epsc[:, 0:1], r=["mv", "epsc"], w=["rstd"])
                k.act(rstd[:], rstd[:], AF.Exp, scale=-0.5, r=["rstd"], w=["rstd"])
                k.ts("dve", dst, src, mv[:, 0:1], rstd[:], ALU.subtract, ALU.mult, r=[sname_, "mv", "rstd"], w=[dname_])
                k.tt("pool", dst, dst, lng[:, gi, :], ALU.mult, r=[dname_, "lng"], w=[dname_])
                k.tt("pool", dst, dst, lng[:, gi + 1, :], ALU.add, r=[dname_, "lng"], w=[dname_])

            rrc = {"pt": 0, "sc": 0}

            def front(b):
                xb_ = xt[0]
                hb = h0[b % 2]
                xname = "xt0"
                hname = f"h0{b % 2}"
                for _ in range(8):
                    if tbl_jobs:
                        dst_, src_, nm_, i_ = tbl_jobs.pop(0)
                        k.dma("pool", dst_[i_], src_[i_], w=[(nm_, i_)])
                if b == 0:
                    k.memset("pool", xb_[:], 0.0, w=[xname])
                    k.dma("sp", xb_[112:128, :], meta, w=[xname])
                else:
                    k.dma("sp", xb_[:], x[(b - 1) * 128:b * 128, :], w=[xname])
                layer_norm(xb_[:], hb[:], 0, xname, hname)
                if debug:
                    k.dma("sp", dbg["d_h0"][b], hb[:], r=[hname])
                for half in range(2):
                    yield
                    pt_, pn = gbank()
                    for j in range(4):
                        c = half * 4 + j
                        k.tr(pt_[:, j * 128:(j + 1) * 128], hb[:, c * 128:(c + 1) * 128], ident[:], r=[hname, "ident"], w=pn)
                    k.acopy(h0T[:, half * 4:(half + 1) * 4, :], pt_[:].rearrange("p (a b) -> p a b", a=4), r=pn, w=["h0T"])
                yield
                pt_, pn = gbank()
                for j, c0 in enumerate((QA0, QA0 + 128, KA0, KA0 + 128)):
                    for c in range(8):
                        k.mm(pt_[:, j * 128:(j + 1) * 128], winb[:, c, c0:c0 + 128], h0T[:, c, :], start=(c == 0), stop=(c == 7),
                             r=["winb", "h0T"], w=pn)
                k.acopy(qkg[:], pt_[:].rearrange("p (a b) -> p a b", a=4), r=pn, w=["qkg"])
                yield
                pt_, pn = gbank()
                for j in range(4):
                    for c in range(8):
                        k.mm(pt_[:, j * 128:(j + 1) * 128], winb[:, c, KB0 + j * 128:KB0 + (j + 1) * 128], h0T[:, c, :],
                             start=(c == 0), stop=(c == 7), r=["winb", "h0T"], w=pn)
                k.acopy(KT[:, :, b * 128:(b + 1) * 128], pt_[:].rearrange("p (a b) -> p a b", a=4), r=pn, w=[("KT", b)])
                yield
                pt_, pn = gbank()
                for c in range(8):
                    k.mm(pt_[:, 0:128], winb[:, c, GA0:GA0 + 128], h0T[:, c, :], start=(c == 0), stop=(c == 7),
                         r=["winb", "h0T"], w=pn)
                for c in range(8):
                    k.mm(pt_[:, 128:256], h0T[:, c, :], winb[:, c, FB0 - 120:FB0 + 8], start=(c == 0), stop=(c == 7),
                         r=["winb", "h0T"], w=pn)
                k.acopy(gaT[0:16, :], pt_[0:16, 0:128], r=pn, w=["gaT"])
                k.acopy(zt[:], pt_[:, 248:256], r=pn, w=["zt"])
                k.tt("dve", zt[:], zt[:], bfb[:], ALU.add, r=["zt", "bfb"], w=["zt"])
                yield
                pk_, pkn = fbank(2)
                for c in range(8):
                    k.mm(pk_[:, 0:256], h0T[:, c, :], winb[:, c, KA0:KA0 + 256], start=(c == 0), stop=(c == 7),
                         r=["winb", "h0T"], w=pkn)
                yield
                pt_, pn = gbank()
                for c in range(8):
                    k.mm(pt_[:, :], h0T[:, c, :], winb[:, c, VA0:VA0 + 512], start=(c == 0), stop=(c == 7),
                         r=["winb", "h0T"], w=pn)
                k.acopy(vg[:], pt_[:], r=pn, w=["vg"])
                yield
                pz_, pzn = gbank()
                k.mm(pz_[:, 0:256], gaT[0:17, :], wgub[0:17, :], r=["gaT", "wgub"], w=pzn)
                k.act(eg[:], pz_[:, 0:256], AF.Exp, scale=-1.0, r=pzn, w=["eg"])
                k.act(zt[:], zt[:], AF.Exp, scale=-1.0, r=["zt"], w=["zt"])
                k.act(eg[:], eg[:], AF.Ln, bias=1.0, r=["eg"], w=["eg"])
                k.act(zt[:], zt[:], AF.Ln, bias=1.0, r=["zt"], w=["zt"])
                k.ts("dve", glog[:], eg[:], -1.0 / 16.0, None, ALU.mult, r=["eg"], w=["glog"])
                k.ts("dve", logf[:], zt[:], -1.0, None, ALU.mult, r=["zt"], w=["logf"])
                if debug:
                    k.dma("sp", dbg["d_glog"][b], glog[:], r=["glog"])
                    k.dma("sp", dbg["d_logf"][b], logf[:], r=["logf"])
                yield
                pt_, pn = gbank()
                for c in range(8):
                    k.mm(pt_[:, :], h0T[:, c, :], winb[:, c, RA0:RA0 + 512], start=(c == 0), stop=(c == 7),
                         r=["winb", "h0T"], w=pn)
                k.act(sr[:], pt_[:], AF.Silu, r=pn, w=["sr"])
                yield
                pt_, pn = gbank()
                for c in range(8):
                    k.mm(pt_[:, :], h0T[:, c, :], winb[:, c, VB0:VB0 + 512], start=(c == 0), stop=(c == 7),
                         r=["winb", "h0T"], w=pn)
                k.acopy(VP[:, b, :, 0:64], pt_[:].rearrange("p (h d) -> p h d", h=8), r=pn, w=[("VP", b)])

                yield
                pd_, pdn = fbank(3)
                for p_ in range(2):
                    k.mm(pd_[:, p_ * 128:(p_ + 1) * 128], glog[:, p_ * 128:(p_ + 1) * 128], wmid[:], r=["glog", "wmid"], w=pdn)
                    k.mm(pd_[:, 256 + p_ * 128:256 + (p_ + 1) * 128], glog[:, p_ * 128:(p_ + 1) * 128], tri[:],
                         r=["glog", "tri"], w=pdn)
                k.mm(pk_[:, 256:512], sup[:], glog[:], r=["glog", "sup"], w=pkn)
                k.act(E1[:], pd_[:, 0:256].rearrange("p (a b) -> p a b", a=2), AF.Exp, r=pdn, w=["E1"])
                k.act(E2[:], pd_[:, 0:256].rearrange("p (a b) -> p a b", a=2), AF.Exp, scale=-1.0, r=pdn, w=["E2"])
                k.act(E3[:], pd_[:, 256:512].rearrange("p (a b) -> p a b", a=2), AF.Exp, r=pdn, w=["E3"])
                k.act(Erev[:], pk_[:, 256:512], AF.Exp, r=pkn, w=["Erev"])
                k.tt("dve", q_in[:], qkg[:, 0:2, :], E1[:], ALU.mult, r=["qkg", "E1"], w=["q_in"])
                for par in range(2):
                    rs = slice(par * 64, (par + 1) * 64)
                    k.tt("dve", k_in[rs, par, :, :], qkg[rs, 2:4, :], E2[rs, :, :], ALU.mult, r=["qkg", "E2"], w=["k_in"])
                for par in range(2):
                    rs = slice(par * 64, (par + 1) * 64)
                    k.stt("dve", q_dec[rs, par, :, :], qkg[rs, 0:2, :], 0.125, E3[rs, :, :], ALU.mult, ALU.mult, r=["qkg", "E3"], w=["q_dec"])
                k.tt("dve", k_dec[:], pk_[:, 0:256], Erev[:], ALU.mult, r=pkn + ["Erev"], w=["k_dec"])
                if b == 0:
                    k.memset("dve", k_in[:, :, :, 0:112], 0.0, w=["k_in"])
                    k.memset("dve", k_dec[0:112, :], 0.0, w=["k_dec"])
                yield
                pa_, pan = fbank(3)
                for h in range(4):
                    p_, r0 = h // 2, (h % 2) * 64
                    k.mm(pa_[:, h * 128:(h + 1) * 128], k_in[:, h % 2, p_, :], q_in[:, p_, :],
                         r=["k_in", "q_in"], w=pan)
                k.tt("dve", AM[:], pa_[:].rearrange("p (a b) -> p a b", a=4), maskc[:].unsqueeze(1).to_broadcast([128, 4, 128]),
                     ALU.mult, r=pan + ["maskc"], w=["AM"])
                yield
                po_, pon = fbank(2)
                if b > 0:
                    for h in range(4):
                        p_, r0 = h // 2, (h % 2) * 64
                        k.mm(po_[:, h * 128:(h + 1) * 128], AM[:, h, :], vg[:, h * 128:(h + 1) * 128], start=True, stop=False,
                             r=["AM", "vg"], w=pon)
                        k.mm(po_[:, h * 128:(h + 1) * 128], q_dec[:, h % 2, p_, :], Sb[:, p_, :], start=False, stop=True,
                             r=["q_dec", "Sb"], w=pon)
                yield
                ps_, psn = fbank(3)
                for h in range(4):
                    p_ = h // 2
                    k.mm(ps_[:, h * 128:(h + 1) * 128], k_dec[:, p_ * 128:(p_ + 1) * 128], vg[:, h * 128:(h + 1) * 128],
                         r=["k_dec", "vg"], w=psn)
                for h in range(4):
                    p_, r0 = h // 2, (h % 2) * 64
                    k.stt("dve", Sf[r0:r0 + 64, p_, :], Sf[r0:r0 + 64, p_, :], E3[r0:r0 + 64, p_, 127:128],
                          ps_[r0:r0 + 64, h * 128:(h + 1) * 128], ALU.mult, ALU.add, r=["Sf", "E3"] + psn, w=["Sf"])
                k.copy("dve", Sb[:], Sf[:], r=["Sf"], w=["Sb"])

                yield
                pf_, pfn = gbank()
                k.mm(pf_[:, 0:8], tri[:], logf[:], r=["tri", "logf"], w=pfn)
                k.mm(pf_[:, 8:16], ones[:], logf[:], r=["ones", "logf"], w=pfn)
                k.mm(pf_[:, 16:24], sel63[:], logf[:], r=["sel63", "logf"], w=pfn)
                k.stt("dve", negc[:, b, :], pf_[:, 0:8], -1.0, carry[:], ALU.mult, ALU.subtract, r=pfn + ["carry"], w=[("negc", b)])
                if b == 0:
                    k.memset("dve", negc[0:112, 0, :], NEGBIG, w=[("negc", b)])
                k.tt("dve", cmid[:], pf_[:, 16:24], carry[:], ALU.add, r=pfn + ["carry"], w=["cmid"])
                k.tt("dve", carry[:], pf_[:, 8:16], carry[:], ALU.add, r=pfn + ["carry"], w=["carry"])


            def foxq(b):
                pt_, pn = gbank()
                for j in range(4):
                    for c in range(8):
                        k.mm(pt_[:, j * 128:(j + 1) * 128], winb[:, c, QB0 + j * 128:QB0 + (j + 1) * 128], h0T[:, c, :],
                             start=(c == 0), stop=(c == 7), r=["winb", "h0T"], w=pn)
                for par in range(2):
                    k.acopy(qTf[par * 64:(par + 1) * 64, par, :, :], pt_[par * 64:(par + 1) * 64, :].rearrange("p (a b) -> p a b", a=4),
                            r=pn, w=["qTf"])

            def back_pre(b):
                xb_ = xt[0]
                hb = h0[b % 2]
                xname = "xt0"
                hname = f"h0{b % 2}"
                po_, pon = fbank(2)
                k.acopy(o_sb[:], po_[:].rearrange("p (a b) -> p a b", a=4), r=pon, w=["o_sb"])
                if debug:
                    pass
                k.tt("dve", sq[:], o_sb[:].rearrange("p a b -> p (a b)"), o_sb[:].rearrange("p a b -> p (a b)"), ALU.mult,
                     r=["o_sb"], w=["sq"])
                k.reduce(ss[:, 0:4], sq[:].rearrange("p (a b) -> p a b", a=4), ALU.add, r=["sq"], w=["ss"])
                k.act(ss[:, 0:4], ss[:, 0:4], AF.Ln, bias=epsc[:, 0:1], scale=1.0 / 128.0, r=["ss", "epsc"], w=["ss"])
                k.act(ss[:, 0:4], ss[:, 0:4], AF.Exp, scale=-0.5, r=["ss"], w=["ss"])
                k.tt("dve", o_sb[:], o_sb[:], ss[:, 0:4].unsqueeze(2).to_broadcast([128, 4, 128]), ALU.mult, r=["o_sb", "ss"], w=["o_sb"])
                k.tt("pool", sq[:], o_sb[:].rearrange("p a b -> p (a b)"), gngb[:, 0, :], ALU.mult, r=["o_sb", "gngb"], w=["sq"])
                k.tt("pool", ocat[:, 0:512], sq[:], sr[:], ALU.mult, r=["sq", "sr"], w=["ocat_a"])
                if debug:
                    k.dma("sp", dbg["d_oa"][b], ocat[:, 0:512], r=["ocat_a"])

                k.tt("dve", biasb[:, 0:b + 1, :], negc[:, 0:b + 1, :], cmid[:].unsqueeze(1).to_broadcast([128, b + 1, 8]), ALU.add,
                     r=[("negc", j) for j in range(b + 1)] + ["cmid"], w=["biasb"])

            def attention(b, gen):
                iters = [(h, kb) for h in range(8) for kb in range(b + 1)]
                nbat = (len(iters) + 3) // 4
                binfo = {}
                for m in range(nbat + 2):
                    if gen is not None:
                        next(gen, None)
                    if m < nbat:
                        sb_i = 4 + (rrc["sc"] % 2)
                        rrc["sc"] += 1
                        lst = []
                        for j, (h, kb) in enumerate(iters[4 * m:4 * m + 4]):
                            k.mm(psb[sb_i][:, j * 128:(j + 1) * 128], KT[:, h // 2, kb * 128:(kb + 1) * 128], qTf[:, h % 2, h // 2, :],
                                 r=[("KT", kb), "qTf"], w=bq(sb_i))
                            lst.append((h, kb, j))
                        binfo[m] = (sb_i, lst, [])
                    if 1 <= m <= nbat:
                        sb_i, lst, pts = binfo[m - 1]
                        for (h, kb, j) in lst:
                            ptile = PT[rrc["pt"] % NPT]
                            pname = f"PT{rrc["pt"] % NPT}"
                            rrc["pt"] += 1
                            k.act(ptile[:], psb[sb_i][:, j * 128:(j + 1) * 128], AF.Exp, bias=biasb[:, kb, h:h + 1], scale=0.125,
                                  r=bq(sb_i) + ["biasb"], w=[pname])
                            if kb == b:
                                k.tt("pool", ptile[:], ptile[:], trib[:], ALU.mult, r=[pname, "trib"], w=[pname])
                            pts.append((ptile, pname))
                    if m >= 2:
                        sb_i, lst, pts = binfo.pop(m - 2)
                        for (h, kb, j), (ptile, pname) in zip(lst, pts):
                            obn = f"ps{6 + h // 4}"
                            oc0 = (h % 4) * 65
                            k.mm(psb[6 + h // 4][:, oc0:oc0 + 65], ptile[:], VP[:, kb, h, :], start=(kb == 0), stop=(kb == b),
                                 r=[pname, ("VP", kb)], w=[(obn, h % 4)])

            def back_post(b):
                xb_ = xt[0]
                hb = h0[b % 2]
                xname = "xt0"
                hname = f"h0{b % 2}"
                for g_ in range(2):
                    obank = psb[6 + g_]
                    ov = obank[:, 0:260].rearrange("p (h d) -> p h d", h=4)
                    onames = [(f"ps{6 + g_}", j) for j in range(4)]
                    k.op("dve", lambda e, ov=ov, g_=g_: e.reciprocal(out=rinv[:, g_ * 4:(g_ + 1) * 4], in_=ov[:, :, 64]),
                         r=onames, w=[("rinv", g_)])
                    k.tt("dve", on[:, g_ * 4:(g_ + 1) * 4, :], ov[:, :, 0:64],
                         rinv[:, g_ * 4:(g_ + 1) * 4].unsqueeze(2).to_broadcast([128, 4, 64]), ALU.mult,
                         r=onames + [("rinv", g_)], w=["o_sb"])
                onf = on[:].rearrange("p h d -> p (h d)")
                k.tt("dve", sq[:], onf, onf, ALU.mult, r=["o_sb"], w=["sq"])
                k.reduce(ss[:], sq[:].rearrange("p (a b) -> p a b", a=8), ALU.add, r=["sq"], w=["ss"])
                k.act(ss[:], ss[:], AF.Ln, bias=epsc[:, 0:1], scale=1.0 / 64.0, r=["ss", "epsc"], w=["ss"])
                k.act(ss[:], ss[:], AF.Exp, scale=-0.5, r=["ss"], w=["ss"])
                k.tt("dve", on[:], on[:], ss[:].unsqueeze(2).to_broadcast([128, 8, 64]), ALU.mult,
                     r=["o_sb", "ss"], w=["o_sb"])
                k.tt("pool", ocat[:, 512:1024], onf, gngb[:, 1, :], ALU.mult, r=["o_sb", "gngb"], w=["ocat_b"])
                if debug:
                    k.dma("sp", dbg["d_ob"][b], ocat[:, 512:1024], r=["ocat_b"])

                for half in range(2):
                    pt_, pn = gbank()
                    for j in range(4):
                        c = half * 4 + j
                        k.tr(pt_[:, j * 128:(j + 1) * 128], ocat[:, c * 128:(c + 1) * 128], ident[:],
                             r=["ocat_a", "ocat_b", "ident"], w=pn)
                    k.acopy(ocatT[:, half * 4:(half + 1) * 4, :], pt_[:].rearrange("p (a b) -> p a b", a=4), r=pn, w=["ocatT"])
                for half in range(2):
                    pt_, pn = gbank()
                    for c in range(8):
                        k.mm(pt_[:, :], ocatT[:, c, :], woutb[:, c, half * 512:(half + 1) * 512], start=(c == 0), stop=(c == 7),
                             r=["ocatT", "woutb"], w=pn)
                    k.stt("dve", rr[:, half * 512:(half + 1) * 512], hb[:, half * 512:(half + 1) * 512], DN_ALPHA, pt_[:],
                          ALU.mult, ALU.add, r=[hname] + pn, w=["rr"])
                layer_norm(rr[:], rr[:], 2, "rr", "rr")
                k.dma("sp", h1d[(b - 1) * 128:b * 128, :], rr[:], r=["rr"], w=[("h1d", b - 1)])
                if debug:
                    k.dma("sp", dbg["d_h1"][b], rr[:], r=["rr"])


            for _ in front(0):
                pass
            if nblk > 1:
                for _ in front(1):
                    pass
                foxq(1)
            for b in range(1, nblk):
                back_pre(b)
                gen = front(b + 1) if b + 1 < nblk else None
                attention(b, gen)
                if gen is not None:
                    for _ in gen:
                        pass
                    foxq(b + 1)
                back_post(b)

        while tbl_jobs:
            dst_, src_, nm_, i_ = tbl_jobs.pop(0)
            k.dma("pool", dst_[i_], src_[i_], w=[(nm_, i_)])
        fence = P.tails()
        for en in ENGINES:
            P.wait_only(en, fence)
        phase2(nc, P, k, psb, ident, iof, epsc, h1d, ub, vb, wqd, skT, lnv, out, ntile)

        P.wait_only("sp", P.tails())
        P.emit(nc, st)
    return nc


def phase2(nc, P, k, psb, ident, iof, epsc, h1d, ub, vb, wqd, skT, lnv, out, ntile):
    if ntile == 0:
        return
    with ExitStack() as st2:
        def sb2(name, shape, dt):
            return st2.enter_context(nc.sbuf_tensor(name, shape, dt))

        def bq(i, qs=(0, 1, 2, 3)):
            return [(f"ps{i}", q) for q in qs]

        grr = [0]
        gbase = [2]

        def gbank():
            i = gbase[0] + grr[0] % 2
            grr[0] += 1
            return psb[i], bq(i)

        wqs = [sb2(f"wqs{i}", [128, 8, 512], BF16) for i in range(2)]
        skTb = sb2("skTb", [128, 16, 128], BF16)
        ln2 = sb2("ln2", [128, 2, D], F32)
        k.dma("pool", skTb[:], skT.rearrange("hc d n -> d hc n"), w=["skTb"])
        for j in range(2):
            k.dma("sp", ln2[:, j, :], lnv[4 + j].partition_broadcast(128), w=["ln2"])
        h1t2 = [sb2(f"h1t{i}", [128, 2, D], F32) for i in range(2)]
        h1T2 = [sb2(f"h1T{i}", [128, 8, 256], BF16) for i in range(2)]
        qT = sb2("qT", [128, 16, 256], BF16)
        s_sb = sb2("s_sb", [128, 16, 128], F32)
        s_tmp = [sb2(f"s_tmp{i}", [128, 128], F32) for i in range(2)]
        topv = sb2("topv", [128, 16, 16], F32)
        topi = sb2("topi", [128, 16, 16], U32)
        topif = sb2("topif", [128, 16, 16], F32)
        cand = sb2("cand", [128, 8, 256], F32)
        cand_tmp = [sb2(f"cand_tmp{i}", [128, 256], F32) for i in range(2)]
        bv = sb2("bv", [128, 8, 16], F32)
        pos = sb2("pos", [128, 8, 16], U32)
        posf = sb2("posf", [128, 8, 16], F32)
        akf = sb2("akf", [128, 8, 16], F32)
        bkf = sb2("bkf", [128, 8, 16], F32)
        eq4 = sb2("eq4", [128, 8, 16, 16], F32)
        thr16 = sb2("thr16", [128, 16], F32)
        iota16 = sb2("iota16", [128, 16], F32)
        If = sb2("If", [128, 8, 16], F32)
        Jf = sb2("Jf", [128, 8, 16], F32)
        gf = sb2("gf", [128, 8, 16], F32)
        gsum = sb2("gsum", [128, 8], F32)
        IJG = sb2("IJG", [128, 2, 3, 128], F32)
        IT = sb2("IT", [128, 256], F32)
        JT = sb2("JT", [128, 256], F32)
        gT = sb2("gT", [128, 256], F32)
        NAB = 6
        AENG = "dve"
        GAENG = "dve"
        Bt = [sb2(f"Bt{i}", [128, 128], BF16) for i in range(NAB)]
        At = [sb2(f"At{i}", [128, 128], BF16) for i in range(NAB)]
        G_sb = sb2("G_sb", [128, 256, 128], BF16)
        NU = 4
        uTs = [sb2(f"uTs{i}", [128, 8, 128], BF16) for i in range(NU)]
        vs = [sb2(f"vs{i}", [128, D], BF16) for i in range(NU)]
        ga_sb = [sb2(f"ga_sb{i}", [128, 256], BF16) for i in range(2)]
        GA = [sb2(f"GA{i}", [128, 256], BF16) for i in range(4)]
        rr2 = sb2("rr2", [128, D], F32)
        bst = sb2("bst2", [128, 2, 6], F32)
        mv = sb2("mv2", [128, 2], F32)
        rstd = sb2("rstd2", [128, 1], F32)
        iob = sb2("iob", [128, 128], BF16)
        k.copy("dve", iob[:], iof[:], r=["iof"], w=["iob"])
        k.ts("dve", iota16[:], iof[:, 0:16], 1.0, None, ALU.mult, r=["iof"], w=["iota16"])
        k.ts("dve", thr16[:], iof[:, 0:16], 16.0, 15.5, ALU.mult, ALU.add, r=["iof"], w=["thr16"])

        def routing(ti):
            h1t = h1t2[ti % 2]
            h1T = h1T2[ti % 2]
            h1tn = f"h1t{ti % 2}"
            h1Tn = f"h1T{ti % 2}"
            nops = [len(P.ops)]

            def tick():
                if len(P.ops) - nops[0] >= 4:
                    nops[0] = len(P.ops)
                    return True
                return False
            for tb in range(2):
                blk = 2 * ti + tb
                k.dma("sp", h1t[:, tb, :], h1d[blk * 128:(blk + 1) * 128, :], r=[("h1d", blk)], w=[(h1tn, tb)])
            for tb in range(2):
                for half in range(2):
                    yield
                    pt_, pn = gbank()
                    for j in range(4):
                        c = half * 4 + j
                        k.tr(pt_[:, j * 128:(j + 1) * 128], h1t[:, tb, c * 128:(c + 1) * 128], ident[:], r=[(h1tn, tb), "ident"], w=pn)
                    k.acopy(h1T[:, half * 4:(half + 1) * 4, tb * 128:(tb + 1) * 128], pt_[:].rearrange("p (a b) -> p a b", a=4),
                            r=pn, w=[h1Tn])
            for hp in range(8):
                pt_, pn = gbank()
                for j in range(2):
                    yield
                    hc = hp * 2 + j
                    wq_ = wqs[(hc // 4) % 2]
                    wqn = f"wqs{(hc // 4) % 2}"
                    if hc % 4 == 0:
                        k.dma("sp", wq_[:], wqd[:, :, hc * 128:(hc + 4) * 128], r=["wqd"], w=[wqn])
                    for c in range(8):
                        k.mm(pt_[:, j * 256:(j + 1) * 256], wq_[:, c, (hc % 4) * 128:(hc % 4 + 1) * 128], h1T[:, c, :], start=(c == 0), stop=(c == 7),
                             r=[wqn, h1Tn], w=pn)
                k.acopy(qT[:, hp * 2:hp * 2 + 2, :], pt_[:].rearrange("p (a b) -> p a b", a=2), r=pn, w=["qT"])
            for tb in range(2):
                for g4 in range(4):
                    yield
                    pt_, pn = gbank()
                    for j in range(4):
                        hc = g4 * 4 + j
                        k.mm(pt_[:, j * 128:(j + 1) * 128], qT[:, hc, tb * 128:(tb + 1) * 128], skTb[:, hc, :], r=["qT", "skTb"], w=pn)
                    k.acopy(s_sb[:, g4 * 4:(g4 + 1) * 4, :], pt_[:].rearrange("p (a b) -> p a b", a=4), r=pn, w=["s_sb"])
                for hc in range(16):
                    yield
                    sv = s_sb[:, hc, :]
                    stp = s_tmp[hc % 2]
                    stn = f"s_tmp{hc % 2}"
                    k.op("dve", lambda e, sv=sv, hc=hc: e.max(out=topv[:, hc, 0:8], in_=sv), r=["s_sb"], w=["topv"])
                    k.op("dve", lambda e, sv=sv, hc=hc: e.max_index(out=topi[:, hc, 0:8], in_max=topv[:, hc, 0:8], in_values=sv),
                         r=["s_sb", "topv"], w=["topi"])
                    k.op("dve", lambda e, sv=sv, hc=hc, stp=stp: e.match_replace(out=stp[:], in_to_replace=topv[:, hc, 0:8], in_values=sv,
                                                                                  imm_value=-1e30), r=["s_sb", "topv"], w=[stn])
                    k.op("dve", lambda e, hc=hc, stp=stp: e.max(out=topv[:, hc, 8:16], in_=stp[:]), r=[stn], w=["topv"])
                    k.op("dve", lambda e, hc=hc, stp=stp: e.max_index(out=topi[:, hc, 8:16], in_max=topv[:, hc, 8:16], in_values=stp[:]),
                         r=[stn, "topv"], w=["topi"])
                k.copy("dve", topif[:], topi[:], r=["topi"], w=["topif"])
                tv = topv[:].rearrange("p (h c) a -> p h c a", c=2)
                tif = topif[:].rearrange("p (h c) a -> p h c a", c=2)
                yield
                k.tt("dve", cand[:].rearrange("p h (a b) -> p h a b", a=16), tv[:, :, 0, :].unsqueeze(3).to_broadcast([128, 8, 16, 16]),
                     tv[:, :, 1, :].unsqueeze(2).to_broadcast([128, 8, 16, 16]), ALU.add, r=["topv"], w=["cand"])
                for h in range(8):
                    yield
                    cv = cand[:, h, :]
                    ctp = cand_tmp[h % 2]
                    ctn = f"cand_tmp{h % 2}"
                    k.op("dve", lambda e, cv=cv, h=h: e.max(out=bv[:, h, 0:8], in_=cv), r=["cand"], w=["bv"])
                    k.op("dve", lambda e, cv=cv, h=h: e.max_index(out=pos[:, h, 0:8], in_max=bv[:, h, 0:8], in_values=cv),
                         r=["cand", "bv"], w=["pos"])
                    k.op("dve", lambda e, cv=cv, h=h, ctp=ctp: e.match_replace(out=ctp[:], in_to_replace=bv[:, h, 0:8], in_values=cv,
                                                                                imm_value=-1e30), r=["cand", "bv"], w=[ctn])
                    k.op("dve", lambda e, h=h, ctp=ctp: e.max(out=bv[:, h, 8:16], in_=ctp[:]), r=[ctn], w=["bv"])
                    k.op("dve", lambda e, h=h, ctp=ctp: e.max_index(out=pos[:, h, 8:16], in_max=bv[:, h, 8:16], in_values=ctp[:]),
                         r=[ctn, "bv"], w=["pos"])
                yield
                k.copy("dve", posf[:], pos[:], r=["pos"], w=["posf"])
                yield
                k.tt("dve", eq4[:, :, :, 0:15], posf[:].unsqueeze(3).to_broadcast([128, 8, 16, 15]),
                     thr16[:, 0:15].unsqueeze(1).unsqueeze(1).to_broadcast([128, 8, 16, 15]), ALU.is_ge, r=["posf", "thr16"], w=["eq4"])
                yield
                k.reduce(akf[:], eq4[:, :, :, 0:15], ALU.add, r=["eq4"], w=["akf"])
                k.stt("dve", bkf[:], akf[:], -16.0, posf[:], ALU.mult, ALU.add, r=["akf", "posf"], w=["bkf"])
                for (src, cidx, dst, dn) in ((akf, 0, If, "If"), (bkf, 1, Jf, "Jf")):
                    yield
                    k.tt("dve", eq4[:], src[:].unsqueeze(3).to_broadcast([128, 8, 16, 16]),
                         iota16[:].unsqueeze(1).unsqueeze(1).to_broadcast([128, 8, 16, 16]), ALU.is_equal,
                         r=["akf", "bkf", "iota16"], w=["eq4"])
                    yield
                    k.tt("dve", eq4[:], eq4[:], tif[:, :, cidx, :].unsqueeze(2).to_broadcast([128, 8, 16, 16]), ALU.mult,
                         r=["eq4", "topif"], w=["eq4"])
                    yield
                    k.reduce(dst[:], eq4[:], ALU.add, r=["eq4"], w=[dn])
                yield
                k.tt("dve", gf[:], bv[:], bv[:, :, 0:1].to_broadcast([128, 8, 16]), ALU.subtract, r=["bv"], w=["gf"])
                k.act(gf[:], gf[:], AF.Exp, r=["gf"], w=["gf"])
                k.reduce(gsum[:], gf[:], ALU.add, r=["gf"], w=["gsum"])
                k.op("dve", lambda e: e.reciprocal(out=gsum[:], in_=gsum[:]), r=["gsum"], w=["gsum"])
                k.tt("dve", gf[:], gf[:], gsum[:].unsqueeze(2).to_broadcast([128, 8, 16]), ALU.mult, r=["gf", "gsum"], w=["gf"])
                k.copy("dve", IJG[:, tb, 0, :], If[:].rearrange("p h k -> p (h k)"), r=["If"], w=[("IJG", tb)])
                k.copy("dve", IJG[:, tb, 1, :], Jf[:].rearrange("p h k -> p (h k)"), r=["Jf"], w=[("IJG", tb)])
                k.copy("dve", IJG[:, tb, 2, :], gf[:].rearrange("p h k -> p (h k)"), r=["gf"], w=[("IJG", tb)])
        def gbuild(ti):
            for tb in range(2):
                pt_, pn = gbank()
                for j in range(3):
                    k.tr(pt_[:, j * 128:(j + 1) * 128], IJG[:, tb, j, :], ident[:], r=[("IJG", tb), "ident"], w=pn)
                for j, (dst, dn) in enumerate(((IT, "IT"), (JT, "JT"), (gT, "gT"))):
                    k.acopy(dst[:, tb * 128:(tb + 1) * 128], pt_[:, j * 128:(j + 1) * 128], r=pn, w=[dn])
            for t in range(256):
                jb = t % NAB
                k.ts("dve", Bt[jb][:], iob[:], JT[:, t:t + 1], None, ALU.is_equal, r=["iob", "JT"], w=[f"Bt{jb}"])
                k.ts(AENG, At[jb][:], iob[:], IT[:, t:t + 1], gT[:, t:t + 1], ALU.is_equal, ALU.mult, r=["iob", "IT", "gT"], w=[f"At{jb}"])
                gb = 2 + (t // 4) % 2
                q_ = t % 4
                k.mm(psb[gb][:, q_ * 128:(q_ + 1) * 128], Bt[jb][:], At[jb][:], r=[f"Bt{jb}", f"At{jb}"], w=[(f"ps{gb}", q_)])
                if q_ == 3:
                    k.acopy(G_sb[:, t - 3:t + 1, :], psb[gb][:].rearrange("p (a b) -> p a b", a=4), r=bq(gb), w=["G_sb"])
        def gloop(ti, gen):
            h1T = h1T2[ti % 2]
            h1Tn = f"h1T{ti % 2}"
            LG = 3
            ginfo = {}
            for i in range(128 + LG):
                if i < 128:
                    ju = i % NU
                    k.dma("sp", uTs[ju][:], ub[i].rearrange("p (c e) -> p c e", c=8), r=[("ub", i)], w=[f"uTs{ju}"])
                    k.dma("act", vs[ju][:], vb[i], r=[("vb", i)], w=[f"vs{ju}"])
                    ab = i % 2
                    for c in range(8):
                        k.mm(psb[ab][:, 0:256], uTs[ju][:, c, :], h1T[:, c, :], start=(c == 0), stop=(c == 7), r=[f"uTs{ju}", h1Tn], w=bq(ab))
                    gs = ga_sb[i % 2]
                    gsn = f"ga_sb{i % 2}"
                    k.act(gs[:], psb[ab][:, 0:256], AF.Gelu, r=bq(ab), w=[gsn])
                    gA = GA[i % 4]
                    gAn = f"GA{i % 4}"
                    k.tt(GAENG, gA[:], gs[:], G_sb[:, :, i], ALU.mult, r=[gsn, "G_sb"], w=[gAn])
                    ginfo[i] = (gA, gAn, ju)
                if gen is not None:
                    next(gen, None)
                if i >= LG:
                    i2 = i - LG
                    gA, gAn, ju = ginfo.pop(i2)
                    for tb in range(2):
                        for half in range(2):
                            yb = 4 + tb * 2 + half
                            k.mm(psb[yb][:, :], gA[:, tb * 128:(tb + 1) * 128], vs[ju][:, half * 512:(half + 1) * 512],
                                 start=(i2 == 0), stop=(i2 == 127), r=[gAn, f"vs{ju}"], w=bq(yb))
        def finalize(ti):
            h1t = h1t2[ti % 2]
            h1tn = f"h1t{ti % 2}"
            for tb in range(2):
                blk = 2 * ti + tb
                for half in range(2):
                    yb = 4 + tb * 2 + half
                    k.stt("dve", rr2[:, half * 512:(half + 1) * 512], h1t[:, tb, half * 512:(half + 1) * 512], DN_ALPHA, psb[yb][:, :],
                          ALU.mult, ALU.add, r=[(h1tn, tb)] + bq(yb), w=["rr2"])
                for hh in range(2):
                    k.op("dve", lambda e, hh=hh: e.bn_stats(out=bst[:, hh, :], in_=rr2[:, hh * 512:(hh + 1) * 512]), r=["rr2"], w=["bst2"])
                k.op("dve", lambda e: e.bn_aggr(out=mv[:], in_=bst[:].rearrange("p a b -> p (a b)")), r=["bst2"], w=["mv2"])
                k.act(rstd[:], mv[:, 1:2], AF.Ln, bias=epsc[:, 0:1], r=["mv2", "epsc"], w=["rstd2"])
                k.act(rstd[:], rstd[:], AF.Exp, scale=-0.5, r=["rstd2"], w=["rstd2"])
                k.ts("dve", rr2[:], rr2[:], mv[:, 0:1], rstd[:], ALU.subtract, ALU.mult, r=["rr2", "mv2", "rstd2"], w=["rr2"])
                k.tt("pool", rr2[:], rr2[:], ln2[:, 0, :], ALU.mult, r=["rr2", "ln2"], w=["rr2"])
                k.tt("pool", rr2[:], rr2[:], ln2[:, 1, :], ALU.add, r=["rr2", "ln2"], w=["rr2"])
                k.dma("sp", out[blk * 128:(blk + 1) * 128, :], rr2[:], r=["rr2"])


        gbase[0] = 0
        for _ in routing(0):
            pass
        for ti in range(ntile):
            gbase[0] = 2
            gbuild(ti)
            gen = routing(ti + 1) if ti + 1 < ntile else None
            gloop(ti, gen)
            if gen is not None:
                for _ in gen:
                    pass
            finalize(ti)

def _prep_shared(inputs):
    f = lambda a: np.ascontiguousarray(np.asarray(a, dtype=np.float32))
    sh = {}
    sh["meta"] = f(inputs["meta_tokens"])
    sh["lnv"] = f(np.stack([inputs["emb_ln_g"], inputs["emb_ln_b"], inputs["ln1_g"][0], inputs["ln1_b"][0],
                            inputs["ln2_g"][0], inputs["ln2_b"][0]]))
    sh["w_in"] = f(inputs["w_in"][0])
    sh["wgu"] = f(np.concatenate([inputs["w_gate_up"][0], inputs["b_gate"][0][None, :]], axis=0))
    sh["bfg"] = f(inputs["b_forget"][0])
    sh["gng"] = f(np.stack([inputs["gla_norm_g"][0], inputs["fox_norm_g"][0]]))
    sh["w_out"] = f(inputs["w_out"][0])
    sh["w_q"] = f(inputs["peer_w_q"][0])
    sk = np.asarray(inputs["peer_sub_keys"][0], dtype=np.float32)
    sh["skT"] = f(sk.transpose(0, 1, 3, 2).reshape(16, 128, 128))
    u = np.asarray(inputs["peer_u"][0], dtype=np.float32)
    sh["uT"] = f(u.reshape(128, 128, 8, 128).transpose(0, 3, 2, 1).reshape(128, 128, 1024))
    sh["vv"] = f(np.asarray(inputs["peer_v"][0], dtype=np.float32).reshape(128, 128, 1024))
    return sh


def kernel(**inputs):
    nc = build()
    sh = _prep_shared(inputs)
    x = np.asarray(inputs["x"], dtype=np.float32)
    in_maps = []
    for b in range(8):
        m = dict(sh)
        m["x"] = np.ascontiguousarray(x[b])
        in_maps.append(m)
    res = run_bass_kernel_spmd(nc, in_maps, core_ids=list(range(8)))
    return np.stack([np.asarray(r["out"], dtype=np.float32) for r in res.results], axis=0)
```

```python
import numpy as np
from contextlib import ExitStack
import concourse.bass as bass
import concourse.mybir as mybir
from concourse.bass_utils import run_bass_kernel_spmd

ALU = mybir.AluOpType
AF = mybir.ActivationFunctionType
AX = mybir.AxisListType
F32 = mybir.dt.float32
BF16 = mybir.dt.bfloat16
U32 = mybir.dt.uint32

ENGINES = ("pe", "act", "dve", "pool", "sp")
SEM_ROLL = 30000
DMA_POOL = 12
NOSYNC_SAME = ("pe",)

D = 1024
SEQ = 4096
NBLK = 33
INW = 3096
QA0, KA0, VA0, RA0, GA0, QB0, KB0, VB0, FB0 = 0, 256, 512, 1024, 1536, 1552, 2064, 2576, 3088
DN_ALPHA = 2.0 ** 0.25
EPS = 1e-5
NEGBIG = -30000.0


class _Op:
    __slots__ = ("eng", "fn", "deps", "dma", "signal", "ticket", "dsem", "dval", "idx", "prev_same_sem", "noop")

    def __init__(self, eng, fn, deps, dma, idx):
        self.eng = eng
        self.fn = fn
        self.deps = deps
        self.dma = dma
        self.signal = dma
        self.ticket = None
        self.dsem = None
        self.dval = None
        self.idx = idx
        self.prev_same_sem = None
        self.noop = False


class Prog:
    def __init__(self, same_engine_sync=True):
        self.ops = []
        self.last_w = {}
        self.readers = {}
        self.same_engine_sync = same_engine_sync

    def add(self, eng, fn, reads=(), writes=(), dma=False, extra_deps=()):
        idx = len(self.ops)
        deps = set(extra_deps)
        for b in reads:
            w = self.last_w.get(b)
            if w is not None:
                deps.add(w)
        for b in writes:
            w = self.last_w.get(b)
            if w is not None:
                deps.add(w)
            for r in self.readers.get(b, ()):
                deps.add(r)
        for b in reads:
            self.readers.setdefault(b, []).append(idx)
        for b in writes:
            self.last_w[b] = idx
            self.readers[b] = []
        deps.discard(idx)
        self.ops.append(_Op(eng, fn, deps, dma, idx))
        return idx

    def wait_only(self, eng, deps):
        idx = self.add(eng, lambda e: None, extra_deps=deps)
        self.ops[idx].noop = True
        return idx

    def tails(self):
        t = [op.idx for op in self.ops if op.dma]
        for en in ENGINES:
            lst = [op.idx for op in self.ops if op.eng == en and not op.dma and not op.noop]
            if lst:
                t.append(lst[-1])
        return t

    def emit(self, nc, stack):
        ops = self.ops
        for op in ops:
            nd = set()
            for d in op.deps:
                p = ops[d]
                if (not p.dma) and p.eng == op.eng and (op.eng in NOSYNC_SAME or not self.same_engine_sync) and not op.dma:
                    continue
                nd.add(d)
            op.deps = nd
            for d in nd:
                ops[d].signal = True
        cnt = {e: 0 for e in ENGINES}
        dcnt = {e: 0 for e in ENGINES}
        nsem = {e: 1 for e in ENGINES}
        dma_hist = {e: [] for e in ENGINES}
        for op in ops:
            if op.dma:
                n = dcnt[op.eng]
                dcnt[op.eng] += 1
                op.dsem = (op.eng, n % DMA_POOL)
                op.dval = 16 * (n // DMA_POOL + 1)
                if n >= DMA_POOL:
                    op.prev_same_sem = dma_hist[op.eng][n - DMA_POOL]
                dma_hist[op.eng].append(op.idx)
            elif op.signal:
                cnt[op.eng] += 1
                c = cnt[op.eng]
                op.ticket = ((c - 1) // SEM_ROLL, (c - 1) % SEM_ROLL + 1)
                nsem[op.eng] = max(nsem[op.eng], op.ticket[0] + 1)
        sems = {}
        for e in ENGINES:
            for k in range(nsem[e]):
                sems[("c", e, k)] = stack.enter_context(nc.semaphore(f"s_{e}_{k}"))
            if dcnt[e] > 0:
                for k in range(min(DMA_POOL, dcnt[e])):
                    sems[("d", e, k)] = stack.enter_context(nc.semaphore(f"d_{e}_{k}"))
        block = stack.enter_context(nc.Block())
        per_eng = {e: [op for op in ops if op.eng == e] for e in ENGINES}

        def run_engine(ename, eng):
            waited = {}
            for op in per_eng[ename]:
                need = {}
                deps = set(op.deps)
                if op.prev_same_sem is not None:
                    deps.add(op.prev_same_sem)
                for d in deps:
                    p = ops[d]
                    if p.dma:
                        key = ("d",) + p.dsem
                        val = p.dval
                    else:
                        key = ("c", p.eng, p.ticket[0])
                        val = p.ticket[1]
                    if waited.get(key, 0) >= val:
                        continue
                    if need.get(key, 0) < val:
                        need[key] = val
                for key, val in need.items():
                    eng.wait_ge(sems[key], val)
                    waited[key] = val
                inst = op.fn(eng)
                if inst is None:
                    continue
                if op.dma:
                    inst.then_inc(sems[("d",) + op.dsem], 16)
                elif op.signal:
                    inst.then_inc(sems[("c", op.eng, op.ticket[0])], 1)

        if per_eng["pe"]:
            @block.tensor
            def _(e):
                run_engine("pe", e)
        if per_eng["act"]:
            @block.scalar
            def _(e):
                run_engine("act", e)
        if per_eng["dve"]:
            @block.vector
            def _(e):
                run_engine("dve", e)
        if per_eng["pool"]:
            @block.gpsimd
            def _(e):
                run_engine("pool", e)
        if per_eng["sp"]:
            @block.sync
            def _(e):
                run_engine("sp", e)


class K:
    def __init__(self, P):
        self.P = P

    def mm(self, out, lhsT, rhs, start=True, stop=True, r=(), w=()):
        return self.P.add("pe", lambda e: e.matmul(out, lhsT=lhsT, rhs=rhs, start=start, stop=stop), r, w)

    def tr(self, out, in_, ident, r=(), w=()):
        return self.P.add("pe", lambda e: e.transpose(out=out, in_=in_, identity=ident), r, w)

    def act(self, out, in_, func, r=(), w=(), bias=None, scale=None):
        kw = {}
        if bias is not None:
            kw["bias"] = bias
        if scale is not None:
            kw["scale"] = scale
        return self.P.add("act", lambda e: e.activation(out=out, in_=in_, func=func, **kw), r, w)

    def acopy(self, out, in_, r=(), w=()):
        return self.P.add("act", lambda e: e.copy(out=out, in_=in_), r, w)

    def tt(self, eng, out, in0, in1, op, r=(), w=()):
        return self.P.add(eng, lambda e: e.tensor_tensor(out=out, in0=in0, in1=in1, op=op), r, w)

    def ts(self, eng, out, in0, s1, s2, op0, op1=None, r=(), w=()):
        if op1 is None:
            return self.P.add(eng, lambda e: e.tensor_scalar(out=out, in0=in0, scalar1=s1, scalar2=None, op0=op0), r, w)
        return self.P.add(eng, lambda e: e.tensor_scalar(out=out, in0=in0, scalar1=s1, scalar2=s2, op0=op0, op1=op1), r, w)

    def stt(self, eng, out, in0, scalar, in1, op0, op1, r=(), w=()):
        return self.P.add(eng, lambda e: e.scalar_tensor_tensor(out=out, in0=in0, scalar=scalar, in1=in1, op0=op0, op1=op1), r, w)

    def copy(self, eng, out, in_, r=(), w=()):
        return self.P.add(eng, lambda e: e.tensor_copy(out=out, in_=in_), r, w)

    def memset(self, eng, ap, val, r=(), w=()):
        return self.P.add(eng, lambda e: e.memset(ap, val), r, w)

    def reduce(self, out, in_, op, r=(), w=()):
        return self.P.add("dve", lambda e: e.tensor_reduce(out=out, in_=in_, axis=AX.X, op=op), r, w)

    def dma(self, q, out, in_, r=(), w=()):
        return self.P.add(q, lambda e: e.dma_start(out=out, in_=in_), r, w, dma=True)

    def op(self, eng, fn, r=(), w=()):
        return self.P.add(eng, fn, r, w)


def build(nblk=NBLK, ntile=16, debug=False, stage=99):
    nc = bass.Bass("TRN2", target_bir_lowering=False)
    dt_in = lambda name, shape, dt=F32: nc.dram_tensor(name, shape, dt, kind="ExternalInput").ap()
    x = dt_in("x", [SEQ, D])
    meta = dt_in("meta", [16, D])
    lnv = dt_in("lnv", [6, D])
    w_in = dt_in("w_in", [D, INW])
    wgu = dt_in("wgu", [17, 256])
    bfg = dt_in("bfg", [8])
    gng = dt_in("gng", [2, 512])
    w_out = dt_in("w_out", [D, D])
    w_q = dt_in("w_q", [D, 2048])
    skT = dt_in("skT", [16, 128, 128])
    uT = dt_in("uT", [128, 128, 1024])
    vv = dt_in("vv", [128, 128, 1024])
    out = nc.dram_tensor("out", [SEQ, D], F32, kind="ExternalOutput").ap()
    h1d = nc.dram_tensor("h1d", [SEQ, D], F32).ap()
    ub = nc.dram_tensor("ub", [128, 128, 1024], BF16).ap()
    vb = nc.dram_tensor("vb", [128, 128, 1024], BF16).ap()
    wqd = nc.dram_tensor("wqd", [128, 8, 2048], BF16).ap()
    dbg = {}
    if debug:
        for name, shape in (("d_h0", [128, D]), ("d_glog", [128, 256]), ("d_oa", [128, 512]), ("d_ob", [128, 512]),
                            ("d_logf", [128, 8]), ("d_h1", [128, D])):
            dbg[name] = nc.dram_tensor(name, [nblk] + shape, F32, kind="ExternalOutput").ap()

    P = Prog()
    k = K(P)
    out_dmas = []

    with ExitStack() as st:
        def sb(name, shape, dt):
            return st.enter_context(nc.sbuf_tensor(name, shape, dt))

        psb = [st.enter_context(nc.psum_tensor(f"ps{i}", [128, 512], F32)) for i in range(8)]
        gen_rr = [0]

        def bq(i, qs=(0, 1, 2, 3)):
            return [(f"ps{i}", q) for q in qs]

        def gbank():
            i = gen_rr[0] % 2
            gen_rr[0] += 1
            return psb[i], bq(i)

        def fbank(i):
            return psb[i], bq(i)

        ident = sb("ident", [128, 128], F32)
        tri = sb("tri", [128, 128], F32)
        wmid = sb("wmid", [128, 128], F32)
        sup = sb("sup", [128, 128], F32)
        ones = sb("ones", [128, 128], F32)
        sel63 = sb("sel63", [128, 128], F32)
        maskc = sb("maskc", [128, 128], F32)
        trib = sb("trib", [128, 128], BF16)
        iof = sb("iof", [128, 128], F32)
        dif = maskc
        pidx = sel63
        k.op("pool", lambda e: e.iota(iof[:], pattern=[[1, 128]], base=0, channel_multiplier=0,
                                      allow_small_or_imprecise_dtypes=True), w=["iof"])
        k.op("pool", lambda e: e.iota(dif[:], pattern=[[1, 128]], base=0, channel_multiplier=-1,
                                      allow_small_or_imprecise_dtypes=True), w=["maskc"])
        k.op("pool", lambda e: e.iota(pidx[:], pattern=[[0, 128]], base=0, channel_multiplier=1,
                                      allow_small_or_imprecise_dtypes=True), w=["sel63"])
        k.ts("dve", ident[:], dif[:], 0.0, None, ALU.is_equal, r=["maskc"], w=["ident"])
        k.ts("dve", tri[:], dif[:], 0.0, None, ALU.is_ge, r=["maskc"], w=["tri"])
        k.ts("dve", sup[:], dif[:], 0.0, None, ALU.is_lt, r=["maskc"], w=["sup"])
        k.ts("dve", sel63[:], pidx[:], 63.0, None, ALU.is_le, r=["sel63"], w=["sel63"])
        k.tt("dve", wmid[:], tri[:], sel63[:], ALU.subtract, r=["tri", "sel63"], w=["wmid"])
        k.memset("dve", ones[:], 1.0, w=["ones"])
        epsc = sb("epsc", [128, 1], F32)
        k.memset("dve", epsc[:], EPS, w=["epsc"])
        k.ts("dve", maskc[:], tri[:], 0.125, None, ALU.mult, r=["tri"], w=["maskc"])
        k.copy("dve", trib[:], tri[:], r=["tri"], w=["trib"])

        tbl_jobs = []
        if ntile > 0:
            w_q_r = w_q.rearrange("(c p) n -> p c n", p=128)
            for c in range(8):
                k.dma("pool", wqd[:, c, :], w_q_r[:, c, :], w=["wqd"])
            for i in range(128):
                tbl_jobs.append((ub, uT, "ub", i))
                tbl_jobs.append((vb, vv, "vb", i))

        with ExitStack() as st1:
            def sb1(name, shape, dt):
                return st1.enter_context(nc.sbuf_tensor(name, shape, dt))

            winb = sb1("winb", [128, 8, INW], BF16)
            woutb = sb1("woutb", [128, 8, D], BF16)
            wgub = sb1("wgub", [32, 256], BF16)
            lng = sb1("lng", [128, 4, D], F32)
            gngb = sb1("gngb", [128, 2, 512], F32)
            bfb = sb1("bfb", [128, 8], F32)
            w_in_r = w_in.rearrange("(c p) n -> p c n", p=128)
            w_out_r = w_out.rearrange("(c p) n -> p c n", p=128)
            for c in range(8):
                k.dma("pool", winb[:, c, :], w_in_r[:, c, :], w=["winb"])
            for c in range(8):
                k.dma("pool", woutb[:, c, :], w_out_r[:, c, :], w=["woutb"])
            k.dma("pool", wgub[0:17, :], wgu, w=["wgub"])
            for j in range(4):
                k.dma("sp", lng[:, j, :], lnv[j].partition_broadcast(128), w=["lng"])
            for j in range(2):
                k.dma("sp", gngb[:, j, :], gng[j].partition_broadcast(128), w=["gngb"])
            k.dma("sp", bfb[:], bfg.partition_broadcast(128), w=["bfb"])

            KT = sb1("KT", [128, 4, nblk * 128], BF16)
            VP = sb1("VP", [128, nblk, 8, 65], BF16)
            negc = sb1("negc", [128, nblk, 8], F32)
            carry = sb1("carry", [128, 8], F32)
            cmid = sb1("cmid", [128, 8], F32)
            Sf = sb1("Sf", [128, 2, 128], F32)
            Sb = sb1("Sb", [128, 2, 128], BF16)
            k.memset("pool", VP[:], 1.0, w=[("VP", j) for j in range(nblk)])
            k.memset("pool", carry[:], 0.0, w=["carry"])
            k.memset("pool", Sf[:], 0.0, w=["Sf"])
            k.memset("pool", Sb[:], 0.0, w=["Sb"])

            xt = [sb1("xt0", [128, D], F32)]
            h0 = [sb1(f"h0{i}", [128, D], F32) for i in range(2)]
            h0T = sb1("h0T", [128, 8, 128], BF16)
            bst = sb1("bst", [128, 2, 6], F32)
            mv = sb1("mv", [128, 2], F32)
            rstd = sb1("rstd", [128, 1], F32)
            qTf = sb1("qTf", [128, 2, 4, 128], BF16)
            k.memset("pool", qTf[:], 0.0, w=["qTf"])
            qkg = sb1("qkg", [128, 4, 128], F32)
            gaT = sb1("gaT", [32, 128], BF16)
            k.memset("dve", gaT[:], 1.0, w=["gaT"])
            eg = sb1("eg", [128, 256], F32)
            glog = sb1("glog", [128, 256], F32)
            E1 = sb1("E1", [128, 2, 128], F32)
            E2 = sb1("E2", [128, 2, 128], F32)
            E3 = sb1("E3", [128, 2, 128], F32)
            Erev = sb1("Erev", [128, 256], F32)
            q_in = sb1("q_in", [128, 2, 128], BF16)
            k_in = sb1("k_in", [128, 2, 2, 128], BF16)
            k.memset("pool", k_in[:], 0.0, w=["k_in"])
            q_dec = sb1("q_dec", [128, 2, 2, 128], BF16)
            k.memset("pool", q_dec[:], 0.0, w=["q_dec"])
            k_dec = sb1("k_dec", [128, 256], BF16)
            vg = sb1("vg", [128, 512], BF16)
            sr = sb1("sr", [128, 512], F32)
            AM = sb1("AM", [128, 4, 128], BF16)
            o_sb = sb1("o_sb", [128, 4, 128], F32)
            sq = sb1("sq", [128, 512], F32)
            ss = sb1("ss", [128, 8], F32)
            zt = sb1("zt", [128, 8], F32)
            logf = sb1("logf", [128, 8], F32)
            biasb = sb1("biasb", [128, nblk, 8], F32)
            NPT = 8
            PT = [sb1(f"PT{i}", [128, 128], BF16) for i in range(NPT)]
            rinv = sb1("rinv", [128, 8], F32)
            on = o_sb[:].rearrange("p a (b c) -> p (a b) c", c=64)
            ocat = sb1("ocat", [128, D], F32)
            ocatT = sb1("ocatT", [128, 8, 128], BF16)
            rr = sb1("rr", [128, D], F32)

            def layer_norm(src, dst, gi, sname_, dname_):
                for hh in range(2):
                    k.op("dve", lambda e, hh=hh: e.bn_stats(out=bst[:, hh, :], in_=src[:, hh * 512:(hh + 1) * 512]),
                         r=[sname_], w=["bst"])
                k.op("dve", lambda e: e.bn_aggr(out=mv[:], in_=bst[:].rearrange("p a b -> p (a b)")), r=["bst"], w=["mv"])
                k.act(rstd[:], mv[:, 1:2], AF.Ln, bias=epsc[:, 0:1], r=["mv", "epsc"], w=["rstd"])
                k.act(rstd[:], rstd[:], AF.Exp, scale=-0.5, r=["rstd"], w=["rstd"])
                k.ts("dve", dst, src, mv[:, 0:1], rstd[:], ALU.subtract, ALU.mult, r=[sname_, "mv", "rstd"], w=[dname_])
                k.tt("pool", dst, dst, lng[:, gi, :], ALU.mult, r=[dname_, "lng"], w=[dname_])
                k.tt("pool", dst, dst, lng[:, gi + 1, :], ALU.add, r=[dname_, "lng"], w=[dname_])

            rrc = {"pt": 0, "sc": 0}

            def front(b):
                xb_ = xt[0]
                hb = h0[b % 2]
                xname = "xt0"
                hname = f"h0{b % 2}"
                for _ in range(8):
                    if tbl_jobs:
                        dst_, src_, nm_, i_ = tbl_jobs.pop(0)
                        k.dma("pool", dst_[i_], src_[i_], w=[(nm_, i_)])
                if b == 0:
                    k.memset("pool", xb_[:], 0.0, w=[xname])
                    k.dma("sp", xb_[112:128, :], meta, w=[xname])
                else:
                    k.dma("sp", xb_[:], x[(b - 1) * 128:b * 128, :], w=[xname])
                layer_norm(xb_[:], hb[:], 0, xname, hname)
                if debug:
                    k.dma("sp", dbg["d_h0"][b], hb[:], r=[hname])
                for half in range(2):
                    yield
                    pt_, pn = gbank()
                    for j in range(4):
                        c = half * 4 + j
                        k.tr(pt_[:, j * 128:(j + 1) * 128], hb[:, c * 128:(c + 1) * 128], ident[:], r=[hname, "ident"], w=pn)
                    k.acopy(h0T[:, half * 4:(half + 1) * 4, :], pt_[:].rearrange("p (a b) -> p a b", a=4), r=pn, w=["h0T"])
                yield
                pt_, pn = gbank()
                for j, c0 in enumerate((QA0, QA0 + 128, KA0, KA0 + 128)):
                    for c in range(8):
                        k.mm(pt_[:, j * 128:(j + 1) * 128], winb[:, c, c0:c0 + 128], h0T[:, c, :], start=(c == 0), stop=(c == 7),
                             r=["winb", "h0T"], w=pn)
                k.acopy(qkg[:], pt_[:].rearrange("p (a b) -> p a b", a=4), r=pn, w=["qkg"])
                yield
                pt_, pn = gbank()
                for j in range(4):
                    for c in range(8):
                        k.mm(pt_[:, j * 128:(j + 1) * 128], winb[:, c, KB0 + j * 128:KB0 + (j + 1) * 128], h0T[:, c, :],
                             start=(c == 0), stop=(c == 7), r=["winb", "h0T"], w=pn)
                k.acopy(KT[:, :, b * 128:(b + 1) * 128], pt_[:].rearrange("p (a b) -> p a b", a=4), r=pn, w=[("KT", b)])
                yield
                pt_, pn = gbank()
                for c in range(8):
                    k.mm(pt_[:, 0:128], winb[:, c, GA0:GA0 + 128], h0T[:, c, :], start=(c == 0), stop=(c == 7),
                         r=["winb", "h0T"], w=pn)
                for c in range(8):
                    k.mm(pt_[:, 128:256], h0T[:, c, :], winb[:, c, FB0 - 120:FB0 + 8], start=(c == 0), stop=(c == 7),
                         r=["winb", "h0T"], w=pn)
                k.acopy(gaT[0:16, :], pt_[0:16, 0:128], r=pn, w=["gaT"])
                k.acopy(zt[:], pt_[:, 248:256], r=pn, w=["zt"])
                k.tt("dve", zt[:], zt[:], bfb[:], ALU.add, r=["zt", "bfb"], w=["zt"])
                yield
                pk_, pkn = fbank(2)
                for c in range(8):
                    k.mm(pk_[:, 0:256], h0T[:, c, :], winb[:, c, KA0:KA0 + 256], start=(c == 0), stop=(c == 7),
                         r=["winb", "h0T"], w=pkn)
                yield
                pt_, pn = gbank()
                for c in range(8):
                    k.mm(pt_[:, :], h0T[:, c, :], winb[:, c, VA0:VA0 + 512], start=(c == 0), stop=(c == 7),
                         r=["winb", "h0T"], w=pn)
                k.acopy(vg[:], pt_[:], r=pn, w=["vg"])
                yield
                pz_, pzn = gbank()
                k.mm(pz_[:, 0:256], gaT[0:17, :], wgub[0:17, :], r=["gaT", "wgub"], w=pzn)
                k.act(eg[:], pz_[:, 0:256], AF.Exp, scale=-1.0, r=pzn, w=["eg"])
                k.act(zt[:], zt[:], AF.Exp, scale=-1.0, r=["zt"], w=["zt"])
                k.act(eg[:], eg[:], AF.Ln, bias=1.0, r=["eg"], w=["eg"])
                k.act(zt[:], zt[:], AF.Ln, bias=1.0, r=["zt"], w=["zt"])
                k.ts("dve", glog[:], eg[:], -1.0 / 16.0, None, ALU.mult, r=["eg"], w=["glog"])
                k.ts("dve", logf[:], zt[:], -1.0, None, ALU.mult, r=["zt"], w=["logf"])
                if debug:
                    k.dma("sp", dbg["d_glog"][b], glog[:], r=["glog"])
                    k.dma("sp", dbg["d_logf"][b], logf[:], r=["logf"])
                yield
                pt_, pn = gbank()
                for c in range(8):
                    k.mm(pt_[:, :], h0T[:, c, :], winb[:, c, RA0:RA0 + 512], start=(c == 0), stop=(c == 7),
                         r=["winb", "h0T"], w=pn)
                k.act(sr[:], pt_[:], AF.Silu, r=pn, w=["sr"])
                yield
                pt_, pn = gbank()
                for c in range(8):
                    k.mm(pt_[:, :], h0T[:, c, :], winb[:, c, VB0:VB0 + 512], start=(c == 0), stop=(c == 7),
                         r=["winb", "h0T"], w=pn)
                k.acopy(VP[:, b, :, 0:64], pt_[:].rearrange("p (h d) -> p h d", h=8), r=pn, w=[("VP", b)])

                yield
                pd_, pdn = fbank(3)
                for p_ in range(2):
                    k.mm(pd_[:, p_ * 128:(p_ + 1) * 128], glog[:, p_ * 128:(p_ + 1) * 128], wmid[:], r=["glog", "wmid"], w=pdn)
                    k.mm(pd_[:, 256 + p_ * 128:256 + (p_ + 1) * 128], glog[:, p_ * 128:(p_ + 1) * 128], tri[:],
                         r=["glog", "tri"], w=pdn)
                k.mm(pk_[:, 256:512], sup[:], glog[:], r=["glog", "sup"], w=pkn)
                k.act(E1[:], pd_[:, 0:256].rearrange("p (a b) -> p a b", a=2), AF.Exp, r=pdn, w=["E1"])
                k.act(E2[:], pd_[:, 0:256].rearrange("p (a b) -> p a b", a=2), AF.Exp, scale=-1.0, r=pdn, w=["E2"])
                k.act(E3[:], pd_[:, 256:512].rearrange("p (a b) -> p a b", a=2), AF.Exp, r=pdn, w=["E3"])
                k.act(Erev[:], pk_[:, 256:512], AF.Exp, r=pkn, w=["Erev"])
                k.tt("dve", q_in[:], qkg[:, 0:2, :], E1[:], ALU.mult, r=["qkg", "E1"], w=["q_in"])
                for par in range(2):
                    rs = slice(par * 64, (par + 1) * 64)
                    k.tt("dve", k_in[rs, par, :, :], qkg[rs, 2:4, :], E2[rs, :, :], ALU.mult, r=["qkg", "E2"], w=["k_in"])
                for par in range(2):
                    rs = slice(par * 64, (par + 1) * 64)
                    k.stt("dve", q_dec[rs, par, :, :], qkg[rs, 0:2, :], 0.125, E3[rs, :, :], ALU.mult, ALU.mult, r=["qkg", "E3"], w=["q_dec"])
                k.tt("dve", k_dec[:], pk_[:, 0:256], Erev[:], ALU.mult, r=pkn + ["Erev"], w=["k_dec"])
                if b == 0:
                    k.memset("dve", k_in[:, :, :, 0:112], 0.0, w=["k_in"])
                    k.memset("dve", k_dec[0:112, :], 0.0, w=["k_dec"])
                yield
                pa_, pan = fbank(3)
                for h in range(4):
                    p_, r0 = h // 2, (h % 2) * 64
                    k.mm(pa_[:, h * 128:(h + 1) * 128], k_in[:, h % 2, p_, :], q_in[:, p_, :],
                         r=["k_in", "q_in"], w=pan)
                k.tt("dve", AM[:], pa_[:].rearrange("p (a b) -> p a b", a=4), maskc[:].unsqueeze(1).to_broadcast([128, 4, 128]),
                     ALU.mult, r=pan + ["maskc"], w=["AM"])
                yield
                po_, pon = fbank(2)
                if b > 0:
                    for h in range(4):
                        p_, r0 = h // 2, (h % 2) * 64
                        k.mm(po_[:, h * 128:(h + 1) * 128], AM[:, h, :], vg[:, h * 128:(h + 1) * 128], start=True, stop=False,
                             r=["AM", "vg"], w=pon)
                        k.mm(po_[:, h * 128:(h + 1) * 128], q_dec[:, h % 2, p_, :], Sb[:, p_, :], start=False, stop=True,
                             r=["q_dec", "Sb"], w=pon)
                yield
                ps_, psn = fbank(3)
                for h in range(4):
                    p_ = h // 2
                    k.mm(ps_[:, h * 128:(h + 1) * 128], k_dec[:, p_ * 128:(p_ + 1) * 128], vg[:, h * 128:(h + 1) * 128],
                         r=["k_dec", "vg"], w=psn)
                for h in range(4):
                    p_, r0 = h // 2, (h % 2) * 64
                    k.stt("dve", Sf[r0:r0 + 64, p_, :], Sf[r0:r0 + 64, p_, :], E3[r0:r0 + 64, p_, 127:128],
                          ps_[r0:r0 + 64, h * 128:(h + 1) * 128], ALU.mult, ALU.add, r=["Sf", "E3"] + psn, w=["Sf"])
                k.copy("dve", Sb[:], Sf[:], r=["Sf"], w=["Sb"])

                yield
                pf_, pfn = gbank()
                k.mm(pf_[:, 0:8], tri[:], logf[:], r=["tri", "logf"], w=pfn)
                k.mm(pf_[:, 8:16], ones[:], logf[:], r=["ones", "logf"], w=pfn)
                k.mm(pf_[:, 16:24], sel63[:], logf[:], r=["sel63", "logf"], w=pfn)
                k.stt("dve", negc[:, b, :], pf_[:, 0:8], -1.0, carry[:], ALU.mult, ALU.subtract, r=pfn + ["carry"], w=[("negc", b)])
                if b == 0:
                    k.memset("dve", negc[0:112, 0, :], NEGBIG, w=[("negc", b)])
                k.tt("dve", cmid[:], pf_[:, 16:24], carry[:], ALU.add, r=pfn + ["carry"], w=["cmid"])
                k.tt("dve", carry[:], pf_[:, 8:16], carry[:], ALU.add, r=pfn + ["carry"], w=["carry"])


            def foxq(b):
                pt_, pn = gbank()
                for j in range(4):
                    for c in range(8):
                        k.mm(pt_[:, j * 128:(j + 1) * 128], winb[:, c, QB0 + j * 128:QB0 + (j + 1) * 128], h0T[:, c, :],
                             start=(c == 0), stop=(c == 7), r=["winb", "h0T"], w=pn)
                for par in range(2):
                    k.acopy(qTf[par * 64:(par + 1) * 64, par, :, :], pt_[par * 64:(par + 1) * 64, :].rearrange("p (a b) -> p a b", a=4),
                            r=pn, w=["qTf"])

            def back_pre(b):
                xb_ = xt[0]
                hb = h0[b % 2]
                xname = "xt0"
                hname = f"h0{b % 2}"
                po_, pon = fbank(2)
                k.acopy(o_sb[:], po_[:].rearrange("p (a b) -> p a b", a=4), r=pon, w=["o_sb"])
                if debug:
                    pass
                k.tt("dve", sq[:], o_sb[:].rearrange("p a b -> p (a b)"), o_sb[:].rearrange("p a b -> p (a b)"), ALU.mult,
                     r=["o_sb"], w=["sq"])
                k.reduce(ss[:, 0:4], sq[:].rearrange("p (a b) -> p a b", a=4), ALU.add, r=["sq"], w=["ss"])
                k.act(ss[:, 0:4], ss[:, 0:4], AF.Ln, bias=epsc[:, 0:1], scale=1.0 / 128.0, r=["ss", "epsc"], w=["ss"])
                k.act(ss[:, 0:4], ss[:, 0:4], AF.Exp, scale=-0.5, r=["ss"], w=["ss"])
                k.tt("dve", o_sb[:], o_sb[:], ss[:, 0:4].unsqueeze(2).to_broadcast([128, 4, 128]), ALU.mult, r=["o_sb", "ss"], w=["o_sb"])
                k.tt("pool", sq[:], o_sb[:].rearrange("p a b -> p (a b)"), gngb[:, 0, :], ALU.mult, r=["o_sb", "gngb"], w=["sq"])
                k.tt("pool", ocat[:, 0:512], sq[:], sr[:], ALU.mult, r=["sq", "sr"], w=["ocat_a"])
                if debug:
                    k.dma("sp", dbg["d_oa"][b], ocat[:, 0:512], r=["ocat_a"])

                k.tt("dve", biasb[:, 0:b + 1, :], negc[:, 0:b + 1, :], cmid[:].unsqueeze(1).to_broadcast([128, b + 1, 8]), ALU.add,
                     r=[("negc", j) for j in range(b + 1)] + ["cmid"], w=["biasb"])

            def attention(b, gen):
                iters = [(h, kb) for h in range(8) for kb in range(b + 1)]
                nbat = (len(iters) + 3) // 4
                binfo = {}
                for m in range(nbat + 2):
                    if gen is not None:
                        next(gen, None)
                    if m < nbat:
                        sb_i = 4 + (rrc["sc"] % 2)
                        rrc["sc"] += 1
                        lst = []
                        for j, (h, kb) in enumerate(iters[4 * m:4 * m + 4]):
                            k.mm(psb[sb_i][:, j * 128:(j + 1) * 128], KT[:, h // 2, kb * 128:(kb + 1) * 128], qTf[:, h % 2, h // 2, :],
                                 r=[("KT", kb), "qTf"], w=bq(sb_i))
                            lst.append((h, kb, j))
                        binfo[m] = (sb_i, lst, [])
                    if 1 <= m <= nbat:
                        sb_i, lst, pts = binfo[m - 1]
                        for (h, kb, j) in lst:
                            ptile = PT[rrc["pt"] % NPT]
                            pname = f"PT{rrc["pt"] % NPT}"
                            rrc["pt"] += 1
                            k.act(ptile[:], psb[sb_i][:, j * 128:(j + 1) * 128], AF.Exp, bias=biasb[:, kb, h:h + 1], scale=0.125,
                                  r=bq(sb_i) + ["biasb"], w=[pname])
                            if kb == b:
                                k.tt("pool", ptile[:], ptile[:], trib[:], ALU.mult, r=[pname, "trib"], w=[pname])
                            pts.append((ptile, pname))
                    if m >= 2:
                        sb_i, lst, pts = binfo.pop(m - 2)
                        for (h, kb, j), (ptile, pname) in zip(lst, pts):
                            obn = f"ps{6 + h // 4}"
                            oc0 = (h % 4) * 65
                            k.mm(psb[6 + h // 4][:, oc0:oc0 + 65], ptile[:], VP[:, kb, h, :], start=(kb == 0), stop=(kb == b),
                                 r=[pname, ("VP", kb)], w=[(obn, h % 4)])

            def back_post(b):
                xb_ = xt[0]
                hb = h0[b % 2]
                xname = "xt0"
                hname = f"h0{b % 2}"
                for g_ in range(2):
                    obank = psb[6 + g_]
                    ov = obank[:, 0:260].rearrange("p (h d) -> p h d", h=4)
                    onames = [(f"ps{6 + g_}", j) for j in range(4)]
                    k.op("dve", lambda e, ov=ov, g_=g_: e.reciprocal(out=rinv[:, g_ * 4:(g_ + 1) * 4], in_=ov[:, :, 64]),
                         r=onames, w=[("rinv", g_)])
                    k.tt("dve", on[:, g_ * 4:(g_ + 1) * 4, :], ov[:, :, 0:64],
                         rinv[:, g_ * 4:(g_ + 1) * 4].unsqueeze(2).to_broadcast([128, 4, 64]), ALU.mult,
                         r=onames + [("rinv", g_)], w=["o_sb"])
                onf = on[:].rearrange("p h d -> p (h d)")
                k.tt("dve", sq[:], onf, onf, ALU.mult, r=["o_sb"], w=["sq"])
                k.reduce(ss[:], sq[:].rearrange("p (a b) -> p a b", a=8), ALU.add, r=["sq"], w=["ss"])
                k.act(ss[:], ss[:], AF.Ln, bias=epsc[:, 0:1], scale=1.0 / 64.0, r=["ss", "epsc"], w=["ss"])
                k.act(ss[:], ss[:], AF.Exp, scale=-0.5, r=["ss"], w=["ss"])
                k.tt("dve", on[:], on[:], ss[:].unsqueeze(2).to_broadcast([128, 8, 64]), ALU.mult,
                     r=["o_sb", "ss"], w=["o_sb"])
                k.tt("pool", ocat[:, 512:1024], onf, gngb[:, 1, :], ALU.mult, r=["o_sb", "gngb"], w=["ocat_b"])
                if debug:
                    k.dma("sp", dbg["d_ob"][b], ocat[:, 512:1024], r=["ocat_b"])

                for half in range(2):
                    pt_, pn = gbank()
                    for j in range(4):
                        c = half * 4 + j
                        k.tr(pt_[:, j * 128:(j + 1) * 128], ocat[:, c * 128:(c + 1) * 128], ident[:],
                             r=["ocat_a", "ocat_b", "ident"], w=pn)
                    k.acopy(ocatT[:, half * 4:(half + 1) * 4, :], pt_[:].rearrange("p (a b) -> p a b", a=4), r=pn, w=["ocatT"])
                for half in range(2):
                    pt_, pn = gbank()
                    for c in range(8):
                        k.mm(pt_[:, :], ocatT[:, c, :], woutb[:, c, half * 512:(half + 1) * 512], start=(c == 0), stop=(c == 7),
                             r=["ocatT", "woutb"], w=pn)
                    k.stt("dve", rr[:, half * 512:(half + 1) * 512], hb[:, half * 512:(half + 1) * 512], DN_ALPHA, pt_[:],
                          ALU.mult, ALU.add, r=[hname] + pn, w=["rr"])
                layer_norm(rr[:], rr[:], 2, "rr", "rr")
                k.dma("sp", h1d[(b - 1) * 128:b * 128, :], rr[:], r=["rr"], w=[("h1d", b - 1)])
                if debug:
                    k.dma("sp", dbg["d_h1"][b], rr[:], r=["rr"])


            for _ in front(0):
                pass
            if nblk > 1:
                for _ in front(1):
                    pass
                foxq(1)
            for b in range(1, nblk):
                back_pre(b)
                gen = front(b + 1) if b + 1 < nblk else None
                attention(b, gen)
                if gen is not None:
                    for _ in gen:
                        pass
                    foxq(b + 1)
                back_post(b)

        while tbl_jobs:
            dst_, src_, nm_, i_ = tbl_jobs.pop(0)
            k.dma("pool", dst_[i_], src_[i_], w=[(nm_, i_)])
        fence = P.tails()
        for en in ENGINES:
            P.wait_only(en, fence)
        phase2(nc, P, k, psb, ident, iof, epsc, h1d, ub, vb, wqd, skT, lnv, out, ntile)

        P.wait_only("sp", P.tails())
        P.emit(nc, st)
    return nc


def phase2(nc, P, k, psb, ident, iof, epsc, h1d, ub, vb, wqd, skT, lnv, out, ntile):
    if ntile == 0:
        return
    with ExitStack() as st2:
        def sb2(name, shape, dt):
            return st2.enter_context(nc.sbuf_tensor(name, shape, dt))

        def bq(i, qs=(0, 1, 2, 3)):
            return [(f"ps{i}", q) for q in qs]

        grr = [0]
        gbase = [2]

        def gbank():
            i = gbase[0] + grr[0] % 2
            grr[0] += 1
            return psb[i], bq(i)

        wqs = [sb2(f"wqs{i}", [128, 8, 512], BF16) for i in range(2)]
        skTb = sb2("skTb", [128, 16, 128], BF16)
        ln2 = sb2("ln2", [128, 2, D], F32)
        k.dma("pool", skTb[:], skT.rearrange("hc d n -> d hc n"), w=["skTb"])
        for j in range(2):
            k.dma("sp", ln2[:, j, :], lnv[4 + j].partition_broadcast(128), w=["ln2"])
        h1t2 = [sb2(f"h1t{i}", [128, 2, D], F32) for i in range(2)]
        h1T2 = [sb2(f"h1T{i}", [128, 8, 256], BF16) for i in range(2)]
        qT = sb2("qT", [128, 16, 256], BF16)
        s_sb = sb2("s_sb", [128, 16, 128], F32)
        s_tmp = [sb2(f"s_tmp{i}", [128, 128], F32) for i in range(2)]
        topv = sb2("topv", [128, 16, 16], F32)
        topi = sb2("topi", [128, 16, 16], U32)
        topif = sb2("topif", [128, 16, 16], F32)
        cand = sb2("cand", [128, 8, 256], F32)
        cand_tmp = [sb2(f"cand_tmp{i}", [128, 256], F32) for i in range(2)]
        bv = sb2("bv", [128, 8, 16], F32)
        pos = sb2("pos", [128, 8, 16], U32)
        posf = sb2("posf", [128, 8, 16], F32)
        akf = sb2("akf", [128, 8, 16], F32)
        bkf = sb2("bkf", [128, 8, 16], F32)
        eq4 = sb2("eq4", [128, 8, 16, 16], F32)
        thr16 = sb2("thr16", [128, 16], F32)
        iota16 = sb2("iota16", [128, 16], F32)
        If = sb2("If", [128, 8, 16], F32)
        Jf = sb2("Jf", [128, 8, 16], F32)
        gf = sb2("gf", [128, 8, 16], F32)
        gsum = sb2("gsum", [128, 8], F32)
        IJG = sb2("IJG", [128, 2, 3, 128], F32)
        IT = sb2("IT", [128, 256], F32)
        JT = sb2("JT", [128, 256], F32)
        gT = sb2("gT", [128, 256], F32)
        NAB = 6
        AENG = "dve"
        GAENG = "dve"
        Bt = [sb2(f"Bt{i}", [128, 128], BF16) for i in range(NAB)]
        At = [sb2(f"At{i}", [128, 128], BF16) for i in range(NAB)]
        G_sb = sb2("G_sb", [128, 256, 128], BF16)
        NU = 5
        uTs = [sb2(f"uTs{i}", [128, 8, 128], BF16) for i in range(NU)]
        vs = [sb2(f"vs{i}", [128, D], BF16) for i in range(NU)]
        ga_sb = [sb2(f"ga_sb{i}", [128, 256], BF16) for i in range(3)]
        GA = [sb2(f"GA{i}", [128, 256], BF16) for i in range(5)]
        rr2 = sb2("rr2", [128, D], F32)
        bst = sb2("bst2", [128, 2, 6], F32)
        mv = sb2("mv2", [128, 2], F32)
        rstd = sb2("rstd2", [128, 1], F32)
        iob = sb2("iob", [128, 128], BF16)
        k.copy("dve", iob[:], iof[:], r=["iof"], w=["iob"])
        k.ts("dve", iota16[:], iof[:, 0:16], 1.0, None, ALU.mult, r=["iof"], w=["iota16"])
        k.ts("dve", thr16[:], iof[:, 0:16], 16.0, 15.5, ALU.mult, ALU.add, r=["iof"], w=["thr16"])

        def routing(ti):
            h1t = h1t2[ti % 2]
            h1T = h1T2[ti % 2]
            h1tn = f"h1t{ti % 2}"
            h1Tn = f"h1T{ti % 2}"
            nops = [len(P.ops)]

            def tick():
                if len(P.ops) - nops[0] >= 4:
                    nops[0] = len(P.ops)
                    return True
                return False
            for tb in range(2):
                blk = 2 * ti + tb
                k.dma("sp", h1t[:, tb, :], h1d[blk * 128:(blk + 1) * 128, :], r=[("h1d", blk)], w=[(h1tn, tb)])
            for tb in range(2):
                for half in range(2):
                    yield
                    pt_, pn = gbank()
                    for j in range(4):
                        c = half * 4 + j
                        k.tr(pt_[:, j * 128:(j + 1) * 128], h1t[:, tb, c * 128:(c + 1) * 128], ident[:], r=[(h1tn, tb), "ident"], w=pn)
                    k.acopy(h1T[:, half * 4:(half + 1) * 4, tb * 128:(tb + 1) * 128], pt_[:].rearrange("p (a b) -> p a b", a=4),
                            r=pn, w=[h1Tn])
            for hp in range(8):
                pt_, pn = gbank()
                for j in range(2):
                    yield
                    hc = hp * 2 + j
                    wq_ = wqs[(hc // 4) % 2]
                    wqn = f"wqs{(hc // 4) % 2}"
                    if hc % 4 == 0:
                        k.dma("sp", wq_[:], wqd[:, :, hc * 128:(hc + 4) * 128], r=["wqd"], w=[wqn])
                    for c in range(8):
                        k.mm(pt_[:, j * 256:(j + 1) * 256], wq_[:, c, (hc % 4) * 128:(hc % 4 + 1) * 128], h1T[:, c, :], start=(c == 0), stop=(c == 7),
                             r=[wqn, h1Tn], w=pn)
                k.acopy(qT[:, hp * 2:hp * 2 + 2, :], pt_[:].rearrange("p (a b) -> p a b", a=2), r=pn, w=["qT"])
            for tb in range(2):
                for g4 in range(4):
                    yield
                    pt_, pn = gbank()
                    for j in range(4):
                        hc = g4 * 4 + j
                        k.mm(pt_[:, j * 128:(j + 1) * 128], qT[:, hc, tb * 128:(tb + 1) * 128], skTb[:, hc, :], r=["qT", "skTb"], w=pn)
                    k.acopy(s_sb[:, g4 * 4:(g4 + 1) * 4, :], pt_[:].rearrange("p (a b) -> p a b", a=4), r=pn, w=["s_sb"])
                for hc in range(16):
                    yield
                    sv = s_sb[:, hc, :]
                    stp = s_tmp[hc % 2]
                    stn = f"s_tmp{hc % 2}"
                    k.op("dve", lambda e, sv=sv, hc=hc: e.max(out=topv[:, hc, 0:8], in_=sv), r=["s_sb"], w=["topv"])
                    k.op("dve", lambda e, sv=sv, hc=hc: e.max_index(out=topi[:, hc, 0:8], in_max=topv[:, hc, 0:8], in_values=sv),
                         r=["s_sb", "topv"], w=["topi"])
                    k.op("dve", lambda e, sv=sv, hc=hc, stp=stp: e.match_replace(out=stp[:], in_to_replace=topv[:, hc, 0:8], in_values=sv,
                                                                                  imm_value=-1e30), r=["s_sb", "topv"], w=[stn])
                    k.op("dve", lambda e, hc=hc, stp=stp: e.max(out=topv[:, hc, 8:16], in_=stp[:]), r=[stn], w=["topv"])
                    k.op("dve", lambda e, hc=hc, stp=stp: e.max_index(out=topi[:, hc, 8:16], in_max=topv[:, hc, 8:16], in_values=stp[:]),
                         r=[stn, "topv"], w=["topi"])
                k.copy("dve", topif[:], topi[:], r=["topi"], w=["topif"])
                tv = topv[:].rearrange("p (h c) a -> p h c a", c=2)
                tif = topif[:].rearrange("p (h c) a -> p h c a", c=2)
                yield
                k.tt("dve", cand[:].rearrange("p h (a b) -> p h a b", a=16), tv[:, :, 0, :].unsqueeze(3).to_broadcast([128, 8, 16, 16]),
                     tv[:, :, 1, :].unsqueeze(2).to_broadcast([128, 8, 16, 16]), ALU.add, r=["topv"], w=["cand"])
                for h in range(8):
                    yield
                    cv = cand[:, h, :]
                    ctp = cand_tmp[h % 2]
                    ctn = f"cand_tmp{h % 2}"
                    k.op("dve", lambda e, cv=cv, h=h: e.max(out=bv[:, h, 0:8], in_=cv), r=["cand"], w=["bv"])
                    k.op("dve", lambda e, cv=cv, h=h: e.max_index(out=pos[:, h, 0:8], in_max=bv[:, h, 0:8], in_values=cv),
                         r=["cand", "bv"], w=["pos"])
                    k.op("dve", lambda e, cv=cv, h=h, ctp=ctp: e.match_replace(out=ctp[:], in_to_replace=bv[:, h, 0:8], in_values=cv,
                                                                                imm_value=-1e30), r=["cand", "bv"], w=[ctn])
                    k.op("dve", lambda e, h=h, ctp=ctp: e.max(out=bv[:, h, 8:16], in_=ctp[:]), r=[ctn], w=["bv"])
                    k.op("dve", lambda e, h=h, ctp=ctp: e.max_index(out=pos[:, h, 8:16], in_max=bv[:, h, 8:16], in_values=ctp[:]),
                         r=[ctn, "bv"], w=["pos"])
                yield
                k.copy("dve", posf[:], pos[:], r=["pos"], w=["posf"])
                yield
                k.tt("dve", eq4[:, :, :, 0:15], posf[:].unsqueeze(3).to_broadcast([128, 8, 16, 15]),
                     thr16[:, 0:15].unsqueeze(1).unsqueeze(1).to_broadcast([128, 8, 16, 15]), ALU.is_ge, r=["posf", "thr16"], w=["eq4"])
                yield
                k.reduce(akf[:], eq4[:, :, :, 0:15], ALU.add, r=["eq4"], w=["akf"])
                k.stt("dve", bkf[:], akf[:], -16.0, posf[:], ALU.mult, ALU.add, r=["akf", "posf"], w=["bkf"])
                for (src, cidx, dst, dn) in ((akf, 0, If, "If"), (bkf, 1, Jf, "Jf")):
                    yield
                    k.tt("dve", eq4[:], src[:].unsqueeze(3).to_broadcast([128, 8, 16, 16]),
                         iota16[:].unsqueeze(1).unsqueeze(1).to_broadcast([128, 8, 16, 16]), ALU.is_equal,
                         r=["akf", "bkf", "iota16"], w=["eq4"])
                    yield
                    k.tt("dve", eq4[:], eq4[:], tif[:, :, cidx, :].unsqueeze(2).to_broadcast([128, 8, 16, 16]), ALU.mult,
                         r=["eq4", "topif"], w=["eq4"])
                    yield
                    k.reduce(dst[:], eq4[:], ALU.add, r=["eq4"], w=[dn])
                yield
                k.tt("dve", gf[:], bv[:], bv[:, :, 0:1].to_broadcast([128, 8, 16]), ALU.subtract, r=["bv"], w=["gf"])
                k.act(gf[:], gf[:], AF.Exp, r=["gf"], w=["gf"])
                k.reduce(gsum[:], gf[:], ALU.add, r=["gf"], w=["gsum"])
                k.op("dve", lambda e: e.reciprocal(out=gsum[:], in_=gsum[:]), r=["gsum"], w=["gsum"])
                k.tt("dve", gf[:], gf[:], gsum[:].unsqueeze(2).to_broadcast([128, 8, 16]), ALU.mult, r=["gf", "gsum"], w=["gf"])
                k.copy("dve", IJG[:, tb, 0, :], If[:].rearrange("p h k -> p (h k)"), r=["If"], w=[("IJG", tb)])
                k.copy("dve", IJG[:, tb, 1, :], Jf[:].rearrange("p h k -> p (h k)"), r=["Jf"], w=[("IJG", tb)])
                k.copy("dve", IJG[:, tb, 2, :], gf[:].rearrange("p h k -> p (h k)"), r=["gf"], w=[("IJG", tb)])
        def gbuild(ti):
            for tb in range(2):
                pt_, pn = gbank()
                for j in range(3):
                    k.tr(pt_[:, j * 128:(j + 1) * 128], IJG[:, tb, j, :], ident[:], r=[("IJG", tb), "ident"], w=pn)
                for j, (dst, dn) in enumerate(((IT, "IT"), (JT, "JT"), (gT, "gT"))):
                    k.acopy(dst[:, tb * 128:(tb + 1) * 128], pt_[:, j * 128:(j + 1) * 128], r=pn, w=[dn])
            for t in range(256):
                jb = t % NAB
                k.ts("dve", Bt[jb][:], iob[:], JT[:, t:t + 1], None, ALU.is_equal, r=["iob", "JT"], w=[f"Bt{jb}"])
                k.ts(AENG, At[jb][:], iob[:], IT[:, t:t + 1], gT[:, t:t + 1], ALU.is_equal, ALU.mult, r=["iob", "IT", "gT"], w=[f"At{jb}"])
                gb = 2 + (t // 4) % 2
                q_ = t % 4
                k.mm(psb[gb][:, q_ * 128:(q_ + 1) * 128], Bt[jb][:], At[jb][:], r=[f"Bt{jb}", f"At{jb}"], w=[(f"ps{gb}", q_)])
                if q_ == 3:
                    k.acopy(G_sb[:, t - 3:t + 1, :], psb[gb][:].rearrange("p (a b) -> p a b", a=4), r=bq(gb), w=["G_sb"])
        def gloop(ti, gen):
            h1T = h1T2[ti % 2]
            h1Tn = f"h1T{ti % 2}"
            LG = 4
            ginfo = {}
            for i in range(128 + LG):
                if i < 128:
                    ju = i % NU
                    k.dma("sp", uTs[ju][:], ub[i].rearrange("p (c e) -> p c e", c=8), r=[("ub", i)], w=[f"uTs{ju}"])
                    k.dma("act", vs[ju][:], vb[i], r=[("vb", i)], w=[f"vs{ju}"])
                    ab = i % 2
                    for c in range(8):
                        k.mm(psb[ab][:, 0:256], uTs[ju][:, c, :], h1T[:, c, :], start=(c == 0), stop=(c == 7), r=[f"uTs{ju}", h1Tn], w=bq(ab))
                    gs = ga_sb[i % 3]
                    gsn = f"ga_sb{i % 3}"
                    k.act(gs[:], psb[ab][:, 0:256], AF.Gelu, r=bq(ab), w=[gsn])
                    gA = GA[i % 5]
                    gAn = f"GA{i % 5}"
                    k.tt(GAENG, gA[:], gs[:], G_sb[:, :, i], ALU.mult, r=[gsn, "G_sb"], w=[gAn])
                    ginfo[i] = (gA, gAn, ju)
                if gen is not None:
                    next(gen, None)
                if i >= LG:
                    i2 = i - LG
                    gA, gAn, ju = ginfo.pop(i2)
                    for tb in range(2):
                        for half in range(2):
                            yb = 4 + tb * 2 + half
                            k.mm(psb[yb][:, :], gA[:, tb * 128:(tb + 1) * 128], vs[ju][:, half * 512:(half + 1) * 512],
                                 start=(i2 == 0), stop=(i2 == 127), r=[gAn, f"vs{ju}"], w=bq(yb))
        def finalize(ti):
            h1t = h1t2[ti % 2]
            h1tn = f"h1t{ti % 2}"
            for tb in range(2):
                blk = 2 * ti + tb
                for half in range(2):
                    yb = 4 + tb * 2 + half
                    k.stt("dve", rr2[:, half * 512:(half + 1) * 512], h1t[:, tb, half * 512:(half + 1) * 512], DN_ALPHA, psb[yb][:, :],
                          ALU.mult, ALU.add, r=[(h1tn, tb)] + bq(yb), w=["rr2"])
                for hh in range(2):
                    k.op("dve", lambda e, hh=hh: e.bn_stats(out=bst[:, hh, :], in_=rr2[:, hh * 512:(hh + 1) * 512]), r=["rr2"], w=["bst2"])
                k.op("dve", lambda e: e.bn_aggr(out=mv[:], in_=bst[:].rearrange("p a b -> p (a b)")), r=["bst2"], w=["mv2"])
                k.act(rstd[:], mv[:, 1:2], AF.Ln, bias=epsc[:, 0:1], r=["mv2", "epsc"], w=["rstd2"])
                k.act(rstd[:], rstd[:], AF.Exp, scale=-0.5, r=["rstd2"], w=["rstd2"])
                k.ts("dve", rr2[:], rr2[:], mv[:, 0:1], rstd[:], ALU.subtract, ALU.mult, r=["rr2", "mv2", "rstd2"], w=["rr2"])
                k.tt("pool", rr2[:], rr2[:], ln2[:, 0, :], ALU.mult, r=["rr2", "ln2"], w=["rr2"])
                k.tt("pool", rr2[:], rr2[:], ln2[:, 1, :], ALU.add, r=["rr2", "ln2"], w=["rr2"])
                k.dma("sp", out[blk * 128:(blk + 1) * 128, :], rr2[:], r=["rr2"])


        gbase[0] = 0
        for _ in routing(0):
            pass
        for ti in range(ntile):
            gbase[0] = 2
            gbuild(ti)
            gen = routing(ti + 1) if ti + 1 < ntile else None
            gloop(ti, gen)
            if gen is not None:
                for _ in gen:
                    pass
            finalize(ti)

def _prep_shared(inputs):
    f = lambda a: np.ascontiguousarray(np.asarray(a, dtype=np.float32))
    sh = {}
    sh["meta"] = f(inputs["meta_tokens"])
    sh["lnv"] = f(np.stack([inputs["emb_ln_g"], inputs["emb_ln_b"], inputs["ln1_g"][0], inputs["ln1_b"][0],
                            inputs["ln2_g"][0], inputs["ln2_b"][0]]))
    sh["w_in"] = f(inputs["w_in"][0])
    sh["wgu"] = f(np.concatenate([inputs["w_gate_up"][0], inputs["b_gate"][0][None, :]], axis=0))
    sh["bfg"] = f(inputs["b_forget"][0])
    sh["gng"] = f(np.stack([inputs["gla_norm_g"][0], inputs["fox_norm_g"][0]]))
    sh["w_out"] = f(inputs["w_out"][0])
    sh["w_q"] = f(inputs["peer_w_q"][0])
    sk = np.asarray(inputs["peer_sub_keys"][0], dtype=np.float32)
    sh["skT"] = f(sk.transpose(0, 1, 3, 2).reshape(16, 128, 128))
    u = np.asarray(inputs["peer_u"][0], dtype=np.float32)
    sh["uT"] = f(u.reshape(128, 128, 8, 128).transpose(0, 3, 2, 1).reshape(128, 128, 1024))
    sh["vv"] = f(np.asarray(inputs["peer_v"][0], dtype=np.float32).reshape(128, 128, 1024))
    return sh


def kernel(**inputs):
    nc = build()
    sh = _prep_shared(inputs)
    x = np.asarray(inputs["x"], dtype=np.float32)
    in_maps = []
    for b in range(8):
        m = dict(sh)
        m["x"] = np.ascontiguousarray(x[b])
        in_maps.append(m)
    res = run_bass_kernel_spmd(nc, in_maps, core_ids=list(range(8)))
    return np.stack([np.asarray(r["out"], dtype=np.float32) for r in res.results], axis=0)
```

```python
import numpy as np
from contextlib import ExitStack
import concourse.bass as bass
import concourse.mybir as mybir
from concourse.bass_utils import run_bass_kernel_spmd

ALU = mybir.AluOpType
AF = mybir.ActivationFunctionType
AX = mybir.AxisListType
F32 = mybir.dt.float32
BF16 = mybir.dt.bfloat16
U32 = mybir.dt.uint32

ENGINES = ("pe", "act", "dve", "pool", "sp")
SEM_ROLL = 30000
DMA_POOL = 12
NOSYNC_SAME = ("pe",)

D = 1024
SEQ = 4096
NBLK = 33
INW = 3096
QA0, KA0, VA0, RA0, GA0, QB0, KB0, VB0, FB0 = 0, 256, 512, 1024, 1536, 1552, 2064, 2576, 3088
DN_ALPHA = 2.0 ** 0.25
EPS = 1e-5
NEGBIG = -30000.0


class _Op:
    __slots__ = ("eng", "fn", "deps", "dma", "signal", "ticket", "dsem", "dval", "idx", "prev_same_sem", "noop")

    def __init__(self, eng, fn, deps, dma, idx):
        self.eng = eng
        self.fn = fn
        self.deps = deps
        self.dma = dma
        self.signal = dma
        self.ticket = None
        self.dsem = None
        self.dval = None
        self.idx = idx
        self.prev_same_sem = None
        self.noop = False


class Prog:
    def __init__(self, same_engine_sync=True):
        self.ops = []
        self.last_w = {}
        self.readers = {}
        self.same_engine_sync = same_engine_sync

    def add(self, eng, fn, reads=(), writes=(), dma=False, extra_deps=()):
        idx = len(self.ops)
        deps = set(extra_deps)
        for b in reads:
            w = self.last_w.get(b)
            if w is not None:
                deps.add(w)
        for b in writes:
            w = self.last_w.get(b)
            if w is not None:
                deps.add(w)
            for r in self.readers.get(b, ()):
                deps.add(r)
        for b in reads:
            self.readers.setdefault(b, []).append(idx)
        for b in writes:
            self.last_w[b] = idx
            self.readers[b] = []
        deps.discard(idx)
        self.ops.append(_Op(eng, fn, deps, dma, idx))
        return idx

    def wait_only(self, eng, deps):
        idx = self.add(eng, lambda e: None, extra_deps=deps)
        self.ops[idx].noop = True
        return idx

    def tails(self):
        t = [op.idx for op in self.ops if op.dma]
        for en in ENGINES:
            lst = [op.idx for op in self.ops if op.eng == en and not op.dma and not op.noop]
            if lst:
                t.append(lst[-1])
        return t

    def emit(self, nc, stack):
        ops = self.ops
        for op in ops:
            nd = set()
            for d in op.deps:
                p = ops[d]
                if (not p.dma) and p.eng == op.eng and (op.eng in NOSYNC_SAME or not self.same_engine_sync) and not op.dma:
                    continue
                nd.add(d)
            op.deps = nd
            for d in nd:
                ops[d].signal = True
        cnt = {e: 0 for e in ENGINES}
        dcnt = {e: 0 for e in ENGINES}
        nsem = {e: 1 for e in ENGINES}
        dma_hist = {e: [] for e in ENGINES}
        for op in ops:
            if op.dma:
                n = dcnt[op.eng]
                dcnt[op.eng] += 1
                op.dsem = (op.eng, n % DMA_POOL)
                op.dval = 16 * (n // DMA_POOL + 1)
                if n >= DMA_POOL:
                    op.prev_same_sem = dma_hist[op.eng][n - DMA_POOL]
                dma_hist[op.eng].append(op.idx)
            elif op.signal:
                cnt[op.eng] += 1
                c = cnt[op.eng]
                op.ticket = ((c - 1) // SEM_ROLL, (c - 1) % SEM_ROLL + 1)
                nsem[op.eng] = max(nsem[op.eng], op.ticket[0] + 1)
        sems = {}
        for e in ENGINES:
            for k in range(nsem[e]):
                sems[("c", e, k)] = stack.enter_context(nc.semaphore(f"s_{e}_{k}"))
            if dcnt[e] > 0:
                for k in range(min(DMA_POOL, dcnt[e])):
                    sems[("d", e, k)] = stack.enter_context(nc.semaphore(f"d_{e}_{k}"))
        block = stack.enter_context(nc.Block())
        per_eng = {e: [op for op in ops if op.eng == e] for e in ENGINES}

        def run_engine(ename, eng):
            waited = {}
            for op in per_eng[ename]:
                need = {}
                deps = set(op.deps)
                if op.prev_same_sem is not None:
                    deps.add(op.prev_same_sem)
                for d in deps:
                    p = ops[d]
                    if p.dma:
                        key = ("d",) + p.dsem
                        val = p.dval
                    else:
                        key = ("c", p.eng, p.ticket[0])
                        val = p.ticket[1]
                    if waited.get(key, 0) >= val:
                        continue
                    if need.get(key, 0) < val:
                        need[key] = val
                for key, val in need.items():
                    eng.wait_ge(sems[key], val)
                    waited[key] = val
                inst = op.fn(eng)
                if inst is None:
                    continue
                if op.dma:
                    inst.then_inc(sems[("d",) + op.dsem], 16)
                elif op.signal:
                    inst.then_inc(sems[("c", op.eng, op.ticket[0])], 1)

        if per_eng["pe"]:
            @block.tensor
            def _(e):
                run_engine("pe", e)
        if per_eng["act"]:
            @block.scalar
            def _(e):
                run_engine("act", e)
        if per_eng["dve"]:
            @block.vector
            def _(e):
                run_engine("dve", e)
        if per_eng["pool"]:
            @block.gpsimd
            def _(e):
                run_engine("pool", e)
        if per_eng["sp"]:
            @block.sync
            def _(e):
                run_engine("sp", e)


class K:
    def __init__(self, P):
        self.P = P

    def mm(self, out, lhsT, rhs, start=True, stop=True, r=(), w=()):
        return self.P.add("pe", lambda e: e.matmul(out, lhsT=lhsT, rhs=rhs, start=start, stop=stop), r, w)

    def tr(self, out, in_, ident, r=(), w=()):
        return self.P.add("pe", lambda e: e.transpose(out=out, in_=in_, identity=ident), r, w)

    def act(self, out, in_, func, r=(), w=(), bias=None, scale=None):
        kw = {}
        if bias is not None:
            kw["bias"] = bias
        if scale is not None:
            kw["scale"] = scale
        return self.P.add("act", lambda e: e.activation(out=out, in_=in_, func=func, **kw), r, w)

    def acopy(self, out, in_, r=(), w=()):
        return self.P.add("act", lambda e: e.copy(out=out, in_=in_), r, w)

    def tt(self, eng, out, in0, in1, op, r=(), w=()):
        return self.P.add(eng, lambda e: e.tensor_tensor(out=out, in0=in0, in1=in1, op=op), r, w)

    def ts(self, eng, out, in0, s1, s2, op0, op1=None, r=(), w=()):
        if op1 is None:
            return self.P.add(eng, lambda e: e.tensor_scalar(out=out, in0=in0, scalar1=s1, scalar2=None, op0=op0), r, w)
        return self.P.add(eng, lambda e: e.tensor_scalar(out=out, in0=in0, scalar1=s1, scalar2=s2, op0=op0, op1=op1), r, w)

    def stt(self, eng, out, in0, scalar, in1, op0, op1, r=(), w=()):
        return self.P.add(eng, lambda e: e.scalar_tensor_tensor(out=out, in0=in0, scalar=scalar, in1=in1, op0=op0, op1=op1), r, w)

    def copy(self, eng, out, in_, r=(), w=()):
        return self.P.add(eng, lambda e: e.tensor_copy(out=out, in_=in_), r, w)

    def memset(self, eng, ap, val, r=(), w=()):
        return self.P.add(eng, lambda e: e.memset(ap, val), r, w)

    def reduce(self, out, in_, op, r=(), w=()):
        return self.P.add("dve", lambda e: e.tensor_reduce(out=out, in_=in_, axis=AX.X, op=op), r, w)

    def dma(self, q, out, in_, r=(), w=()):
        return self.P.add(q, lambda e: e.dma_start(out=out, in_=in_), r, w, dma=True)

    def op(self, eng, fn, r=(), w=()):
        return self.P.add(eng, fn, r, w)


def build(nblk=NBLK, ntile=16, debug=False, stage=99):
    nc = bass.Bass("TRN2", target_bir_lowering=False)
    dt_in = lambda name, shape, dt=F32: nc.dram_tensor(name, shape, dt, kind="ExternalInput").ap()
    x = dt_in("x", [SEQ, D])
    meta = dt_in("meta", [16, D])
    lnv = dt_in("lnv", [6, D])
    w_in = dt_in("w_in", [D, INW])
    wgu = dt_in("wgu", [17, 256])
    bfg = dt_in("bfg", [8])
    gng = dt_in("gng", [2, 512])
    w_out = dt_in("w_out", [D, D])
    w_q = dt_in("w_q", [D, 2048])
    skT = dt_in("skT", [16, 128, 128])
    uT = dt_in("uT", [128, 128, 1024])
    vv = dt_in("vv", [128, 128, 1024])
    out = nc.dram_tensor("out", [SEQ, D], F32, kind="ExternalOutput").ap()
    h1d = nc.dram_tensor("h1d", [SEQ, D], F32).ap()
    ub = nc.dram_tensor("ub", [128, 128, 1024], BF16).ap()
    vb = nc.dram_tensor("vb", [128, 128, 1024], BF16).ap()
    wqd = nc.dram_tensor("wqd", [128, 8, 2048], BF16).ap()
    dbg = {}
    if debug:
        for name, shape in (("d_h0", [128, D]), ("d_glog", [128, 256]), ("d_oa", [128, 512]), ("d_ob", [128, 512]),
                            ("d_logf", [128, 8]), ("d_h1", [128, D])):
            dbg[name] = nc.dram_tensor(name, [nblk] + shape, F32, kind="ExternalOutput").ap()

    P = Prog()
    k = K(P)
    out_dmas = []

    with ExitStack() as st:
        def sb(name, shape, dt):
            return st.enter_context(nc.sbuf_tensor(name, shape, dt))

        psb = [st.enter_context(nc.psum_tensor(f"ps{i}", [128, 512], F32)) for i in range(8)]
        gen_rr = [0]

        def bq(i, qs=(0, 1, 2, 3)):
            return [(f"ps{i}", q) for q in qs]

        def gbank():
            i = gen_rr[0] % 2
            gen_rr[0] += 1
            return psb[i], bq(i)

        def fbank(i):
            return psb[i], bq(i)

        ident = sb("ident", [128, 128], F32)
        tri = sb("tri", [128, 128], F32)
        wmid = sb("wmid", [128, 128], F32)
        sup = sb("sup", [128, 128], F32)
        ones = sb("ones", [128, 128], F32)
        sel63 = sb("sel63", [128, 128], F32)
        maskc = sb("maskc", [128, 128], F32)
        trib = sb("trib", [128, 128], BF16)
        iof = sb("iof", [128, 128], F32)
        dif = maskc
        pidx = sel63
        k.op("pool", lambda e: e.iota(iof[:], pattern=[[1, 128]], base=0, channel_multiplier=0,
                                      allow_small_or_imprecise_dtypes=True), w=["iof"])
        k.op("pool", lambda e: e.iota(dif[:], pattern=[[1, 128]], base=0, channel_multiplier=-1,
                                      allow_small_or_imprecise_dtypes=True), w=["maskc"])
        k.op("pool", lambda e: e.iota(pidx[:], pattern=[[0, 128]], base=0, channel_multiplier=1,
                                      allow_small_or_imprecise_dtypes=True), w=["sel63"])
        k.ts("dve", ident[:], dif[:], 0.0, None, ALU.is_equal, r=["maskc"], w=["ident"])
        k.ts("dve", tri[:], dif[:], 0.0, None, ALU.is_ge, r=["maskc"], w=["tri"])
        k.ts("dve", sup[:], dif[:], 0.0, None, ALU.is_lt, r=["maskc"], w=["sup"])
        k.ts("dve", sel63[:], pidx[:], 63.0, None, ALU.is_le, r=["sel63"], w=["sel63"])
        k.tt("dve", wmid[:], tri[:], sel63[:], ALU.subtract, r=["tri", "sel63"], w=["wmid"])
        k.memset("dve", ones[:], 1.0, w=["ones"])
        epsc = sb("epsc", [128, 1], F32)
        k.memset("dve", epsc[:], EPS, w=["epsc"])
        k.ts("dve", maskc[:], tri[:], 0.125, None, ALU.mult, r=["tri"], w=["maskc"])
        k.copy("dve", trib[:], tri[:], r=["tri"], w=["trib"])

        tbl_jobs = []
        if ntile > 0:
            w_q_r = w_q.rearrange("(c p) n -> p c n", p=128)
            for c in range(8):
                k.dma("pool", wqd[:, c, :], w_q_r[:, c, :], w=["wqd"])
            for i in range(128):
                tbl_jobs.append((ub, uT, "ub", i))
                tbl_jobs.append((vb, vv, "vb", i))

        with ExitStack() as st1:
            def sb1(name, shape, dt):
                return st1.enter_context(nc.sbuf_tensor(name, shape, dt))

            winb = sb1("winb", [128, 8, INW], BF16)
            woutb = sb1("woutb", [128, 8, D], BF16)
            wgub = sb1("wgub", [32, 256], BF16)
            lng = sb1("lng", [128, 4, D], F32)
            gngb = sb1("gngb", [128, 2, 512], F32)
            bfb = sb1("bfb", [128, 8], F32)
            w_in_r = w_in.rearrange("(c p) n -> p c n", p=128)
            w_out_r = w_out.rearrange("(c p) n -> p c n", p=128)
            for c in range(8):
                k.dma("pool", winb[:, c, :], w_in_r[:, c, :], w=["winb"])
            for c in range(8):
                k.dma("pool", woutb[:, c, :], w_out_r[:, c, :], w=["woutb"])
            k.dma("pool", wgub[0:17, :], wgu, w=["wgub"])
            for j in range(4):
                k.dma("sp", lng[:, j, :], lnv[j].partition_broadcast(128), w=["lng"])
            for j in range(2):
                k.dma("sp", gngb[:, j, :], gng[j].partition_broadcast(128), w=["gngb"])
            k.dma("sp", bfb[:], bfg.partition_broadcast(128), w=["bfb"])

            KT = sb1("KT", [128, 4, nblk * 128], BF16)
            VP = sb1("VP", [128, nblk, 8, 65], BF16)
            negc = sb1("negc", [128, nblk, 8], F32)
            carry = sb1("carry", [128, 8], F32)
            cmid = sb1("cmid", [128, 8], F32)
            Sf = sb1("Sf", [128, 2, 128], F32)
            Sb = sb1("Sb", [128, 2, 128], BF16)
            k.memset("pool", VP[:], 1.0, w=[("VP", j) for j in range(nblk)])
            k.memset("pool", carry[:], 0.0, w=["carry"])
            k.memset("pool", Sf[:], 0.0, w=["Sf"])
            k.memset("pool", Sb[:], 0.0, w=["Sb"])

            xt = [sb1("xt0", [128, D], F32)]
            h0 = [sb1(f"h0{i}", [128, D], F32) for i in range(2)]
            h0T = sb1("h0T", [128, 8, 128], BF16)
            bst = sb1("bst", [128, 2, 6], F32)
            mv = sb1("mv", [128, 2], F32)
            rstd = sb1("rstd", [128, 1], F32)
            qTf = sb1("qTf", [128, 2, 4, 128], BF16)
            k.memset("pool", qTf[:], 0.0, w=["qTf"])
            qkg = sb1("qkg", [128, 4, 128], F32)
            gaT = sb1("gaT", [32, 128], BF16)
            k.memset("dve", gaT[:], 1.0, w=["gaT"])
            eg = sb1("eg", [128, 256], F32)
            glog = sb1("glog", [128, 256], F32)
            E1 = sb1("E1", [128, 2, 128], F32)
            E2 = sb1("E2", [128, 2, 128], F32)
            E3 = sb1("E3", [128, 2, 128], F32)
            Erev = sb1("Erev", [128, 256], F32)
            q_in = sb1("q_in", [128, 2, 128], BF16)
            k_in = sb1("k_in", [128, 2, 2, 128], BF16)
            k.memset("pool", k_in[:], 0.0, w=["k_in"])
            q_dec = sb1("q_dec", [128, 2, 2, 128], BF16)
            k.memset("pool", q_dec[:], 0.0, w=["q_dec"])
            k_dec = sb1("k_dec", [128, 256], BF16)
            vg = sb1("vg", [128, 512], BF16)
            sr = sb1("sr", [128, 512], F32)
            AM = sb1("AM", [128, 4, 128], BF16)
            o_sb = sb1("o_sb", [128, 4, 128], F32)
            sq = sb1("sq", [128, 512], F32)
            ss = sb1("ss", [128, 8], F32)
            zt = sb1("zt", [128, 8], F32)
            logf = sb1("logf", [128, 8], F32)
            biasb = sb1("biasb", [128, nblk, 8], F32)
            NPT = 8
            PT = [sb1(f"PT{i}", [128, 128], BF16) for i in range(NPT)]
            rinv = sb1("rinv", [128, 8], F32)
            on = o_sb[:].rearrange("p a (b c) -> p (a b) c", c=64)
            ocat = sb1("ocat", [128, D], F32)
            ocatT = sb1("ocatT", [128, 8, 128], BF16)
            rr = sb1("rr", [128, D], F32)

            def layer_norm(src, dst, gi, sname_, dname_):
                for hh in range(2):
                    k.op("dve", lambda e, hh=hh: e.bn_stats(out=bst[:, hh, :], in_=src[:, hh * 512:(hh + 1) * 512]),
                         r=[sname_], w=["bst"])
                k.op("dve", lambda e: e.bn_aggr(out=mv[:], in_=bst[:].rearrange("p a b -> p (a b)")), r=["bst"], w=["mv"])
                k.act(rstd[:], mv[:, 1:2], AF.Ln, bias=epsc[:, 0:1], r=["mv", "epsc"], w=["rstd"])
                k.act(rstd[:], rstd[:], AF.Exp, scale=-0.5, r=["rstd"], w=["rstd"])
                k.ts("dve", dst, src, mv[:, 0:1], rstd[:], ALU.subtract, ALU.mult, r=[sname_, "mv", "rstd"], w=[dname_])
                k.tt("pool", dst, dst, lng[:, gi, :], ALU.mult, r=[dname_, "lng"], w=[dname_])
                k.tt("pool", dst, dst, lng[:, gi + 1, :], ALU.add, r=[dname_, "lng"], w=[dname_])

            rrc = {"pt": 0, "sc": 0}

            def front(b):
                xb_ = xt[0]
                hb = h0[b % 2]
                xname = "xt0"
                hname = f"h0{b % 2}"
                for _ in range(8):
                    if tbl_jobs:
                        dst_, src_, nm_, i_ = tbl_jobs.pop(0)
                        k.dma("pool", dst_[i_], src_[i_], w=[(nm_, i_)])
                if b == 0:
                    k.memset("pool", xb_[:], 0.0, w=[xname])
                    k.dma("sp", xb_[112:128, :], meta, w=[xname])
                else:
                    k.dma("sp", xb_[:], x[(b - 1) * 128:b * 128, :], w=[xname])
                layer_norm(xb_[:], hb[:], 0, xname, hname)
                if debug:
                    k.dma("sp", dbg["d_h0"][b], hb[:], r=[hname])
                for half in range(2):
                    yield
                    pt_, pn = gbank()
                    for j in range(4):
                        c = half * 4 + j
                        k.tr(pt_[:, j * 128:(j + 1) * 128], hb[:, c * 128:(c + 1) * 128], ident[:], r=[hname, "ident"], w=pn)
                    k.acopy(h0T[:, half * 4:(half + 1) * 4, :], pt_[:].rearrange("p (a b) -> p a b", a=4), r=pn, w=["h0T"])
                yield
                pt_, pn = gbank()
                for j, c0 in enumerate((QA0, QA0 + 128, KA0, KA0 + 128)):
                    for c in range(8):
                        k.mm(pt_[:, j * 128:(j + 1) * 128], winb[:, c, c0:c0 + 128], h0T[:, c, :], start=(c == 0), stop=(c == 7),
                             r=["winb", "h0T"], w=pn)
                k.acopy(qkg[:], pt_[:].rearrange("p (a b) -> p a b", a=4), r=pn, w=["qkg"])
                yield
                pt_, pn = gbank()
                for j in range(4):
                    for c in range(8):
                        k.mm(pt_[:, j * 128:(j + 1) * 128], winb[:, c, KB0 + j * 128:KB0 + (j + 1) * 128], h0T[:, c, :],
                             start=(c == 0), stop=(c == 7), r=["winb", "h0T"], w=pn)
                k.acopy(KT[:, :, b * 128:(b + 1) * 128], pt_[:].rearrange("p (a b) -> p a b", a=4), r=pn, w=[("KT", b)])
                yield
                pt_, pn = gbank()
                for c in range(8):
                    k.mm(pt_[:, 0:128], winb[:, c, GA0:GA0 + 128], h0T[:, c, :], start=(c == 0), stop=(c == 7),
                         r=["winb", "h0T"], w=pn)
                for c in range(8):
                    k.mm(pt_[:, 128:256], h0T[:, c, :], winb[:, c, FB0 - 120:FB0 + 8], start=(c == 0), stop=(c == 7),
                         r=["winb", "h0T"], w=pn)
                k.acopy(gaT[0:16, :], pt_[0:16, 0:128], r=pn, w=["gaT"])
                k.acopy(zt[:], pt_[:, 248:256], r=pn, w=["zt"])
                k.tt("dve", zt[:], zt[:], bfb[:], ALU.add, r=["zt", "bfb"], w=["zt"])
                yield
                pk_, pkn = fbank(2)
                for c in range(8):
                    k.mm(pk_[:, 0:256], h0T[:, c, :], winb[:, c, KA0:KA0 + 256], start=(c == 0), stop=(c == 7),
                         r=["winb", "h0T"], w=pkn)
                yield
                pt_, pn = gbank()
                for c in range(8):
                    k.mm(pt_[:, :], h0T[:, c, :], winb[:, c, VA0:VA0 + 512], start=(c == 0), stop=(c == 7),
                         r=["winb", "h0T"], w=pn)
                k.acopy(vg[:], pt_[:], r=pn, w=["vg"])
                yield
                pz_, pzn = gbank()
                k.mm(pz_[:, 0:256], gaT[0:17, :], wgub[0:17, :], r=["gaT", "wgub"], w=pzn)
                k.act(eg[:], pz_[:, 0:256], AF.Exp, scale=-1.0, r=pzn, w=["eg"])
                k.act(zt[:], zt[:], AF.Exp, scale=-1.0, r=["zt"], w=["zt"])
                k.act(eg[:], eg[:], AF.Ln, bias=1.0, r=["eg"], w=["eg"])
                k.act(zt[:], zt[:], AF.Ln, bias=1.0, r=["zt"], w=["zt"])
                k.ts("dve", glog[:], eg[:], -1.0 / 16.0, None, ALU.mult, r=["eg"], w=["glog"])
                k.ts("dve", logf[:], zt[:], -1.0, None, ALU.mult, r=["zt"], w=["logf"])
                if debug:
                    k.dma("sp", dbg["d_glog"][b], glog[:], r=["glog"])
                    k.dma("sp", dbg["d_logf"][b], logf[:], r=["logf"])
                yield
                pt_, pn = gbank()
                for c in range(8):
                    k.mm(pt_[:, :], h0T[:, c, :], winb[:, c, RA0:RA0 + 512], start=(c == 0), stop=(c == 7),
                         r=["winb", "h0T"], w=pn)
                k.act(sr[:], pt_[:], AF.Silu, r=pn, w=["sr"])
                yield
                pt_, pn = gbank()
                for c in range(8):
                    k.mm(pt_[:, :], h0T[:, c, :], winb[:, c, VB0:VB0 + 512], start=(c == 0), stop=(c == 7),
                         r=["winb", "h0T"], w=pn)
                k.acopy(VP[:, b, :, 0:64], pt_[:].rearrange("p (h d) -> p h d", h=8), r=pn, w=[("VP", b)])

                yield
                pd_, pdn = fbank(3)
                for p_ in range(2):
                    k.mm(pd_[:, p_ * 128:(p_ + 1) * 128], glog[:, p_ * 128:(p_ + 1) * 128], wmid[:], r=["glog", "wmid"], w=pdn)
                    k.mm(pd_[:, 256 + p_ * 128:256 + (p_ + 1) * 128], glog[:, p_ * 128:(p_ + 1) * 128], tri[:],
                         r=["glog", "tri"], w=pdn)
                k.mm(pk_[:, 256:512], sup[:], glog[:], r=["glog", "sup"], w=pkn)
                k.act(E1[:], pd_[:, 0:256].rearrange("p (a b) -> p a b", a=2), AF.Exp, r=pdn, w=["E1"])
                k.act(E2[:], pd_[:, 0:256].rearrange("p (a b) -> p a b", a=2), AF.Exp, scale=-1.0, r=pdn, w=["E2"])
                k.act(E3[:], pd_[:, 256:512].rearrange("p (a b) -> p a b", a=2), AF.Exp, r=pdn, w=["E3"])
                k.act(Erev[:], pk_[:, 256:512], AF.Exp, r=pkn, w=["Erev"])
                k.tt("dve", q_in[:], qkg[:, 0:2, :], E1[:], ALU.mult, r=["qkg", "E1"], w=["q_in"])
                for par in range(2):
                    rs = slice(par * 64, (par + 1) * 64)
                    k.tt("dve", k_in[rs, par, :, :], qkg[rs, 2:4, :], E2[rs, :, :], ALU.mult, r=["qkg", "E2"], w=["k_in"])
                for par in range(2):
                    rs = slice(par * 64, (par + 1) * 64)
                    k.stt("dve", q_dec[rs, par, :, :], qkg[rs, 0:2, :], 0.125, E3[rs, :, :], ALU.mult, ALU.mult, r=["qkg", "E3"], w=["q_dec"])
                k.tt("dve", k_dec[:], pk_[:, 0:256], Erev[:], ALU.mult, r=pkn + ["Erev"], w=["k_dec"])
                if b == 0:
                    k.memset("dve", k_in[:, :, :, 0:112], 0.0, w=["k_in"])
                    k.memset("dve", k_dec[0:112, :], 0.0, w=["k_dec"])
                yield
                pa_, pan = fbank(3)
                for h in range(4):
                    p_, r0 = h // 2, (h % 2) * 64
                    k.mm(pa_[:, h * 128:(h + 1) * 128], k_in[:, h % 2, p_, :], q_in[:, p_, :],
                         r=["k_in", "q_in"], w=pan)
                k.tt("dve", AM[:], pa_[:].rearrange("p (a b) -> p a b", a=4), maskc[:].unsqueeze(1).to_broadcast([128, 4, 128]),
                     ALU.mult, r=pan + ["maskc"], w=["AM"])
                yield
                po_, pon = fbank(2)
                if b > 0:
                    for h in range(4):
                        p_, r0 = h // 2, (h % 2) * 64
                        k.mm(po_[:, h * 128:(h + 1) * 128], AM[:, h, :], vg[:, h * 128:(h + 1) * 128], start=True, stop=False,
                             r=["AM", "vg"], w=pon)
                        k.mm(po_[:, h * 128:(h + 1) * 128], q_dec[:, h % 2, p_, :], Sb[:, p_, :], start=False, stop=True,
                             r=["q_dec", "Sb"], w=pon)
                yield
                ps_, psn = fbank(3)
                for h in range(4):
                    p_ = h // 2
                    k.mm(ps_[:, h * 128:(h + 1) * 128], k_dec[:, p_ * 128:(p_ + 1) * 128], vg[:, h * 128:(h + 1) * 128],
                         r=["k_dec", "vg"], w=psn)
                for h in range(4):
                    p_, r0 = h // 2, (h % 2) * 64
                    k.stt("dve", Sf[r0:r0 + 64, p_, :], Sf[r0:r0 + 64, p_, :], E3[r0:r0 + 64, p_, 127:128],
                          ps_[r0:r0 + 64, h * 128:(h + 1) * 128], ALU.mult, ALU.add, r=["Sf", "E3"] + psn, w=["Sf"])
                k.copy("dve", Sb[:], Sf[:], r=["Sf"], w=["Sb"])

                yield
                pf_, pfn = gbank()
                k.mm(pf_[:, 0:8], tri[:], logf[:], r=["tri", "logf"], w=pfn)
                k.mm(pf_[:, 8:16], ones[:], logf[:], r=["ones", "logf"], w=pfn)
                k.mm(pf_[:, 16:24], sel63[:], logf[:], r=["sel63", "logf"], w=pfn)
                k.stt("dve", negc[:, b, :], pf_[:, 0:8], -1.0, carry[:], ALU.mult, ALU.subtract, r=pfn + ["carry"], w=[("negc", b)])
                if b == 0:
                    k.memset("dve", negc[0:112, 0, :], NEGBIG, w=[("negc", b)])
                k.tt("dve", cmid[:], pf_[:, 16:24], carry[:], ALU.add, r=pfn + ["carry"], w=["cmid"])
                k.tt("dve", carry[:], pf_[:, 8:16], carry[:], ALU.add, r=pfn + ["carry"], w=["carry"])


            def foxq(b):
                pt_, pn = gbank()
                for j in range(4):
                    for c in range(8):
                        k.mm(pt_[:, j * 128:(j + 1) * 128], winb[:, c, QB0 + j * 128:QB0 + (j + 1) * 128], h0T[:, c, :],
                             start=(c == 0), stop=(c == 7), r=["winb", "h0T"], w=pn)
                for par in range(2):
                    k.acopy(qTf[par * 64:(par + 1) * 64, par, :, :], pt_[par * 64:(par + 1) * 64, :].rearrange("p (a b) -> p a b", a=4),
                            r=pn, w=["qTf"])

            def back_pre(b):
                xb_ = xt[0]
                hb = h0[b % 2]
                xname = "xt0"
                hname = f"h0{b % 2}"
                po_, pon = fbank(2)
                k.acopy(o_sb[:], po_[:].rearrange("p (a b) -> p a b", a=4), r=pon, w=["o_sb"])
                if debug:
                    pass
                k.tt("dve", sq[:], o_sb[:].rearrange("p a b -> p (a b)"), o_sb[:].rearrange("p a b -> p (a b)"), ALU.mult,
                     r=["o_sb"], w=["sq"])
                k.reduce(ss[:, 0:4], sq[:].rearrange("p (a b) -> p a b", a=4), ALU.add, r=["sq"], w=["ss"])
                k.act(ss[:, 0:4], ss[:, 0:4], AF.Ln, bias=epsc[:, 0:1], scale=1.0 / 128.0, r=["ss", "epsc"], w=["ss"])
                k.act(ss[:, 0:4], ss[:, 0:4], AF.Exp, scale=-0.5, r=["ss"], w=["ss"])
                k.tt("dve", o_sb[:], o_sb[:], ss[:, 0:4].unsqueeze(2).to_broadcast([128, 4, 128]), ALU.mult, r=["o_sb", "ss"], w=["o_sb"])
                k.tt("pool", sq[:], o_sb[:].rearrange("p a b -> p (a b)"), gngb[:, 0, :], ALU.mult, r=["o_sb", "gngb"], w=["sq"])
                k.tt("pool", ocat[:, 0:512], sq[:], sr[:], ALU.mult, r=["sq", "sr"], w=["ocat_a"])
                if debug:
                    k.dma("sp", dbg["d_oa"][b], ocat[:, 0:512], r=["ocat_a"])

                k.tt("dve", biasb[:, 0:b + 1, :], negc[:, 0:b + 1, :], cmid[:].unsqueeze(1).to_broadcast([128, b + 1, 8]), ALU.add,
                     r=[("negc", j) for j in range(b + 1)] + ["cmid"], w=["biasb"])

            def attention(b, gen):
                iters = [(h, kb) for h in range(8) for kb in range(b + 1)]
                nbat = (len(iters) + 3) // 4
                binfo = {}
                for m in range(nbat + 2):
                    if gen is not None:
                        next(gen, None)
                    if m < nbat:
                        sb_i = 4 + (rrc["sc"] % 2)
                        rrc["sc"] += 1
                        lst = []
                        for j, (h, kb) in enumerate(iters[4 * m:4 * m + 4]):
                            k.mm(psb[sb_i][:, j * 128:(j + 1) * 128], KT[:, h // 2, kb * 128:(kb + 1) * 128], qTf[:, h % 2, h // 2, :],
                                 r=[("KT", kb), "qTf"], w=bq(sb_i))
                            lst.append((h, kb, j))
                        binfo[m] = (sb_i, lst, [])
                    if 1 <= m <= nbat:
                        sb_i, lst, pts = binfo[m - 1]
                        for (h, kb, j) in lst:
                            ptile = PT[rrc["pt"] % NPT]
                            pname = f"PT{rrc["pt"] % NPT}"
                            rrc["pt"] += 1
                            k.act(ptile[:], psb[sb_i][:, j * 128:(j + 1) * 128], AF.Exp, bias=biasb[:, kb, h:h + 1], scale=0.125,
                                  r=bq(sb_i) + ["biasb"], w=[pname])
                            if kb == b:
                                k.tt("pool", ptile[:], ptile[:], trib[:], ALU.mult, r=[pname, "trib"], w=[pname])
                            pts.append((ptile, pname))
                    if m >= 2:
                        sb_i, lst, pts = binfo.pop(m - 2)
                        for (h, kb, j), (ptile, pname) in zip(lst, pts):
                            obn = f"ps{6 + h // 4}"
                            oc0 = (h % 4) * 65
                            k.mm(psb[6 + h // 4][:, oc0:oc0 + 65], ptile[:], VP[:, kb, h, :], start=(kb == 0), stop=(kb == b),
                                 r=[pname, ("VP", kb)], w=[(obn, h % 4)])

            def back_post(b):
                xb_ = xt[0]
                hb = h0[b % 2]
                xname = "xt0"
                hname = f"h0{b % 2}"
                for g_ in range(2):
                    obank = psb[6 + g_]
                    ov = obank[:, 0:260].rearrange("p (h d) -> p h d", h=4)
                    onames = [(f"ps{6 + g_}", j) for j in range(4)]
                    k.op("dve", lambda e, ov=ov, g_=g_: e.reciprocal(out=rinv[:, g_ * 4:(g_ + 1) * 4], in_=ov[:, :, 64]),
                         r=onames, w=[("rinv", g_)])
                    k.tt("dve", on[:, g_ * 4:(g_ + 1) * 4, :], ov[:, :, 0:64],
                         rinv[:, g_ * 4:(g_ + 1) * 4].unsqueeze(2).to_broadcast([128, 4, 64]), ALU.mult,
                         r=onames + [("rinv", g_)], w=["o_sb"])
                onf = on[:].rearrange("p h d -> p (h d)")
                k.tt("dve", sq[:], onf, onf, ALU.mult, r=["o_sb"], w=["sq"])
                k.reduce(ss[:], sq[:].rearrange("p (a b) -> p a b", a=8), ALU.add, r=["sq"], w=["ss"])
                k.act(ss[:], ss[:], AF.Ln, bias=epsc[:, 0:1], scale=1.0 / 64.0, r=["ss", "epsc"], w=["ss"])
                k.act(ss[:], ss[:], AF.Exp, scale=-0.5, r=["ss"], w=["ss"])
                k.tt("dve", on[:], on[:], ss[:].unsqueeze(2).to_broadcast([128, 8, 64]), ALU.mult,
                     r=["o_sb", "ss"], w=["o_sb"])
                k.tt("pool", ocat[:, 512:1024], onf, gngb[:, 1, :], ALU.mult, r=["o_sb", "gngb"], w=["ocat_b"])
                if debug:
                    k.dma("sp", dbg["d_ob"][b], ocat[:, 512:1024], r=["ocat_b"])

                for half in range(2):
                    pt_, pn = gbank()
                    for j in range(4):
                        c = half * 4 + j
                        k.tr(pt_[:, j * 128:(j + 1) * 128], ocat[:, c * 128:(c + 1) * 128], ident[:],
                             r=["ocat_a", "ocat_b", "ident"], w=pn)
                    k.acopy(ocatT[:, half * 4:(half + 1) * 4, :], pt_[:].rearrange("p (a b) -> p a b", a=4), r=pn, w=["ocatT"])
                for half in range(2):
                    pt_, pn = gbank()
                    for c in range(8):
                        k.mm(pt_[:, :], ocatT[:, c, :], woutb[:, c, half * 512:(half + 1) * 512], start=(c == 0), stop=(c == 7),
                             r=["ocatT", "woutb"], w=pn)
                    k.stt("dve", rr[:, half * 512:(half + 1) * 512], hb[:, half * 512:(half + 1) * 512], DN_ALPHA, pt_[:],
                          ALU.mult, ALU.add, r=[hname] + pn, w=["rr"])
                layer_norm(rr[:], rr[:], 2, "rr", "rr")
                k.dma("sp", h1d[(b - 1) * 128:b * 128, :], rr[:], r=["rr"], w=[("h1d", b - 1)])
                if debug:
                    k.dma("sp", dbg["d_h1"][b], rr[:], r=["rr"])


            for _ in front(0):
                pass
            if nblk > 1:
                for _ in front(1):
                    pass
                foxq(1)
            for b in range(1, nblk):
                back_pre(b)
                gen = front(b + 1) if b + 1 < nblk else None
                attention(b, gen)
                if gen is not None:
                    for _ in gen:
                        pass
                    foxq(b + 1)
                back_post(b)

        while tbl_jobs:
            dst_, src_, nm_, i_ = tbl_jobs.pop(0)
            k.dma("pool", dst_[i_], src_[i_], w=[(nm_, i_)])
        fence = P.tails()
        for en in ENGINES:
            P.wait_only(en, fence)
        phase2(nc, P, k, psb, ident, iof, epsc, h1d, ub, vb, wqd, skT, lnv, out, ntile)

        P.wait_only("sp", P.tails())
        P.emit(nc, st)
    return nc


def phase2(nc, P, k, psb, ident, iof, epsc, h1d, ub, vb, wqd, skT, lnv, out, ntile):
    if ntile == 0:
        return
    with ExitStack() as st2:
        def sb2(name, shape, dt):
            return st2.enter_context(nc.sbuf_tensor(name, shape, dt))

        def bq(i, qs=(0, 1, 2, 3)):
            return [(f"ps{i}", q) for q in qs]

        grr = [0]
        gbase = [2]

        def gbank():
            i = gbase[0] + grr[0] % 2
            grr[0] += 1
            return psb[i], bq(i)

        wqs = [sb2(f"wqs{i}", [128, 8, 512], BF16) for i in range(2)]
        skTb = sb2("skTb", [128, 16, 128], BF16)
        ln2 = sb2("ln2", [128, 2, D], F32)
        k.dma("pool", skTb[:], skT.rearrange("hc d n -> d hc n"), w=["skTb"])
        for j in range(2):
            k.dma("sp", ln2[:, j, :], lnv[4 + j].partition_broadcast(128), w=["ln2"])
        h1t2 = [sb2(f"h1t{i}", [128, 2, D], F32) for i in range(2)]
        h1T2 = [sb2(f"h1T{i}", [128, 8, 256], BF16) for i in range(2)]
        qT = sb2("qT", [128, 16, 256], BF16)
        s_sb = sb2("s_sb", [128, 16, 128], F32)
        s_tmp = [sb2(f"s_tmp{i}", [128, 128], F32) for i in range(2)]
        topv = sb2("topv", [128, 16, 16], F32)
        topi = sb2("topi", [128, 16, 16], U32)
        topif = sb2("topif", [128, 16, 16], F32)
        cand = sb2("cand", [128, 8, 256], F32)
        cand_tmp = [sb2(f"cand_tmp{i}", [128, 256], F32) for i in range(2)]
        bv = sb2("bv", [128, 8, 16], F32)
        pos = sb2("pos", [128, 8, 16], U32)
        posf = sb2("posf", [128, 8, 16], F32)
        akf = sb2("akf", [128, 8, 16], F32)
        bkf = sb2("bkf", [128, 8, 16], F32)
        eq4 = sb2("eq4", [128, 8, 16, 16], F32)
        thr16 = sb2("thr16", [128, 16], F32)
        iota16 = sb2("iota16", [128, 16], F32)
        If = sb2("If", [128, 8, 16], F32)
        Jf = sb2("Jf", [128, 8, 16], F32)
        gf = sb2("gf", [128, 8, 16], F32)
        gsum = sb2("gsum", [128, 8], F32)
        IJG = sb2("IJG", [128, 2, 3, 128], F32)
        IT = sb2("IT", [128, 256], F32)
        JT = sb2("JT", [128, 256], F32)
        gT = sb2("gT", [128, 256], F32)
        NAB = 6
        AENG = "dve"
        GAENG = "dve"
        Bt = [sb2(f"Bt{i}", [128, 128], BF16) for i in range(NAB)]
        At = [sb2(f"At{i}", [128, 128], BF16) for i in range(NAB)]
        G_sb = sb2("G_sb", [128, 256, 128], BF16)
        NU = 5
        uTs = [sb2(f"uTs{i}", [128, 8, 128], BF16) for i in range(NU)]
        vs = [sb2(f"vs{i}", [128, D], BF16) for i in range(NU)]
        ga_sb = [sb2(f"ga_sb{i}", [128, 256], BF16) for i in range(3)]
        GA = [sb2(f"GA{i}", [128, 256], BF16) for i in range(5)]
        rr2s = [sb2(f"rr2_{i}", [128, D], F32) for i in range(2)]
        bst = sb2("bst2", [128, 2, 6], F32)
        mv = sb2("mv2", [128, 2], F32)
        rstd = sb2("rstd2", [128, 1], F32)
        iob = sb2("iob", [128, 128], BF16)
        k.copy("dve", iob[:], iof[:], r=["iof"], w=["iob"])
        k.ts("dve", iota16[:], iof[:, 0:16], 1.0, None, ALU.mult, r=["iof"], w=["iota16"])
        k.ts("dve", thr16[:], iof[:, 0:16], 16.0, 15.5, ALU.mult, ALU.add, r=["iof"], w=["thr16"])

        def routing(ti):
            h1t = h1t2[ti % 2]
            h1T = h1T2[ti % 2]
            h1tn = f"h1t{ti % 2}"
            h1Tn = f"h1T{ti % 2}"
            nops = [len(P.ops)]

            def tick():
                if len(P.ops) - nops[0] >= 4:
                    nops[0] = len(P.ops)
                    return True
                return False
            for tb in range(2):
                blk = 2 * ti + tb
                k.dma("sp", h1t[:, tb, :], h1d[blk * 128:(blk + 1) * 128, :], r=[("h1d", blk)], w=[(h1tn, tb)])
            for tb in range(2):
                for half in range(2):
                    yield
                    pt_, pn = gbank()
                    for j in range(4):
                        c = half * 4 + j
                        k.tr(pt_[:, j * 128:(j + 1) * 128], h1t[:, tb, c * 128:(c + 1) * 128], ident[:], r=[(h1tn, tb), "ident"], w=pn)
                    k.acopy(h1T[:, half * 4:(half + 1) * 4, tb * 128:(tb + 1) * 128], pt_[:].rearrange("p (a b) -> p a b", a=4),
                            r=pn, w=[h1Tn])
            for hp in range(8):
                pt_, pn = gbank()
                for j in range(2):
                    yield
                    hc = hp * 2 + j
                    wq_ = wqs[(hc // 4) % 2]
                    wqn = f"wqs{(hc // 4) % 2}"
                    if hc % 4 == 0:
                        k.dma("sp", wq_[:], wqd[:, :, hc * 128:(hc + 4) * 128], r=["wqd"], w=[wqn])
                    for c in range(8):
                        k.mm(pt_[:, j * 256:(j + 1) * 256], wq_[:, c, (hc % 4) * 128:(hc % 4 + 1) * 128], h1T[:, c, :], start=(c == 0), stop=(c == 7),
                             r=[wqn, h1Tn], w=pn)
                k.acopy(qT[:, hp * 2:hp * 2 + 2, :], pt_[:].rearrange("p (a b) -> p a b", a=2), r=pn, w=["qT"])
            for tb in range(2):
                for g4 in range(4):
                    yield
                    pt_, pn = gbank()
                    for j in range(4):
                        hc = g4 * 4 + j
                        k.mm(pt_[:, j * 128:(j + 1) * 128], qT[:, hc, tb * 128:(tb + 1) * 128], skTb[:, hc, :], r=["qT", "skTb"], w=pn)
                    k.acopy(s_sb[:, g4 * 4:(g4 + 1) * 4, :], pt_[:].rearrange("p (a b) -> p a b", a=4), r=pn, w=["s_sb"])
                for hc in range(16):
                    yield
                    sv = s_sb[:, hc, :]
                    stp = s_tmp[hc % 2]
                    stn = f"s_tmp{hc % 2}"
                    k.op("dve", lambda e, sv=sv, hc=hc: e.max(out=topv[:, hc, 0:8], in_=sv), r=["s_sb"], w=["topv"])
                    k.op("dve", lambda e, sv=sv, hc=hc: e.max_index(out=topi[:, hc, 0:8], in_max=topv[:, hc, 0:8], in_values=sv),
                         r=["s_sb", "topv"], w=["topi"])
                    k.op("dve", lambda e, sv=sv, hc=hc, stp=stp: e.match_replace(out=stp[:], in_to_replace=topv[:, hc, 0:8], in_values=sv,
                                                                                  imm_value=-1e30), r=["s_sb", "topv"], w=[stn])
                    k.op("dve", lambda e, hc=hc, stp=stp: e.max(out=topv[:, hc, 8:16], in_=stp[:]), r=[stn], w=["topv"])
                    k.op("dve", lambda e, hc=hc, stp=stp: e.max_index(out=topi[:, hc, 8:16], in_max=topv[:, hc, 8:16], in_values=stp[:]),
                         r=[stn, "topv"], w=["topi"])
                k.copy("dve", topif[:], topi[:], r=["topi"], w=["topif"])
                tv = topv[:].rearrange("p (h c) a -> p h c a", c=2)
                tif = topif[:].rearrange("p (h c) a -> p h c a", c=2)
                yield
                k.tt("dve", cand[:].rearrange("p h (a b) -> p h a b", a=16), tv[:, :, 0, :].unsqueeze(3).to_broadcast([128, 8, 16, 16]),
                     tv[:, :, 1, :].unsqueeze(2).to_broadcast([128, 8, 16, 16]), ALU.add, r=["topv"], w=["cand"])
                for h in range(8):
                    yield
                    cv = cand[:, h, :]
                    ctp = cand_tmp[h % 2]
                    ctn = f"cand_tmp{h % 2}"
                    k.op("dve", lambda e, cv=cv, h=h: e.max(out=bv[:, h, 0:8], in_=cv), r=["cand"], w=["bv"])
                    k.op("dve", lambda e, cv=cv, h=h: e.max_index(out=pos[:, h, 0:8], in_max=bv[:, h, 0:8], in_values=cv),
                         r=["cand", "bv"], w=["pos"])
                    k.op("dve", lambda e, cv=cv, h=h, ctp=ctp: e.match_replace(out=ctp[:], in_to_replace=bv[:, h, 0:8], in_values=cv,
                                                                                imm_value=-1e30), r=["cand", "bv"], w=[ctn])
                    k.op("dve", lambda e, h=h, ctp=ctp: e.max(out=bv[:, h, 8:16], in_=ctp[:]), r=[ctn], w=["bv"])
                    k.op("dve", lambda e, h=h, ctp=ctp: e.max_index(out=pos[:, h, 8:16], in_max=bv[:, h, 8:16], in_values=ctp[:]),
                         r=[ctn, "bv"], w=["pos"])
                yield
                k.copy("dve", posf[:], pos[:], r=["pos"], w=["posf"])
                yield
                k.tt("dve", eq4[:, :, :, 0:15], posf[:].unsqueeze(3).to_broadcast([128, 8, 16, 15]),
                     thr16[:, 0:15].unsqueeze(1).unsqueeze(1).to_broadcast([128, 8, 16, 15]), ALU.is_ge, r=["posf", "thr16"], w=["eq4"])
                yield
                k.reduce(akf[:], eq4[:, :, :, 0:15], ALU.add, r=["eq4"], w=["akf"])
                k.stt("dve", bkf[:], akf[:], -16.0, posf[:], ALU.mult, ALU.add, r=["akf", "posf"], w=["bkf"])
                for (src, cidx, dst, dn) in ((akf, 0, If, "If"), (bkf, 1, Jf, "Jf")):
                    yield
                    k.tt("dve", eq4[:], src[:].unsqueeze(3).to_broadcast([128, 8, 16, 16]),
                         iota16[:].unsqueeze(1).unsqueeze(1).to_broadcast([128, 8, 16, 16]), ALU.is_equal,
                         r=["akf", "bkf", "iota16"], w=["eq4"])
                    yield
                    k.tt("dve", eq4[:], eq4[:], tif[:, :, cidx, :].unsqueeze(2).to_broadcast([128, 8, 16, 16]), ALU.mult,
                         r=["eq4", "topif"], w=["eq4"])
                    yield
                    k.reduce(dst[:], eq4[:], ALU.add, r=["eq4"], w=[dn])
                yield
                k.tt("dve", gf[:], bv[:], bv[:, :, 0:1].to_broadcast([128, 8, 16]), ALU.subtract, r=["bv"], w=["gf"])
                k.act(gf[:], gf[:], AF.Exp, r=["gf"], w=["gf"])
                k.reduce(gsum[:], gf[:], ALU.add, r=["gf"], w=["gsum"])
                k.op("dve", lambda e: e.reciprocal(out=gsum[:], in_=gsum[:]), r=["gsum"], w=["gsum"])
                k.tt("dve", gf[:], gf[:], gsum[:].unsqueeze(2).to_broadcast([128, 8, 16]), ALU.mult, r=["gf", "gsum"], w=["gf"])
                k.copy("dve", IJG[:, tb, 0, :], If[:].rearrange("p h k -> p (h k)"), r=["If"], w=[("IJG", tb)])
                k.copy("dve", IJG[:, tb, 1, :], Jf[:].rearrange("p h k -> p (h k)"), r=["Jf"], w=[("IJG", tb)])
                k.copy("dve", IJG[:, tb, 2, :], gf[:].rearrange("p h k -> p (h k)"), r=["gf"], w=[("IJG", tb)])
        def gbuild(ti, fin=None):
            for tb in range(2):
                pt_, pn = gbank()
                for j in range(3):
                    k.tr(pt_[:, j * 128:(j + 1) * 128], IJG[:, tb, j, :], ident[:], r=[("IJG", tb), "ident"], w=pn)
                for j, (dst, dn) in enumerate(((IT, "IT"), (JT, "JT"), (gT, "gT"))):
                    k.acopy(dst[:, tb * 128:(tb + 1) * 128], pt_[:, j * 128:(j + 1) * 128], r=pn, w=[dn])
            for t in range(256):
                if fin is not None and t >= 8 and t % 8 == 0:
                    next(fin, None)
                jb = t % NAB
                k.ts("dve", Bt[jb][:], iob[:], JT[:, t:t + 1], None, ALU.is_equal, r=["iob", "JT"], w=[f"Bt{jb}"])
                k.ts(AENG, At[jb][:], iob[:], IT[:, t:t + 1], gT[:, t:t + 1], ALU.is_equal, ALU.mult, r=["iob", "IT", "gT"], w=[f"At{jb}"])
                gb = 2 + (t // 4) % 2
                q_ = t % 4
                k.mm(psb[gb][:, q_ * 128:(q_ + 1) * 128], Bt[jb][:], At[jb][:], r=[f"Bt{jb}", f"At{jb}"], w=[(f"ps{gb}", q_)])
                if q_ == 3:
                    k.acopy(G_sb[:, t - 3:t + 1, :], psb[gb][:].rearrange("p (a b) -> p a b", a=4), r=bq(gb), w=["G_sb"])
        def gloop(ti, gen):
            h1T = h1T2[ti % 2]
            h1Tn = f"h1T{ti % 2}"
            LG = 4
            ginfo = {}
            for i in range(128 + LG):
                if i < 128:
                    ju = i % NU
                    k.dma("sp", uTs[ju][:], ub[i].rearrange("p (c e) -> p c e", c=8), r=[("ub", i)], w=[f"uTs{ju}"])
                    k.dma("act", vs[ju][:], vb[i], r=[("vb", i)], w=[f"vs{ju}"])
                    ab = i % 2
                    for c in range(8):
                        k.mm(psb[ab][:, 0:256], uTs[ju][:, c, :], h1T[:, c, :], start=(c == 0), stop=(c == 7), r=[f"uTs{ju}", h1Tn], w=bq(ab))
                    gs = ga_sb[i % 3]
                    gsn = f"ga_sb{i % 3}"
                    k.act(gs[:], psb[ab][:, 0:256], AF.Gelu, r=bq(ab), w=[gsn])
                    gA = GA[i % 5]
                    gAn = f"GA{i % 5}"
                    k.tt(GAENG, gA[:], gs[:], G_sb[:, :, i], ALU.mult, r=[gsn, "G_sb"], w=[gAn])
                    ginfo[i] = (gA, gAn, ju)
                if gen is not None:
                    next(gen, None)
                if i >= LG:
                    i2 = i - LG
                    gA, gAn, ju = ginfo.pop(i2)
                    for tb in range(2):
                        for half in range(2):
                            yb = 4 + tb * 2 + half
                            k.mm(psb[yb][:, :], gA[:, tb * 128:(tb + 1) * 128], vs[ju][:, half * 512:(half + 1) * 512],
                                 start=(i2 == 0), stop=(i2 == 127), r=[gAn, f"vs{ju}"], w=bq(yb))
        def finalize(ti):
            h1t = h1t2[ti % 2]
            h1tn = f"h1t{ti % 2}"
            for tb in range(2):
                blk = 2 * ti + tb
                rr2 = rr2s[tb]
                rn = f"rr2_{tb}"
                for half in range(2):
                    yb = 4 + tb * 2 + half
                    k.stt("dve", rr2[:, half * 512:(half + 1) * 512], h1t[:, tb, half * 512:(half + 1) * 512], DN_ALPHA, psb[yb][:, :],
                          ALU.mult, ALU.add, r=[(h1tn, tb)] + bq(yb), w=[rn])
                    yield
                for hh in range(2):
                    k.op("dve", lambda e, hh=hh, rr2=rr2: e.bn_stats(out=bst[:, hh, :], in_=rr2[:, hh * 512:(hh + 1) * 512]), r=[rn], w=["bst2"])
                k.op("dve", lambda e: e.bn_aggr(out=mv[:], in_=bst[:].rearrange("p a b -> p (a b)")), r=["bst2"], w=["mv2"])
                k.act(rstd[:], mv[:, 1:2], AF.Ln, bias=epsc[:, 0:1], r=["mv2", "epsc"], w=["rstd2"])
                k.act(rstd[:], rstd[:], AF.Exp, scale=-0.5, r=["rstd2"], w=["rstd2"])
                yield
                yield
                k.ts("dve", rr2[:], rr2[:], mv[:, 0:1], rstd[:], ALU.subtract, ALU.mult, r=[rn, "mv2", "rstd2"], w=[rn])
                k.tt("pool", rr2[:], rr2[:], ln2[:, 0, :], ALU.mult, r=[rn, "ln2"], w=[rn])
                k.tt("pool", rr2[:], rr2[:], ln2[:, 1, :], ALU.add, r=[rn, "ln2"], w=[rn])
                k.dma("sp", out[blk * 128:(blk + 1) * 128, :], rr2[:], r=[rn])
                yield

        gbase[0] = 0
        for _ in routing(0):
            pass
        fin = None
        for ti in range(ntile):
            gbase[0] = 2
            gbuild(ti, fin)
            if fin is not None:
                for _ in fin:
                    pass
            gen = routing(ti + 1) if ti + 1 < ntile else None
            gloop(ti, gen)
            if gen is not None:
                for _ in gen:
                    pass
            fin = finalize(ti)
        if fin is not None:
            for _ in fin:
                pass


def _prep_shared(inputs):
    f = lambda a: np.ascontiguousarray(np.asarray(a, dtype=np.float32))
    sh = {}
    sh["meta"] = f(inputs["meta_tokens"])
    sh["lnv"] = f(np.stack([inputs["emb_ln_g"], inputs["emb_ln_b"], inputs["ln1_g"][0], inputs["ln1_b"][0],
                            inputs["ln2_g"][0], inputs["ln2_b"][0]]))
    sh["w_in"] = f(inputs["w_in"][0])
    sh["wgu"] = f(np.concatenate([inputs["w_gate_up"][0], inputs["b_gate"][0][None, :]], axis=0))
    sh["bfg"] = f(inputs["b_forget"][0])
    sh["gng"] = f(np.stack([inputs["gla_norm_g"][0], inputs["fox_norm_g"][0]]))
    sh["w_out"] = f(inputs["w_out"][0])
    sh["w_q"] = f(inputs["peer_w_q"][0])
    sk = np.asarray(inputs["peer_sub_keys"][0], dtype=np.float32)
    sh["skT"] = f(sk.transpose(0, 1, 3, 2).reshape(16, 128, 128))
    u = np.asarray(inputs["peer_u"][0], dtype=np.float32)
    sh["uT"] = f(u.reshape(128, 128, 8, 128).transpose(0, 3, 2, 1).reshape(128, 128, 1024))
    sh["vv"] = f(np.asarray(inputs["peer_v"][0], dtype=np.float32).reshape(128, 128, 1024))
    return sh


def kernel(**inputs):
    nc = build()
    sh = _prep_shared(inputs)
    x = np.asarray(inputs["x"], dtype=np.float32)
    in_maps = []
    for b in range(8):
        m = dict(sh)
        m["x"] = np.ascontiguousarray(x[b])
        in_maps.append(m)
    res = run_bass_kernel_spmd(nc, in_maps, core_ids=list(range(8)))
    return np.stack([np.asarray(r["out"], dtype=np.float32) for r in res.results], axis=0)
```

```python
import numpy as np
from contextlib import ExitStack
import concourse.bass as bass
import concourse.mybir as mybir
from concourse.bass_utils import run_bass_kernel_spmd

ALU = mybir.AluOpType
AF = mybir.ActivationFunctionType
AX = mybir.AxisListType
F32 = mybir.dt.float32
BF16 = mybir.dt.bfloat16
U32 = mybir.dt.uint32

ENGINES = ("pe", "act", "dve", "pool", "sp")
SEM_ROLL = 30000
DMA_POOL = 12
NOSYNC_SAME = ("pe",)

D = 1024
SEQ = 4096
NBLK = 33
INW = 3096
QA0, KA0, VA0, RA0, GA0, QB0, KB0, VB0, FB0 = 0, 256, 512, 1024, 1536, 1552, 2064, 2576, 3088
DN_ALPHA = 2.0 ** 0.25
EPS = 1e-5
NEGBIG = -30000.0


class _Op:
    __slots__ = ("eng", "fn", "deps", "dma", "signal", "ticket", "dsem", "dval", "idx", "prev_same_sem", "noop")

    def __init__(self, eng, fn, deps, dma, idx):
        self.eng = eng
        self.fn = fn
        self.deps = deps
        self.dma = dma
        self.signal = dma
        self.ticket = None
        self.dsem = None
        self.dval = None
        self.idx = idx
        self.prev_same_sem = None
        self.noop = False


class Prog:
    def __init__(self, same_engine_sync=True):
        self.ops = []
        self.last_w = {}
        self.readers = {}
        self.same_engine_sync = same_engine_sync

    def add(self, eng, fn, reads=(), writes=(), dma=False, extra_deps=()):
        idx = len(self.ops)
        deps = set(extra_deps)
        for b in reads:
            w = self.last_w.get(b)
            if w is not None:
                deps.add(w)
        for b in writes:
            w = self.last_w.get(b)
            if w is not None:
                deps.add(w)
            for r in self.readers.get(b, ()):
                deps.add(r)
        for b in reads:
            self.readers.setdefault(b, []).append(idx)
        for b in writes:
            self.last_w[b] = idx
            self.readers[b] = []
        deps.discard(idx)
        self.ops.append(_Op(eng, fn, deps, dma, idx))
        return idx

    def wait_only(self, eng, deps):
        idx = self.add(eng, lambda e: None, extra_deps=deps)
        self.ops[idx].noop = True
        return idx

    def tails(self):
        t = [op.idx for op in self.ops if op.dma]
        for en in ENGINES:
            lst = [op.idx for op in self.ops if op.eng == en and not op.dma and not op.noop]
            if lst:
                t.append(lst[-1])
        return t

    def emit(self, nc, stack):
        ops = self.ops
        for op in ops:
            nd = set()
            for d in op.deps:
                p = ops[d]
                if (not p.dma) and p.eng == op.eng and (op.eng in NOSYNC_SAME or not self.same_engine_sync) and not op.dma:
                    continue
                nd.add(d)
            op.deps = nd
            for d in nd:
                ops[d].signal = True
        cnt = {e: 0 for e in ENGINES}
        dcnt = {e: 0 for e in ENGINES}
        nsem = {e: 1 for e in ENGINES}
        dma_hist = {e: [] for e in ENGINES}
        for op in ops:
            if op.dma:
                n = dcnt[op.eng]
                dcnt[op.eng] += 1
                op.dsem = (op.eng, n % DMA_POOL)
                op.dval = 16 * (n // DMA_POOL + 1)
                if n >= DMA_POOL:
                    op.prev_same_sem = dma_hist[op.eng][n - DMA_POOL]
                dma_hist[op.eng].append(op.idx)
            elif op.signal:
                cnt[op.eng] += 1
                c = cnt[op.eng]
                op.ticket = ((c - 1) // SEM_ROLL, (c - 1) % SEM_ROLL + 1)
                nsem[op.eng] = max(nsem[op.eng], op.ticket[0] + 1)
        sems = {}
        for e in ENGINES:
            for k in range(nsem[e]):
                sems[("c", e, k)] = stack.enter_context(nc.semaphore(f"s_{e}_{k}"))
            if dcnt[e] > 0:
                for k in range(min(DMA_POOL, dcnt[e])):
                    sems[("d", e, k)] = stack.enter_context(nc.semaphore(f"d_{e}_{k}"))
        block = stack.enter_context(nc.Block())
        per_eng = {e: [op for op in ops if op.eng == e] for e in ENGINES}

        def run_engine(ename, eng):
            waited = {}
            for op in per_eng[ename]:
                need = {}
                deps = set(op.deps)
                if op.prev_same_sem is not None:
                    deps.add(op.prev_same_sem)
                for d in deps:
                    p = ops[d]
                    if p.dma:
                        key = ("d",) + p.dsem
                        val = p.dval
                    else:
                        key = ("c", p.eng, p.ticket[0])
                        val = p.ticket[1]
                    if waited.get(key, 0) >= val:
                        continue
                    if need.get(key, 0) < val:
                        need[key] = val
                for key, val in need.items():
                    eng.wait_ge(sems[key], val)
                    waited[key] = val
                inst = op.fn(eng)
                if inst is None:
                    continue
                if op.dma:
                    inst.then_inc(sems[("d",) + op.dsem], 16)
                elif op.signal:
                    inst.then_inc(sems[("c", op.eng, op.ticket[0])], 1)

        if per_eng["pe"]:
            @block.tensor
            def _(e):
                run_engine("pe", e)
        if per_eng["act"]:
            @block.scalar
            def _(e):
                run_engine("act", e)
        if per_eng["dve"]:
            @block.vector
            def _(e):
                run_engine("dve", e)
        if per_eng["pool"]:
            @block.gpsimd
            def _(e):
                run_engine("pool", e)
        if per_eng["sp"]:
            @block.sync
            def _(e):
                run_engine("sp", e)


class K:
    def __init__(self, P):
        self.P = P

    def mm(self, out, lhsT, rhs, start=True, stop=True, r=(), w=()):
        return self.P.add("pe", lambda e: e.matmul(out, lhsT=lhsT, rhs=rhs, start=start, stop=stop), r, w)

    def tr(self, out, in_, ident, r=(), w=()):
        return self.P.add("pe", lambda e: e.transpose(out=out, in_=in_, identity=ident), r, w)

    def act(self, out, in_, func, r=(), w=(), bias=None, scale=None):
        kw = {}
        if bias is not None:
            kw["bias"] = bias
        if scale is not None:
            kw["scale"] = scale
        return self.P.add("act", lambda e: e.activation(out=out, in_=in_, func=func, **kw), r, w)

    def acopy(self, out, in_, r=(), w=()):
        return self.P.add("act", lambda e: e.copy(out=out, in_=in_), r, w)

    def tt(self, eng, out, in0, in1, op, r=(), w=()):
        return self.P.add(eng, lambda e: e.tensor_tensor(out=out, in0=in0, in1=in1, op=op), r, w)

    def ts(self, eng, out, in0, s1, s2, op0, op1=None, r=(), w=()):
        if op1 is None:
            return self.P.add(eng, lambda e: e.tensor_scalar(out=out, in0=in0, scalar1=s1, scalar2=None, op0=op0), r, w)
        return self.P.add(eng, lambda e: e.tensor_scalar(out=out, in0=in0, scalar1=s1, scalar2=s2, op0=op0, op1=op1), r, w)

    def stt(self, eng, out, in0, scalar, in1, op0, op1, r=(), w=()):
        return self.P.add(eng, lambda e: e.scalar_tensor_tensor(out=out, in0=in0, scalar=scalar, in1=in1, op0=op0, op1=op1), r, w)

    def copy(self, eng, out, in_, r=(), w=()):
        return self.P.add(eng, lambda e: e.tensor_copy(out=out, in_=in_), r, w)

    def memset(self, eng, ap, val, r=(), w=()):
        return self.P.add(eng, lambda e: e.memset(ap, val), r, w)

    def reduce(self, out, in_, op, r=(), w=()):
        return self.P.add("dve", lambda e: e.tensor_reduce(out=out, in_=in_, axis=AX.X, op=op), r, w)

    def dma(self, q, out, in_, r=(), w=()):
        return self.P.add(q, lambda e: e.dma_start(out=out, in_=in_), r, w, dma=True)

    def op(self, eng, fn, r=(), w=()):
        return self.P.add(eng, fn, r, w)


def build(nblk=NBLK, ntile=16, debug=False, stage=99):
    nc = bass.Bass("TRN2", target_bir_lowering=False)
    dt_in = lambda name, shape, dt=F32: nc.dram_tensor(name, shape, dt, kind="ExternalInput").ap()
    x = dt_in("x", [SEQ, D])
    meta = dt_in("meta", [16, D])
    lnv = dt_in("lnv", [6, D])
    w_in = dt_in("w_in", [D, INW])
    wgu = dt_in("wgu", [17, 256])
    bfg = dt_in("bfg", [8])
    gng = dt_in("gng", [2, 512])
    w_out = dt_in("w_out", [D, D])
    w_q = dt_in("w_q", [D, 2048])
    skT = dt_in("skT", [16, 128, 128])
    uT = dt_in("uT", [128, 128, 1024])
    vv = dt_in("vv", [128, 128, 1024])
    out = nc.dram_tensor("out", [SEQ, D], F32, kind="ExternalOutput").ap()
    h1d = nc.dram_tensor("h1d", [SEQ, D], F32).ap()
    ub = nc.dram_tensor("ub", [128, 128, 1024], BF16).ap()
    vb = nc.dram_tensor("vb", [128, 128, 1024], BF16).ap()
    wqd = nc.dram_tensor("wqd", [128, 8, 2048], BF16).ap()
    dbg = {}
    if debug:
        for name, shape in (("d_h0", [128, D]), ("d_glog", [128, 256]), ("d_oa", [128, 512]), ("d_ob", [128, 512]),
                            ("d_logf", [128, 8]), ("d_h1", [128, D])):
            dbg[name] = nc.dram_tensor(name, [nblk] + shape, F32, kind="ExternalOutput").ap()

    P = Prog()
    k = K(P)
    out_dmas = []

    with ExitStack() as st:
        def sb(name, shape, dt):
            return st.enter_context(nc.sbuf_tensor(name, shape, dt))

        psb = [st.enter_context(nc.psum_tensor(f"ps{i}", [128, 512], F32)) for i in range(8)]
        gen_rr = [0]

        def bq(i, qs=(0, 1, 2, 3)):
            return [(f"ps{i}", q) for q in qs]

        def gbank():
            i = gen_rr[0] % 2
            gen_rr[0] += 1
            return psb[i], bq(i)

        def fbank(i):
            return psb[i], bq(i)

        ident = sb("ident", [128, 128], F32)
        tri = sb("tri", [128, 128], F32)
        wmid = sb("wmid", [128, 128], F32)
        sup = sb("sup", [128, 128], F32)
        ones = sb("ones", [128, 128], F32)
        sel63 = sb("sel63", [128, 128], F32)
        maskc = sb("maskc", [128, 128], F32)
        trib = sb("trib", [128, 128], BF16)
        iof = sb("iof", [128, 128], F32)
        dif = maskc
        pidx = sel63
        k.op("pool", lambda e: e.iota(iof[:], pattern=[[1, 128]], base=0, channel_multiplier=0,
                                      allow_small_or_imprecise_dtypes=True), w=["iof"])
        k.op("pool", lambda e: e.iota(dif[:], pattern=[[1, 128]], base=0, channel_multiplier=-1,
                                      allow_small_or_imprecise_dtypes=True), w=["maskc"])
        k.op("pool", lambda e: e.iota(pidx[:], pattern=[[0, 128]], base=0, channel_multiplier=1,
                                      allow_small_or_imprecise_dtypes=True), w=["sel63"])
        k.ts("dve", ident[:], dif[:], 0.0, None, ALU.is_equal, r=["maskc"], w=["ident"])
        k.ts("dve", tri[:], dif[:], 0.0, None, ALU.is_ge, r=["maskc"], w=["tri"])
        k.ts("dve", sup[:], dif[:], 0.0, None, ALU.is_lt, r=["maskc"], w=["sup"])
        k.ts("dve", sel63[:], pidx[:], 63.0, None, ALU.is_le, r=["sel63"], w=["sel63"])
        k.tt("dve", wmid[:], tri[:], sel63[:], ALU.subtract, r=["tri", "sel63"], w=["wmid"])
        k.memset("dve", ones[:], 1.0, w=["ones"])
        epsc = sb("epsc", [128, 1], F32)
        k.memset("dve", epsc[:], EPS, w=["epsc"])
        k.ts("dve", maskc[:], tri[:], 0.125, None, ALU.mult, r=["tri"], w=["maskc"])
        k.copy("dve", trib[:], tri[:], r=["tri"], w=["trib"])

        tbl_jobs = []
        if ntile > 0:
            w_q_r = w_q.rearrange("(c p) n -> p c n", p=128)
            for c in range(8):
                k.dma("pool", wqd[:, c, :], w_q_r[:, c, :], w=["wqd"])
            for i in range(128):
                tbl_jobs.append((ub, uT, "ub", i))
                tbl_jobs.append((vb, vv, "vb", i))

        with ExitStack() as st1:
            def sb1(name, shape, dt):
                return st1.enter_context(nc.sbuf_tensor(name, shape, dt))

            winb = sb1("winb", [128, 8, INW], BF16)
            woutb = sb1("woutb", [128, 8, D], BF16)
            wgub = sb1("wgub", [32, 256], BF16)
            lng = sb1("lng", [128, 4, D], F32)
            gngb = sb1("gngb", [128, 2, 512], F32)
            bfb = sb1("bfb", [128, 8], F32)
            w_in_r = w_in.rearrange("(c p) n -> p c n", p=128)
            w_out_r = w_out.rearrange("(c p) n -> p c n", p=128)
            for c in range(8):
                k.dma("pool", winb[:, c, :], w_in_r[:, c, :], w=["winb"])
            for c in range(8):
                k.dma("pool", woutb[:, c, :], w_out_r[:, c, :], w=["woutb"])
            k.dma("pool", wgub[0:17, :], wgu, w=["wgub"])
            for j in range(4):
                k.dma("sp", lng[:, j, :], lnv[j].partition_broadcast(128), w=["lng"])
            for j in range(2):
                k.dma("sp", gngb[:, j, :], gng[j].partition_broadcast(128), w=["gngb"])
            k.dma("sp", bfb[:], bfg.partition_broadcast(128), w=["bfb"])

            KT = sb1("KT", [128, 4, nblk * 128], BF16)
            VP = sb1("VP", [128, nblk, 8, 65], BF16)
            negc = sb1("negc", [128, nblk, 8], F32)
            carry = sb1("carry", [128, 8], F32)
            cmid = sb1("cmid", [128, 8], F32)
            Sf = sb1("Sf", [128, 2, 128], F32)
            Sb = sb1("Sb", [128, 2, 128], BF16)
            k.memset("pool", VP[:], 1.0, w=[("VP", j) for j in range(nblk)])
            k.memset("pool", carry[:], 0.0, w=["carry"])
            k.memset("pool", Sf[:], 0.0, w=["Sf"])
            k.memset("pool", Sb[:], 0.0, w=["Sb"])

            xt = [sb1("xt0", [128, D], F32)]
            h0 = [sb1(f"h0{i}", [128, D], F32) for i in range(2)]
            h0T = sb1("h0T", [128, 8, 128], BF16)
            bst = sb1("bst", [128, 2, 6], F32)
            mv = sb1("mv", [128, 2], F32)
            rstd = sb1("rstd", [128, 1], F32)
            qTf = sb1("qTf", [128, 2, 4, 128], BF16)
            k.memset("pool", qTf[:], 0.0, w=["qTf"])
            qkg = sb1("qkg", [128, 4, 128], F32)
            gaT = sb1("gaT", [32, 128], BF16)
            k.memset("dve", gaT[:], 1.0, w=["gaT"])
            eg = sb1("eg", [128, 256], F32)
            glog = sb1("glog", [128, 256], F32)
            E1 = sb1("E1", [128, 2, 128], F32)
            E2 = sb1("E2", [128, 2, 128], F32)
            E3 = sb1("E3", [128, 2, 128], F32)
            Erev = sb1("Erev", [128, 256], F32)
            q_in = sb1("q_in", [128, 2, 128], BF16)
            k_in = sb1("k_in", [128, 2, 2, 128], BF16)
            k.memset("pool", k_in[:], 0.0, w=["k_in"])
            q_dec = sb1("q_dec", [128, 2, 2, 128], BF16)
            k.memset("pool", q_dec[:], 0.0, w=["q_dec"])
            k_dec = sb1("k_dec", [128, 256], BF16)
            vg = sb1("vg", [128, 512], BF16)
            sr = sb1("sr", [128, 512], F32)
            AM = sb1("AM", [128, 4, 128], BF16)
            o_sb = sb1("o_sb", [128, 4, 128], F32)
            sq = sb1("sq", [128, 512], F32)
            ss = sb1("ss", [128, 8], F32)
            zt = sb1("zt", [128, 8], F32)
            logf = sb1("logf", [128, 8], F32)
            biasb = sb1("biasb", [128, nblk, 8], F32)
            NPT = 8
            PT = [sb1(f"PT{i}", [128, 128], BF16) for i in range(NPT)]
            rinv = sb1("rinv", [128, 8], F32)
            on = o_sb[:].rearrange("p a (b c) -> p (a b) c", c=64)
            ocat = sb1("ocat", [128, D], F32)
            ocatT = sb1("ocatT", [128, 8, 128], BF16)
            rr = sb1("rr", [128, D], F32)

            def layer_norm(src, dst, gi, sname_, dname_):
                for hh in range(2):
                    k.op("dve", lambda e, hh=hh: e.bn_stats(out=bst[:, hh, :], in_=src[:, hh * 512:(hh + 1) * 512]),
                         r=[sname_], w=["bst"])
                k.op("dve", lambda e: e.bn_aggr(out=mv[:], in_=bst[:].rearrange("p a b -> p (a b)")), r=["bst"], w=["mv"])
                k.act(rstd[:], mv[:, 1:2], AF.Ln, bias=epsc[:, 0:1], r=["mv", "epsc"], w=["rstd"])
                k.act(rstd[:], rstd[:], AF.Exp, scale=-0.5, r=["rstd"], w=["rstd"])
                k.ts("dve", dst, src, mv[:, 0:1], rstd[:], ALU.subtract, ALU.mult, r=[sname_, "mv", "rstd"], w=[dname_])
                k.tt("pool", dst, dst, lng[:, gi, :], ALU.mult, r=[dname_, "lng"], w=[dname_])
                k.tt("pool", dst, dst, lng[:, gi + 1, :], ALU.add, r=[dname_, "lng"], w=[dname_])

            rrc = {"pt": 0, "sc": 0}

            def front(b):
                xb_ = xt[0]
                hb = h0[b % 2]
                xname = "xt0"
                hname = f"h0{b % 2}"
                for _ in range(8):
                    if tbl_jobs:
                        dst_, src_, nm_, i_ = tbl_jobs.pop(0)
                        k.dma("pool", dst_[i_], src_[i_], w=[(nm_, i_)])
                if b == 0:
                    k.memset("pool", xb_[:], 0.0, w=[xname])
                    k.dma("sp", xb_[112:128, :], meta, w=[xname])
                else:
                    k.dma("sp", xb_[:], x[(b - 1) * 128:b * 128, :], w=[xname])
                layer_norm(xb_[:], hb[:], 0, xname, hname)
                if debug:
                    k.dma("sp", dbg["d_h0"][b], hb[:], r=[hname])
                for half in range(2):
                    yield
                    pt_, pn = gbank()
                    for j in range(4):
                        c = half * 4 + j
                        k.tr(pt_[:, j * 128:(j + 1) * 128], hb[:, c * 128:(c + 1) * 128], ident[:], r=[hname, "ident"], w=pn)
                    k.acopy(h0T[:, half * 4:(half + 1) * 4, :], pt_[:].rearrange("p (a b) -> p a b", a=4), r=pn, w=["h0T"])
                yield
                pt_, pn = gbank()
                for j, c0 in enumerate((QA0, QA0 + 128, KA0, KA0 + 128)):
                    for c in range(8):
                        k.mm(pt_[:, j * 128:(j + 1) * 128], winb[:, c, c0:c0 + 128], h0T[:, c, :], start=(c == 0), stop=(c == 7),
                             r=["winb", "h0T"], w=pn)
                k.acopy(qkg[:], pt_[:].rearrange("p (a b) -> p a b", a=4), r=pn, w=["qkg"])
                yield
                pt_, pn = gbank()
                for j in range(4):
                    for c in range(8):
                        k.mm(pt_[:, j * 128:(j + 1) * 128], winb[:, c, KB0 + j * 128:KB0 + (j + 1) * 128], h0T[:, c, :],
                             start=(c == 0), stop=(c == 7), r=["winb", "h0T"], w=pn)
                k.acopy(KT[:, :, b * 128:(b + 1) * 128], pt_[:].rearrange("p (a b) -> p a b", a=4), r=pn, w=[("KT", b)])
                yield
                pt_, pn = gbank()
                for c in range(8):
                    k.mm(pt_[:, 0:128], winb[:, c, GA0:GA0 + 128], h0T[:, c, :], start=(c == 0), stop=(c == 7),
                         r=["winb", "h0T"], w=pn)
                for c in range(8):
                    k.mm(pt_[:, 128:256], h0T[:, c, :], winb[:, c, FB0 - 120:FB0 + 8], start=(c == 0), stop=(c == 7),
                         r=["winb", "h0T"], w=pn)
                k.acopy(gaT[0:16, :], pt_[0:16, 0:128], r=pn, w=["gaT"])
                k.acopy(zt[:], pt_[:, 248:256], r=pn, w=["zt"])
                k.tt("dve", zt[:], zt[:], bfb[:], ALU.add, r=["zt", "bfb"], w=["zt"])
                yield
                pk_, pkn = fbank(2)
                for c in range(8):
                    k.mm(pk_[:, 0:256], h0T[:, c, :], winb[:, c, KA0:KA0 + 256], start=(c == 0), stop=(c == 7),
                         r=["winb", "h0T"], w=pkn)
                yield
                pt_, pn = gbank()
                for c in range(8):
                    k.mm(pt_[:, :], h0T[:, c, :], winb[:, c, VA0:VA0 + 512], start=(c == 0), stop=(c == 7),
                         r=["winb", "h0T"], w=pn)
                k.acopy(vg[:], pt_[:], r=pn, w=["vg"])
                yield
                pz_, pzn = gbank()
                k.mm(pz_[:, 0:256], gaT[0:17, :], wgub[0:17, :], r=["gaT", "wgub"], w=pzn)
                k.act(eg[:], pz_[:, 0:256], AF.Exp, scale=-1.0, r=pzn, w=["eg"])
                k.act(zt[:], zt[:], AF.Exp, scale=-1.0, r=["zt"], w=["zt"])
                k.act(eg[:], eg[:], AF.Ln, bias=1.0, r=["eg"], w=["eg"])
                k.act(zt[:], zt[:], AF.Ln, bias=1.0, r=["zt"], w=["zt"])
                k.ts("dve", glog[:], eg[:], -1.0 / 16.0, None, ALU.mult, r=["eg"], w=["glog"])
                k.ts("dve", logf[:], zt[:], -1.0, None, ALU.mult, r=["zt"], w=["logf"])
                if debug:
                    k.dma("sp", dbg["d_glog"][b], glog[:], r=["glog"])
                    k.dma("sp", dbg["d_logf"][b], logf[:], r=["logf"])
                yield
                pt_, pn = gbank()
                for c in range(8):
                    k.mm(pt_[:, :], h0T[:, c, :], winb[:, c, RA0:RA0 + 512], start=(c == 0), stop=(c == 7),
                         r=["winb", "h0T"], w=pn)
                k.act(sr[:], pt_[:], AF.Silu, r=pn, w=["sr"])
                yield
                pt_, pn = gbank()
                for c in range(8):
                    k.mm(pt_[:, :], h0T[:, c, :], winb[:, c, VB0:VB0 + 512], start=(c == 0), stop=(c == 7),
                         r=["winb", "h0T"], w=pn)
                k.acopy(VP[:, b, :, 0:64], pt_[:].rearrange("p (h d) -> p h d", h=8), r=pn, w=[("VP", b)])

                yield
                pd_, pdn = fbank(3)
                for p_ in range(2):
                    k.mm(pd_[:, p_ * 128:(p_ + 1) * 128], glog[:, p_ * 128:(p_ + 1) * 128], wmid[:], r=["glog", "wmid"], w=pdn)
                    k.mm(pd_[:, 256 + p_ * 128:256 + (p_ + 1) * 128], glog[:, p_ * 128:(p_ + 1) * 128], tri[:],
                         r=["glog", "tri"], w=pdn)
                k.mm(pk_[:, 256:512], sup[:], glog[:], r=["glog", "sup"], w=pkn)
                k.act(E1[:], pd_[:, 0:256].rearrange("p (a b) -> p a b", a=2), AF.Exp, r=pdn, w=["E1"])
                k.act(E2[:], pd_[:, 0:256].rearrange("p (a b) -> p a b", a=2), AF.Exp, scale=-1.0, r=pdn, w=["E2"])
                k.act(E3[:], pd_[:, 256:512].rearrange("p (a b) -> p a b", a=2), AF.Exp, r=pdn, w=["E3"])
                k.act(Erev[:], pk_[:, 256:512], AF.Exp, r=pkn, w=["Erev"])
                k.tt("dve", q_in[:], qkg[:, 0:2, :], E1[:], ALU.mult, r=["qkg", "E1"], w=["q_in"])
                for par in range(2):
                    rs = slice(par * 64, (par + 1) * 64)
                    k.tt("dve", k_in[rs, par, :, :], qkg[rs, 2:4, :], E2[rs, :, :], ALU.mult, r=["qkg", "E2"], w=["k_in"])
                for par in range(2):
                    rs = slice(par * 64, (par + 1) * 64)
                    k.stt("dve", q_dec[rs, par, :, :], qkg[rs, 0:2, :], 0.125, E3[rs, :, :], ALU.mult, ALU.mult, r=["qkg", "E3"], w=["q_dec"])
                k.tt("dve", k_dec[:], pk_[:, 0:256], Erev[:], ALU.mult, r=pkn + ["Erev"], w=["k_dec"])
                if b == 0:
                    k.memset("dve", k_in[:, :, :, 0:112], 0.0, w=["k_in"])
                    k.memset("dve", k_dec[0:112, :], 0.0, w=["k_dec"])
                yield
                pa_, pan = fbank(3)
                for h in range(4):
                    p_, r0 = h // 2, (h % 2) * 64
                    k.mm(pa_[:, h * 128:(h + 1) * 128], k_in[:, h % 2, p_, :], q_in[:, p_, :],
                         r=["k_in", "q_in"], w=pan)
                k.tt("dve", AM[:], pa_[:].rearrange("p (a b) -> p a b", a=4), maskc[:].unsqueeze(1).to_broadcast([128, 4, 128]),
                     ALU.mult, r=pan + ["maskc"], w=["AM"])
                yield
                po_, pon = fbank(2)
                if b > 0:
                    for h in range(4):
                        p_, r0 = h // 2, (h % 2) * 64
                        k.mm(po_[:, h * 128:(h + 1) * 128], AM[:, h, :], vg[:, h * 128:(h + 1) * 128], start=True, stop=False,
                             r=["AM", "vg"], w=pon)
                        k.mm(po_[:, h * 128:(h + 1) * 128], q_dec[:, h % 2, p_, :], Sb[:, p_, :], start=False, stop=True,
                             r=["q_dec", "Sb"], w=pon)
                yield
                ps_, psn = fbank(3)
                for h in range(4):
                    p_ = h // 2
                    k.mm(ps_[:, h * 128:(h + 1) * 128], k_dec[:, p_ * 128:(p_ + 1) * 128], vg[:, h * 128:(h + 1) * 128],
                         r=["k_dec", "vg"], w=psn)
                for h in range(4):
                    p_, r0 = h // 2, (h % 2) * 64
                    k.stt("dve", Sf[r0:r0 + 64, p_, :], Sf[r0:r0 + 64, p_, :], E3[r0:r0 + 64, p_, 127:128],
                          ps_[r0:r0 + 64, h * 128:(h + 1) * 128], ALU.mult, ALU.add, r=["Sf", "E3"] + psn, w=["Sf"])
                k.copy("dve", Sb[:], Sf[:], r=["Sf"], w=["Sb"])

                yield
                pf_, pfn = gbank()
                k.mm(pf_[:, 0:8], tri[:], logf[:], r=["tri", "logf"], w=pfn)
                k.mm(pf_[:, 8:16], ones[:], logf[:], r=["ones", "logf"], w=pfn)
                k.mm(pf_[:, 16:24], sel63[:], logf[:], r=["sel63", "logf"], w=pfn)
                k.stt("dve", negc[:, b, :], pf_[:, 0:8], -1.0, carry[:], ALU.mult, ALU.subtract, r=pfn + ["carry"], w=[("negc", b)])
                if b == 0:
                    k.memset("dve", negc[0:112, 0, :], NEGBIG, w=[("negc", b)])
                k.tt("dve", cmid[:], pf_[:, 16:24], carry[:], ALU.add, r=pfn + ["carry"], w=["cmid"])
                k.tt("dve", carry[:], pf_[:, 8:16], carry[:], ALU.add, r=pfn + ["carry"], w=["carry"])


            def foxq(b):
                pt_, pn = gbank()
                for j in range(4):
                    for c in range(8):
                        k.mm(pt_[:, j * 128:(j + 1) * 128], winb[:, c, QB0 + j * 128:QB0 + (j + 1) * 128], h0T[:, c, :],
                             start=(c == 0), stop=(c == 7), r=["winb", "h0T"], w=pn)
                for par in range(2):
                    k.acopy(qTf[par * 64:(par + 1) * 64, par, :, :], pt_[par * 64:(par + 1) * 64, :].rearrange("p (a b) -> p a b", a=4),
                            r=pn, w=["qTf"])

            def back_pre(b):
                xb_ = xt[0]
                hb = h0[b % 2]
                xname = "xt0"
                hname = f"h0{b % 2}"
                po_, pon = fbank(2)
                k.acopy(o_sb[:], po_[:].rearrange("p (a b) -> p a b", a=4), r=pon, w=["o_sb"])
                if debug:
                    pass
                k.tt("dve", sq[:], o_sb[:].rearrange("p a b -> p (a b)"), o_sb[:].rearrange("p a b -> p (a b)"), ALU.mult,
                     r=["o_sb"], w=["sq"])
                k.reduce(ss[:, 0:4], sq[:].rearrange("p (a b) -> p a b", a=4), ALU.add, r=["sq"], w=["ss"])
                k.act(ss[:, 0:4], ss[:, 0:4], AF.Ln, bias=epsc[:, 0:1], scale=1.0 / 128.0, r=["ss", "epsc"], w=["ss"])
                k.act(ss[:, 0:4], ss[:, 0:4], AF.Exp, scale=-0.5, r=["ss"], w=["ss"])
                k.tt("dve", o_sb[:], o_sb[:], ss[:, 0:4].unsqueeze(2).to_broadcast([128, 4, 128]), ALU.mult, r=["o_sb", "ss"], w=["o_sb"])
                k.tt("pool", sq[:], o_sb[:].rearrange("p a b -> p (a b)"), gngb[:, 0, :], ALU.mult, r=["o_sb", "gngb"], w=["sq"])
                k.tt("pool", ocat[:, 0:512], sq[:], sr[:], ALU.mult, r=["sq", "sr"], w=["ocat_a"])
                if debug:
                    k.dma("sp", dbg["d_oa"][b], ocat[:, 0:512], r=["ocat_a"])

                k.tt("dve", biasb[:, 0:b + 1, :], negc[:, 0:b + 1, :], cmid[:].unsqueeze(1).to_broadcast([128, b + 1, 8]), ALU.add,
                     r=[("negc", j) for j in range(b + 1)] + ["cmid"], w=["biasb"])

            def attention(b, gen):
                iters = [(h, kb) for h in range(8) for kb in range(b + 1)]
                nbat = (len(iters) + 3) // 4
                binfo = {}
                for m in range(nbat + 2):
                    if gen is not None:
                        next(gen, None)
                    if m < nbat:
                        sb_i = 4 + (rrc["sc"] % 2)
                        rrc["sc"] += 1
                        lst = []
                        for j, (h, kb) in enumerate(iters[4 * m:4 * m + 4]):
                            k.mm(psb[sb_i][:, j * 128:(j + 1) * 128], KT[:, h // 2, kb * 128:(kb + 1) * 128], qTf[:, h % 2, h // 2, :],
                                 r=[("KT", kb), "qTf"], w=bq(sb_i))
                            lst.append((h, kb, j))
                        binfo[m] = (sb_i, lst, [])
                    if 1 <= m <= nbat:
                        sb_i, lst, pts = binfo[m - 1]
                        for (h, kb, j) in lst:
                            ptile = PT[rrc["pt"] % NPT]
                            pname = f"PT{rrc["pt"] % NPT}"
                            rrc["pt"] += 1
                            k.act(ptile[:], psb[sb_i][:, j * 128:(j + 1) * 128], AF.Exp, bias=biasb[:, kb, h:h + 1], scale=0.125,
                                  r=bq(sb_i) + ["biasb"], w=[pname])
                            if kb == b:
                                k.tt("pool", ptile[:], ptile[:], trib[:], ALU.mult, r=[pname, "trib"], w=[pname])
                            pts.append((ptile, pname))
                    if m >= 2:
                        sb_i, lst, pts = binfo.pop(m - 2)
                        for (h, kb, j), (ptile, pname) in zip(lst, pts):
                            obn = f"ps{6 + h // 4}"
                            oc0 = (h % 4) * 65
                            k.mm(psb[6 + h // 4][:, oc0:oc0 + 65], ptile[:], VP[:, kb, h, :], start=(kb == 0), stop=(kb == b),
                                 r=[pname, ("VP", kb)], w=[(obn, h % 4)])

            def back_post(b):
                xb_ = xt[0]
                hb = h0[b % 2]
                xname = "xt0"
                hname = f"h0{b % 2}"
                for g_ in range(2):
                    obank = psb[6 + g_]
                    ov = obank[:, 0:260].rearrange("p (h d) -> p h d", h=4)
                    onames = [(f"ps{6 + g_}", j) for j in range(4)]
                    k.op("dve", lambda e, ov=ov, g_=g_: e.reciprocal(out=rinv[:, g_ * 4:(g_ + 1) * 4], in_=ov[:, :, 64]),
                         r=onames, w=[("rinv", g_)])
                    k.tt("dve", on[:, g_ * 4:(g_ + 1) * 4, :], ov[:, :, 0:64],
                         rinv[:, g_ * 4:(g_ + 1) * 4].unsqueeze(2).to_broadcast([128, 4, 64]), ALU.mult,
                         r=onames + [("rinv", g_)], w=["o_sb"])
                onf = on[:].rearrange("p h d -> p (h d)")
                k.tt("dve", sq[:], onf, onf, ALU.mult, r=["o_sb"], w=["sq"])
                k.reduce(ss[:], sq[:].rearrange("p (a b) -> p a b", a=8), ALU.add, r=["sq"], w=["ss"])
                k.act(ss[:], ss[:], AF.Ln, bias=epsc[:, 0:1], scale=1.0 / 64.0, r=["ss", "epsc"], w=["ss"])
                k.act(ss[:], ss[:], AF.Exp, scale=-0.5, r=["ss"], w=["ss"])
                k.tt("dve", on[:], on[:], ss[:].unsqueeze(2).to_broadcast([128, 8, 64]), ALU.mult,
                     r=["o_sb", "ss"], w=["o_sb"])
                k.tt("pool", ocat[:, 512:1024], onf, gngb[:, 1, :], ALU.mult, r=["o_sb", "gngb"], w=["ocat_b"])
                if debug:
                    k.dma("sp", dbg["d_ob"][b], ocat[:, 512:1024], r=["ocat_b"])

                for half in range(2):
                    pt_, pn = gbank()
                    for j in range(4):
                        c = half * 4 + j
                        k.tr(pt_[:, j * 128:(j + 1) * 128], ocat[:, c * 128:(c + 1) * 128], ident[:],
                             r=["ocat_a", "ocat_b", "ident"], w=pn)
                    k.acopy(ocatT[:, half * 4:(half + 1) * 4, :], pt_[:].rearrange("p (a b) -> p a b", a=4), r=pn, w=["ocatT"])
                for half in range(2):
                    pt_, pn = gbank()
                    for c in range(8):
                        k.mm(pt_[:, :], ocatT[:, c, :], woutb[:, c, half * 512:(half + 1) * 512], start=(c == 0), stop=(c == 7),
                             r=["ocatT", "woutb"], w=pn)
                    k.stt("dve", rr[:, half * 512:(half + 1) * 512], hb[:, half * 512:(half + 1) * 512], DN_ALPHA, pt_[:],
                          ALU.mult, ALU.add, r=[hname] + pn, w=["rr"])
                layer_norm(rr[:], rr[:], 2, "rr", "rr")
                k.dma("sp", h1d[(b - 1) * 128:b * 128, :], rr[:], r=["rr"], w=[("h1d", b - 1)])
                if debug:
                    k.dma("sp", dbg["d_h1"][b], rr[:], r=["rr"])


            for _ in front(0):
                pass
            if nblk > 1:
                for _ in front(1):
                    pass
                foxq(1)
            for b in range(1, nblk):
                back_pre(b)
                gen = front(b + 1) if b + 1 < nblk else None
                attention(b, gen)
                if gen is not None:
                    for _ in gen:
                        pass
                    foxq(b + 1)
                back_post(b)

        while tbl_jobs:
            dst_, src_, nm_, i_ = tbl_jobs.pop(0)
            k.dma("pool", dst_[i_], src_[i_], w=[(nm_, i_)])
        fence = P.tails()
        for en in ENGINES:
            P.wait_only(en, fence)
        phase2(nc, P, k, psb, ident, iof, epsc, h1d, ub, vb, wqd, skT, lnv, out, ntile)

        P.wait_only("sp", P.tails())
        P.emit(nc, st)
    return nc


def phase2(nc, P, k, psb, ident, iof, epsc, h1d, ub, vb, wqd, skT, lnv, out, ntile):
    if ntile == 0:
        return
    with ExitStack() as st2:
        def sb2(name, shape, dt):
            return st2.enter_context(nc.sbuf_tensor(name, shape, dt))

        def bq(i, qs=(0, 1, 2, 3)):
            return [(f"ps{i}", q) for q in qs]

        grr = [0]
        gbase = [2]

        def gbank():
            i = gbase[0] + grr[0] % 2
            grr[0] += 1
            return psb[i], bq(i)

        wqs = [sb2(f"wqs{i}", [128, 8, 512], BF16) for i in range(2)]
        skTb = sb2("skTb", [128, 16, 128], BF16)
        ln2 = sb2("ln2", [128, 2, D], F32)
        k.dma("pool", skTb[:], skT.rearrange("hc d n -> d hc n"), w=["skTb"])
        for j in range(2):
            k.dma("sp", ln2[:, j, :], lnv[4 + j].partition_broadcast(128), w=["ln2"])
        h1t2 = [sb2(f"h1t{i}", [128, 2, D], F32) for i in range(2)]
        h1T2 = [sb2(f"h1T{i}", [128, 8, 256], BF16) for i in range(2)]
        qT = sb2("qT", [128, 16, 256], BF16)
        s_sb = sb2("s_sb", [128, 16, 128], F32)
        s_tmp = [sb2(f"s_tmp{i}", [128, 128], F32) for i in range(2)]
        topv = sb2("topv", [128, 16, 16], F32)
        topi = sb2("topi", [128, 16, 16], U32)
        topif = sb2("topif", [128, 16, 16], F32)
        cand = sb2("cand", [128, 8, 256], F32)
        cand_tmp = [sb2(f"cand_tmp{i}", [128, 256], F32) for i in range(2)]
        bv = sb2("bv", [128, 8, 16], F32)
        pos = sb2("pos", [128, 8, 16], U32)
        posf = sb2("posf", [128, 8, 16], F32)
        akf = sb2("akf", [128, 8, 16], F32)
        bkf = sb2("bkf", [128, 8, 16], F32)
        eq4 = sb2("eq4", [128, 8, 16, 16], F32)
        thr16 = sb2("thr16", [128, 16], F32)
        iota16 = sb2("iota16", [128, 16], F32)
        If = sb2("If", [128, 8, 16], F32)
        Jf = sb2("Jf", [128, 8, 16], F32)
        gf = sb2("gf", [128, 8, 16], F32)
        gsum = sb2("gsum", [128, 8], F32)
        IJG = sb2("IJG", [128, 2, 3, 128], F32)
        IT = sb2("IT", [128, 256], F32)
        JT = sb2("JT", [128, 256], F32)
        gT = sb2("gT", [128, 256], F32)
        NAB = 6
        AENG = "dve"
        GAENG = "dve"
        Bt = [sb2(f"Bt{i}", [128, 128], BF16) for i in range(NAB)]
        At = [sb2(f"At{i}", [128, 128], BF16) for i in range(NAB)]
        G_sb = sb2("G_sb", [128, 256, 128], BF16)
        NU = 6
        uTs = [sb2(f"uTs{i}", [128, 8, 128], BF16) for i in range(NU)]
        vs = [sb2(f"vs{i}", [128, D], BF16) for i in range(NU)]
        ga_sb = [sb2(f"ga_sb{i}", [128, 256], BF16) for i in range(2)]
        GA = [sb2(f"GA{i}", [128, 256], BF16) for i in range(6)]
        rr2s = [sb2(f"rr2_{i}", [128, D], F32) for i in range(2)]
        bst = sb2("bst2", [128, 2, 6], F32)
        mv = sb2("mv2", [128, 2], F32)
        rstd = sb2("rstd2", [128, 1], F32)
        iob = sb2("iob", [128, 128], BF16)
        k.copy("dve", iob[:], iof[:], r=["iof"], w=["iob"])
        k.ts("dve", iota16[:], iof[:, 0:16], 1.0, None, ALU.mult, r=["iof"], w=["iota16"])
        k.ts("dve", thr16[:], iof[:, 0:16], 16.0, 15.5, ALU.mult, ALU.add, r=["iof"], w=["thr16"])

        def routing(ti):
            h1t = h1t2[ti % 2]
            h1T = h1T2[ti % 2]
            h1tn = f"h1t{ti % 2}"
            h1Tn = f"h1T{ti % 2}"
            nops = [len(P.ops)]

            def tick():
                if len(P.ops) - nops[0] >= 4:
                    nops[0] = len(P.ops)
                    return True
                return False
            for tb in range(2):
                blk = 2 * ti + tb
                k.dma("sp", h1t[:, tb, :], h1d[blk * 128:(blk + 1) * 128, :], r=[("h1d", blk)], w=[(h1tn, tb)])
            for tb in range(2):
                for half in range(2):
                    yield
                    pt_, pn = gbank()
                    for j in range(4):
                        c = half * 4 + j
                        k.tr(pt_[:, j * 128:(j + 1) * 128], h1t[:, tb, c * 128:(c + 1) * 128], ident[:], r=[(h1tn, tb), "ident"], w=pn)
                    k.acopy(h1T[:, half * 4:(half + 1) * 4, tb * 128:(tb + 1) * 128], pt_[:].rearrange("p (a b) -> p a b", a=4),
                            r=pn, w=[h1Tn])
            for hp in range(8):
                pt_, pn = gbank()
                for j in range(2):
                    yield
                    hc = hp * 2 + j
                    wq_ = wqs[(hc // 4) % 2]
                    wqn = f"wqs{(hc // 4) % 2}"
                    if hc % 4 == 0:
                        k.dma("sp", wq_[:], wqd[:, :, hc * 128:(hc + 4) * 128], r=["wqd"], w=[wqn])
                    for c in range(8):
                        k.mm(pt_[:, j * 256:(j + 1) * 256], wq_[:, c, (hc % 4) * 128:(hc % 4 + 1) * 128], h1T[:, c, :], start=(c == 0), stop=(c == 7),
                             r=[wqn, h1Tn], w=pn)
                k.acopy(qT[:, hp * 2:hp * 2 + 2, :], pt_[:].rearrange("p (a b) -> p a b", a=2), r=pn, w=["qT"])
            for tb in range(2):
                for g4 in range(4):
                    yield
                    pt_, pn = gbank()
                    for j in range(4):
                        hc = g4 * 4 + j
                        k.mm(pt_[:, j * 128:(j + 1) * 128], qT[:, hc, tb * 128:(tb + 1) * 128], skTb[:, hc, :], r=["qT", "skTb"], w=pn)
                    k.acopy(s_sb[:, g4 * 4:(g4 + 1) * 4, :], pt_[:].rearrange("p (a b) -> p a b", a=4), r=pn, w=["s_sb"])
                for hc in range(16):
                    yield
                    sv = s_sb[:, hc, :]
                    stp = s_tmp[hc % 2]
                    stn = f"s_tmp{hc % 2}"
                    k.op("dve", lambda e, sv=sv, hc=hc: e.max(out=topv[:, hc, 0:8], in_=sv), r=["s_sb"], w=["topv"])
                    k.op("dve", lambda e, sv=sv, hc=hc: e.max_index(out=topi[:, hc, 0:8], in_max=topv[:, hc, 0:8], in_values=sv),
                         r=["s_sb", "topv"], w=["topi"])
                    k.op("dve", lambda e, sv=sv, hc=hc, stp=stp: e.match_replace(out=stp[:], in_to_replace=topv[:, hc, 0:8], in_values=sv,
                                                                                  imm_value=-1e30), r=["s_sb", "topv"], w=[stn])
                    k.op("dve", lambda e, hc=hc, stp=stp: e.max(out=topv[:, hc, 8:16], in_=stp[:]), r=[stn], w=["topv"])
                    k.op("dve", lambda e, hc=hc, stp=stp: e.max_index(out=topi[:, hc, 8:16], in_max=topv[:, hc, 8:16], in_values=stp[:]),
                         r=[stn, "topv"], w=["topi"])
                k.copy("dve", topif[:], topi[:], r=["topi"], w=["topif"])
                tv = topv[:].rearrange("p (h c) a -> p h c a", c=2)
                tif = topif[:].rearrange("p (h c) a -> p h c a", c=2)
                yield
                k.tt("dve", cand[:].rearrange("p h (a b) -> p h a b", a=16), tv[:, :, 0, :].unsqueeze(3).to_broadcast([128, 8, 16, 16]),
                     tv[:, :, 1, :].unsqueeze(2).to_broadcast([128, 8, 16, 16]), ALU.add, r=["topv"], w=["cand"])
                for h in range(8):
                    yield
                    cv = cand[:, h, :]
                    ctp = cand_tmp[h % 2]
                    ctn = f"cand_tmp{h % 2}"
                    k.op("dve", lambda e, cv=cv, h=h: e.max(out=bv[:, h, 0:8], in_=cv), r=["cand"], w=["bv"])
                    k.op("dve", lambda e, cv=cv, h=h: e.max_index(out=pos[:, h, 0:8], in_max=bv[:, h, 0:8], in_values=cv),
                         r=["cand", "bv"], w=["pos"])
                    k.op("dve", lambda e, cv=cv, h=h, ctp=ctp: e.match_replace(out=ctp[:], in_to_replace=bv[:, h, 0:8], in_values=cv,
                                                                                imm_value=-1e30), r=["cand", "bv"], w=[ctn])
                    k.op("dve", lambda e, h=h, ctp=ctp: e.max(out=bv[:, h, 8:16], in_=ctp[:]), r=[ctn], w=["bv"])
                    k.op("dve", lambda e, h=h, ctp=ctp: e.max_index(out=pos[:, h, 8:16], in_max=bv[:, h, 8:16], in_values=ctp[:]),
                         r=[ctn, "bv"], w=["pos"])
                yield
                k.copy("dve", posf[:], pos[:], r=["pos"], w=["posf"])
                yield
                k.tt("dve", eq4[:, :, :, 0:15], posf[:].unsqueeze(3).to_broadcast([128, 8, 16, 15]),
                     thr16[:, 0:15].unsqueeze(1).unsqueeze(1).to_broadcast([128, 8, 16, 15]), ALU.is_ge, r=["posf", "thr16"], w=["eq4"])
                yield
                k.reduce(akf[:], eq4[:, :, :, 0:15], ALU.add, r=["eq4"], w=["akf"])
                k.stt("dve", bkf[:], akf[:], -16.0, posf[:], ALU.mult, ALU.add, r=["akf", "posf"], w=["bkf"])
                for (src, cidx, dst, dn) in ((akf, 0, If, "If"), (bkf, 1, Jf, "Jf")):
                    yield
                    k.tt("dve", eq4[:], src[:].unsqueeze(3).to_broadcast([128, 8, 16, 16]),
                         iota16[:].unsqueeze(1).unsqueeze(1).to_broadcast([128, 8, 16, 16]), ALU.is_equal,
                         r=["akf", "bkf", "iota16"], w=["eq4"])
                    yield
                    k.tt("dve", eq4[:], eq4[:], tif[:, :, cidx, :].unsqueeze(2).to_broadcast([128, 8, 16, 16]), ALU.mult,
                         r=["eq4", "topif"], w=["eq4"])
                    yield
                    k.reduce(dst[:], eq4[:], ALU.add, r=["eq4"], w=[dn])
                yield
                k.tt("dve", gf[:], bv[:], bv[:, :, 0:1].to_broadcast([128, 8, 16]), ALU.subtract, r=["bv"], w=["gf"])
                k.act(gf[:], gf[:], AF.Exp, r=["gf"], w=["gf"])
                k.reduce(gsum[:], gf[:], ALU.add, r=["gf"], w=["gsum"])
                k.op("dve", lambda e: e.reciprocal(out=gsum[:], in_=gsum[:]), r=["gsum"], w=["gsum"])
                k.tt("dve", gf[:], gf[:], gsum[:].unsqueeze(2).to_broadcast([128, 8, 16]), ALU.mult, r=["gf", "gsum"], w=["gf"])
                k.copy("dve", IJG[:, tb, 0, :], If[:].rearrange("p h k -> p (h k)"), r=["If"], w=[("IJG", tb)])
                k.copy("dve", IJG[:, tb, 1, :], Jf[:].rearrange("p h k -> p (h k)"), r=["Jf"], w=[("IJG", tb)])
                k.copy("dve", IJG[:, tb, 2, :], gf[:].rearrange("p h k -> p (h k)"), r=["gf"], w=[("IJG", tb)])
        def gbuild(ti, fin=None):
            for tb in range(2):
                pt_, pn = gbank()
                for j in range(3):
                    k.tr(pt_[:, j * 128:(j + 1) * 128], IJG[:, tb, j, :], ident[:], r=[("IJG", tb), "ident"], w=pn)
                for j, (dst, dn) in enumerate(((IT, "IT"), (JT, "JT"), (gT, "gT"))):
                    k.acopy(dst[:, tb * 128:(tb + 1) * 128], pt_[:, j * 128:(j + 1) * 128], r=pn, w=[dn])
            for t in range(256):
                if fin is not None and t >= 8 and t % 8 == 0:
                    next(fin, None)
                jb = t % NAB
                k.ts("dve", Bt[jb][:], iob[:], JT[:, t:t + 1], None, ALU.is_equal, r=["iob", "JT"], w=[f"Bt{jb}"])
                k.ts(AENG, At[jb][:], iob[:], IT[:, t:t + 1], gT[:, t:t + 1], ALU.is_equal, ALU.mult, r=["iob", "IT", "gT"], w=[f"At{jb}"])
                gb = 2 + (t // 4) % 2
                q_ = t % 4
                k.mm(psb[gb][:, q_ * 128:(q_ + 1) * 128], Bt[jb][:], At[jb][:], r=[f"Bt{jb}", f"At{jb}"], w=[(f"ps{gb}", q_)])
                if q_ == 3:
                    k.acopy(G_sb[:, t - 3:t + 1, :], psb[gb][:].rearrange("p (a b) -> p a b", a=4), r=bq(gb), w=["G_sb"])
        def gloop(ti, gen):
            h1T = h1T2[ti % 2]
            h1Tn = f"h1T{ti % 2}"
            LG = 5
            ginfo = {}
            for i in range(128 + LG):
                if i < 128:
                    ju = i % NU
                    k.dma("sp", uTs[ju][:], ub[i].rearrange("p (c e) -> p c e", c=8), r=[("ub", i)], w=[f"uTs{ju}"])
                    k.dma("act", vs[ju][:], vb[i], r=[("vb", i)], w=[f"vs{ju}"])
                    ab = i % 2
                    for c in range(8):
                        k.mm(psb[ab][:, 0:256], uTs[ju][:, c, :], h1T[:, c, :], start=(c == 0), stop=(c == 7), r=[f"uTs{ju}", h1Tn], w=bq(ab))
                    gs = ga_sb[i % 2]
                    gsn = f"ga_sb{i % 2}"
                    k.act(gs[:], psb[ab][:, 0:256], AF.Gelu, r=bq(ab), w=[gsn])
                    gA = GA[i % 6]
                    gAn = f"GA{i % 6}"
                    k.tt(GAENG, gA[:], gs[:], G_sb[:, :, i], ALU.mult, r=[gsn, "G_sb"], w=[gAn])
                    ginfo[i] = (gA, gAn, ju)
                if gen is not None:
                    next(gen, None)
                if i >= LG:
                    i2 = i - LG
                    gA, gAn, ju = ginfo.pop(i2)
                    for tb in range(2):
                        for half in range(2):
                            yb = 4 + tb * 2 + half
                            k.mm(psb[yb][:, :], gA[:, tb * 128:(tb + 1) * 128], vs[ju][:, half * 512:(half + 1) * 512],
                                 start=(i2 == 0), stop=(i2 == 127), r=[gAn, f"vs{ju}"], w=bq(yb))
        def finalize(ti):
            h1t = h1t2[ti % 2]
            h1tn = f"h1t{ti % 2}"
            for tb in range(2):
                blk = 2 * ti + tb
                rr2 = rr2s[tb]
                rn = f"rr2_{tb}"
                for half in range(2):
                    yb = 4 + tb * 2 + half
                    k.stt("dve", rr2[:, half * 512:(half + 1) * 512], h1t[:, tb, half * 512:(half + 1) * 512], DN_ALPHA, psb[yb][:, :],
                          ALU.mult, ALU.add, r=[(h1tn, tb)] + bq(yb), w=[rn])
                    yield
                for hh in range(2):
                    k.op("dve", lambda e, hh=hh, rr2=rr2: e.bn_stats(out=bst[:, hh, :], in_=rr2[:, hh * 512:(hh + 1) * 512]), r=[rn], w=["bst2"])
                k.op("dve", lambda e: e.bn_aggr(out=mv[:], in_=bst[:].rearrange("p a b -> p (a b)")), r=["bst2"], w=["mv2"])
                k.act(rstd[:], mv[:, 1:2], AF.Ln, bias=epsc[:, 0:1], r=["mv2", "epsc"], w=["rstd2"])
                k.act(rstd[:], rstd[:], AF.Exp, scale=-0.5, r=["rstd2"], w=["rstd2"])
                yield
                yield
                k.ts("dve", rr2[:], rr2[:], mv[:, 0:1], rstd[:], ALU.subtract, ALU.mult, r=[rn, "mv2", "rstd2"], w=[rn])
                k.tt("pool", rr2[:], rr2[:], ln2[:, 0, :], ALU.mult, r=[rn, "ln2"], w=[rn])
                k.tt("pool", rr2[:], rr2[:], ln2[:, 1, :], ALU.add, r=[rn, "ln2"], w=[rn])
                k.dma("sp", out[blk * 128:(blk + 1) * 128, :], rr2[:], r=[rn])
                yield

        gbase[0] = 0
        for _ in routing(0):
            pass
        fin = None
        for ti in range(ntile):
            gbase[0] = 2
            gbuild(ti, fin)
            if fin is not None:
                for _ in fin:
                    pass
            gen = routing(ti + 1) if ti + 1 < ntile else None
            gloop(ti, gen)
            if gen is not None:
                for _ in gen:
                    pass
            fin = finalize(ti)
        if fin is not None:
            for _ in fin:
                pass


def _prep_shared(inputs):
    f = lambda a: np.ascontiguousarray(np.asarray(a, dtype=np.float32))
    sh = {}
    sh["meta"] = f(inputs["meta_tokens"])
    sh["lnv"] = f(np.stack([inputs["emb_ln_g"], inputs["emb_ln_b"], inputs["ln1_g"][0], inputs["ln1_b"][0],
                            inputs["ln2_g"][0], inputs["ln2_b"][0]]))
    sh["w_in"] = f(inputs["w_in"][0])
    sh["wgu"] = f(np.concatenate([inputs["w_gate_up"][0], inputs["b_gate"][0][None, :]], axis=0))
    sh["bfg"] = f(inputs["b_forget"][0])
    sh["gng"] = f(np.stack([inputs["gla_norm_g"][0], inputs["fox_norm_g"][0]]))
    sh["w_out"] = f(inputs["w_out"][0])
    sh["w_q"] = f(inputs["peer_w_q"][0])
    sk = np.asarray(inputs["peer_sub_keys"][0], dtype=np.float32)
    sh["skT"] = f(sk.transpose(0, 1, 3, 2).reshape(16, 128, 128))
    u = np.asarray(inputs["peer_u"][0], dtype=np.float32)
    sh["uT"] = f(u.reshape(128, 128, 8, 128).transpose(0, 3, 2, 1).reshape(128, 128, 1024))
    sh["vv"] = f(np.asarray(inputs["peer_v"][0], dtype=np.float32).reshape(128, 128, 1024))
    return sh


def kernel(**inputs):
    nc = build()
    sh = _prep_shared(inputs)
    x = np.asarray(inputs["x"], dtype=np.float32)
    in_maps = []
    for b in range(8):
        m = dict(sh)
        m["x"] = np.ascontiguousarray(x[b])
        in_maps.append(m)
    res = run_bass_kernel_spmd(nc, in_maps, core_ids=list(range(8)))
    return np.stack([np.asarray(r["out"], dtype=np.float32) for r in res.results], axis=0)
```
